# Optimizing a Trainium2 kernel written in Bass

```python
import jax, jax.numpy as jnp
from jax import lax
import numpy as np

D_MODEL = 2048
BATCH = 1
SEQ = 8192
DEPTH = 4

HEAD_DIM = 128
GDN_HEADS = 8
ATT_HEADS = 8
GDN_WIDTH = GDN_HEADS * HEAD_DIM
ATT_WIDTH = ATT_HEADS * HEAD_DIM
MIX_WIDTH = GDN_WIDTH + ATT_WIDTH
GDN_CONV = 4
GDN_CHUNK = 64
KV_RANK = 256
IDX_HEADS = 16
IDX_DIM = 64
INDEX_TOPK = 256
Q_BLOCK = 128
ROPE_THETA = 10000.0
D_FF = 5632
FFN_CONV = 3
LN_EPS = 1e-5
RMS_EPS = 1e-6
DN_ALPHA = (2 * DEPTH) ** 0.25
DN_BETA = (8 * DEPTH) ** -0.25
PROJ_SIZES = (GDN_WIDTH, GDN_WIDTH, GDN_WIDTH, GDN_WIDTH, GDN_HEADS, GDN_HEADS,
              ATT_WIDTH, KV_RANK, IDX_HEADS * IDX_DIM, IDX_DIM, IDX_HEADS)
PROJ_WIDTH = sum(PROJ_SIZES)

kernel_name = "hymba_gdn_dsa_convffn_deepnorm"


def layer_norm(x, g, b):
    xf = x.astype(jnp.float32)
    mu = jnp.mean(xf, -1, keepdims=True)
    var = jnp.mean(jnp.square(xf - mu), -1, keepdims=True)
    return ((xf - mu) * lax.rsqrt(var + LN_EPS) * g + b).astype(x.dtype)


def rms_norm(x, g):
    xf = x.astype(jnp.float32)
    return (xf * lax.rsqrt(jnp.mean(xf * xf, -1, keepdims=True) + RMS_EPS) * g).astype(x.dtype)


def l2_normalize(x):
    xf = x.astype(jnp.float32)
    return (xf * lax.rsqrt(jnp.sum(xf * xf, -1, keepdims=True) + RMS_EPS)).astype(x.dtype)


def causal_depthwise_conv(x, w):
    k = w.shape[0]
    return lax.conv_general_dilated(
        x, w[:, None, :].astype(x.dtype), window_strides=(1,), padding=((k - 1, 0),),
        dimension_numbers=('NWC', 'WIO', 'NWC'), feature_group_count=x.shape[-1])


def rope_tables(seq, dim):
    inv = ROPE_THETA ** (-jnp.arange(0, dim, 2, dtype=jnp.float32) / dim)
    ang = jnp.arange(seq, dtype=jnp.float32)[:, None] * inv[None, :]
    ang = jnp.concatenate([ang, ang], -1)
    return jnp.cos(ang), jnp.sin(ang)


def apply_rope(x, cos, sin):
    xf = x.astype(jnp.float32)
    x1, x2 = jnp.split(xf, 2, -1)
    rot = jnp.concatenate([-x2, x1], -1)
    return (xf * cos[:, None, :] + rot * sin[:, None, :]).astype(x.dtype)


def gated_delta_rule_chunked(q, k, v, g, beta):
    b, s, h, d = q.shape
    c = GDN_CHUNK
    n = s // c

    def to_chunks(t):
        return jnp.moveaxis(t.reshape((b, n, c, h) + t.shape[3:]), 3, 1)

    qc, kc, vc = [to_chunks(t.astype(jnp.float32)) for t in (q, k, v)]
    gc = jnp.cumsum(to_chunks(g.astype(jnp.float32)), axis=-1)
    bc = to_chunks(beta.astype(jnp.float32))
    causal = jnp.tril(jnp.ones((c, c), dtype=bool))
    decay = jnp.exp(jnp.where(causal, gc[..., :, None] - gc[..., None, :], -jnp.inf))
    kkt = jnp.einsum('bhnid,bhnjd->bhnij', kc, kc)
    a_mat = jnp.eye(c, dtype=jnp.float32) + jnp.tril(bc[..., :, None] * kkt * decay, -1)
    rhs = jnp.concatenate([vc * bc[..., None], kc * (bc * jnp.exp(gc))[..., None]], -1)
    sol = lax.linalg.triangular_solve(a_mat, rhs, left_side=True, lower=True, unit_diagonal=True)
    u, w = jnp.split(sol, 2, -1)
    attn_intra = jnp.einsum('bhnid,bhnjd->bhnij', qc, kc) * decay
    q_dec = qc * jnp.exp(gc)[..., None]
    g_last = gc[..., -1]
    k_dec = kc * jnp.exp(g_last[..., None] - gc)[..., None]

    def step(state, xs):
        u_i, w_i, qd_i, kd_i, a_i, gl_i = xs
        v_new = u_i - jnp.einsum('bhcd,bhde->bhce', w_i, state)
        o_i = jnp.einsum('bhcd,bhde->bhce', qd_i, state) + jnp.einsum('bhij,bhje->bhie', a_i, v_new)
        state = state * jnp.exp(gl_i)[..., None, None] + jnp.einsum('bhcd,bhce->bhde', kd_i, v_new)
        return state, o_i

    xs = tuple(jnp.moveaxis(t, 2, 0) for t in (u, w, q_dec, k_dec, attn_intra, g_last))
    state0 = jnp.zeros((b, h, d, d), jnp.float32)
    _, o = lax.scan(step, state0, xs)
    o = jnp.moveaxis(o, 0, 2).reshape(b, h, s, d)
    return jnp.swapaxes(o, 1, 2).astype(v.dtype)


def gdn_group(q, k, v, z, a, bg, conv_w, a_log, dt_bias, norm_w):
    bsz, s, _ = q.shape
    qkv = jax.nn.silu(causal_depthwise_conv(jnp.concatenate([q, k, v], -1), conv_w))
    q, k, v = [t.reshape(bsz, s, GDN_HEADS, HEAD_DIM) for t in jnp.split(qkv, 3, -1)]
    q = l2_normalize(q) * (HEAD_DIM ** -0.5)
    k = l2_normalize(k)
    beta = jax.nn.sigmoid(bg.astype(jnp.float32))
    g = -jnp.exp(a_log.astype(jnp.float32)) * jax.nn.softplus(a.astype(jnp.float32) + dt_bias.astype(jnp.float32))
    o = gated_delta_rule_chunked(q, k, v, g, beta)
    o = rms_norm(o, norm_w) * jax.nn.silu(z.reshape(bsz, s, GDN_HEADS, HEAD_DIM))
    return o.reshape(bsz, s, GDN_WIDTH)


def dsa_group(q, c_kv, q_idx, k_idx, w_idx, kv_norm_w, w_ukv, idxk_g, idxk_b, cos, sin, icos, isin):
    bsz, s, _ = q.shape
    k, v = jnp.split(rms_norm(c_kv, kv_norm_w) @ w_ukv, 2, -1)
    q = apply_rope(q.reshape(bsz, s, ATT_HEADS, HEAD_DIM), cos, sin)
    k = apply_rope(k.reshape(bsz, s, ATT_HEADS, HEAD_DIM), cos, sin)
    v = v.reshape(bsz, s, ATT_HEADS, HEAD_DIM)
    q_idx = apply_rope(q_idx.reshape(bsz, s, IDX_HEADS, IDX_DIM), icos, isin)
    k_idx = apply_rope(layer_norm(k_idx, idxk_g, idxk_b)[:, :, None, :], icos, isin)[:, :, 0]
    w_idx = w_idx.astype(jnp.float32) * (IDX_HEADS ** -0.5)
    top_k = min(INDEX_TOPK, s // 4)
    nb = s // Q_BLOCK
    kv_cat = jnp.concatenate([k, v], -1)
    key_pos = jnp.arange(s, dtype=jnp.int32)

    def blocks(t):
        return jnp.moveaxis(t.reshape((bsz, nb, Q_BLOCK) + t.shape[2:]), 1, 0)

    def attend_block(xs):
        q_b, qi_b, wi_b, start = xs
        q_pos = start + jnp.arange(Q_BLOCK, dtype=jnp.int32)
        visible = key_pos[None, :] <= q_pos[:, None]
        dots = jnp.einsum('bthd,bsd->bths', qi_b, k_idx, preferred_element_type=jnp.float32) * (IDX_DIM ** -0.5)
        index = jnp.einsum('bths,bth->bts', jax.nn.relu(dots), wi_b)
        index = jnp.where(visible[None], index, -jnp.inf)
        _, sel = lax.top_k(index, top_k)
        kv_sel = jax.vmap(lambda kvb, ib: kvb[ib])(kv_cat, sel)
        k_sel, v_sel = jnp.split(kv_sel, 2, -1)
        logits = jnp.einsum('bthd,btkhd->bthk', q_b, k_sel, preferred_element_type=jnp.float32) * (HEAD_DIM ** -0.5)
        ok = (sel <= q_pos[None, :, None])[:, :, None, :]
        p = jax.nn.softmax(jnp.where(ok, logits, -jnp.inf), axis=-1)
        return jnp.einsum('bthk,btkhd->bthd', p.astype(v_sel.dtype), v_sel)

    starts = jnp.arange(nb, dtype=jnp.int32) * Q_BLOCK
    o = lax.map(attend_block, (blocks(q), blocks(q_idx), blocks(w_idx), starts))
    return jnp.moveaxis(o, 0, 1).reshape(bsz, s, ATT_WIDTH)


def conv_ffn(x, w_up, conv_w, conv_b, w_down):
    u = causal_depthwise_conv(x @ w_up, conv_w) + conv_b
    gate, val = jnp.split(u, 2, -1)
    return (jax.nn.silu(gate) * val) @ w_down


def setup_inputs(seed: int = 0) -> dict:
    key = jax.random.key(seed)
    ks = jax.random.split(key, 20)
    L = DEPTH
    nrm = jax.random.normal
    dt = jnp.exp(jax.random.uniform(ks[4], (L, GDN_HEADS), minval=np.log(1e-3), maxval=np.log(1e-1)))
    return {
        "x": nrm(ks[0], (BATCH, SEQ, D_MODEL), jnp.float32),
        "w_in": nrm(ks[1], (L, D_MODEL, PROJ_WIDTH), jnp.float32) * D_MODEL ** -0.5,
        "gdn_conv_w": nrm(ks[2], (L, GDN_CONV, 3 * GDN_WIDTH), jnp.float32) * GDN_CONV ** -0.5,
        "gdn_a_log": jnp.log(jax.random.uniform(ks[3], (L, GDN_HEADS), minval=1.0, maxval=16.0)),
        "gdn_dt_bias": dt + jnp.log(-jnp.expm1(-dt)),
        "gdn_norm_w": 1.0 + 0.02 * nrm(ks[5], (L, HEAD_DIM), jnp.float32),
        "kv_norm_w": 1.0 + 0.02 * nrm(ks[6], (L, KV_RANK), jnp.float32),
        "w_ukv": nrm(ks[7], (L, KV_RANK, 2 * ATT_WIDTH), jnp.float32) * KV_RANK ** -0.5,
        "idx_k_norm_g": 1.0 + 0.02 * nrm(ks[8], (L, IDX_DIM), jnp.float32),
        "idx_k_norm_b": 0.02 * nrm(ks[9], (L, IDX_DIM), jnp.float32),
        "w_out": nrm(ks[10], (L, MIX_WIDTH, D_MODEL), jnp.float32) * (MIX_WIDTH ** -0.5 * DN_BETA),
        "ln1_g": 1.0 + 0.02 * nrm(ks[11], (L, D_MODEL), jnp.float32),
        "ln1_b": 0.02 * nrm(ks[12], (L, D_MODEL), jnp.float32),
        "ffn_up": nrm(ks[13], (L, D_MODEL, 2 * D_FF), jnp.float32) * D_MODEL ** -0.5,
        "ffn_conv_w": nrm(ks[14], (L, FFN_CONV, 2 * D_FF), jnp.float32) * FFN_CONV ** -0.5,
        "ffn_conv_b": 0.02 * nrm(ks[15], (L, 2 * D_FF), jnp.float32),
        "ffn_down": nrm(ks[16], (L, D_FF, D_MODEL), jnp.float32) * (D_FF ** -0.5 * DN_BETA),
        "ln2_g": 1.0 + 0.02 * nrm(ks[17], (L, D_MODEL), jnp.float32),
        "ln2_b": 0.02 * nrm(ks[18], (L, D_MODEL), jnp.float32),
    }


def reference(x, w_in, gdn_conv_w, gdn_a_log, gdn_dt_bias, gdn_norm_w, kv_norm_w, w_ukv,
              idx_k_norm_g, idx_k_norm_b, w_out, ln1_g, ln1_b, ffn_up, ffn_conv_w, ffn_conv_b,
              ffn_down, ln2_g, ln2_b):
    s = x.shape[1]
    cos, sin = rope_tables(s, HEAD_DIM)
    icos, isin = rope_tables(s, IDX_DIM)
    split_points = [int(p) for p in np.cumsum(PROJ_SIZES)[:-1]]
    for i in range(DEPTH):
        (gq, gk, gv, gz, ga, gb, aq, ckv, iq, ik, iw) = jnp.split(x @ w_in[i], split_points, axis=-1)
        o_gdn = gdn_group(gq, gk, gv, gz, ga, gb, gdn_conv_w[i], gdn_a_log[i], gdn_dt_bias[i], gdn_norm_w[i])
        o_dsa = dsa_group(aq, ckv, iq, ik, iw, kv_norm_w[i], w_ukv[i], idx_k_norm_g[i], idx_k_norm_b[i],
                          cos, sin, icos, isin)
        y = jnp.concatenate([o_gdn, o_dsa], -1) @ w_out[i]
        x = layer_norm(DN_ALPHA * x + y, ln1_g[i], ln1_b[i])
        f = conv_ffn(x, ffn_up[i], ffn_conv_w[i], ffn_conv_b[i], ffn_down[i])
        x = layer_norm(DN_ALPHA * x + f, ln2_g[i], ln2_b[i])
    return x
```

```python
import numpy as np
import concourse.bass as bass
import concourse.mybir as mybir
from concourse.bass_utils import run_bass_kernel_spmd

F32 = mybir.dt.float32
BF16 = mybir.dt.bfloat16
ALU = mybir.AluOpType
AF = mybir.ActivationFunctionType
AX = mybir.AxisListType

SEM_ROT = 30000


class Prog:
    ENGS = ("pe", "act", "dve", "pool", "sp")

    def __init__(self, nc, n_dma_sems=16):
        self.nc = nc
        self.recs = []
        self.state = {}
        self.n_dma_sems = n_dma_sems

    @staticmethod
    def _split(key):
        if isinstance(key, tuple):
            return key[0], key[1:]
        return key, None

    def _conflicts(self, key):
        base, sub = self._split(key)
        d = self.state.get(base)
        if not d:
            return []
        if sub is None:
            return list(d.values())
        out = []
        if sub in d:
            out.append(d[sub])
        if None in d:
            out.append(d[None])
        return out

    def op(self, eng, fn, r=(), w=(), dma=False):
        oid = len(self.recs)
        deps = set()
        for k in r:
            for st in self._conflicts(k):
                if st[0] is not None:
                    deps.add(st[0])
        for k in w:
            for st in self._conflicts(k):
                if st[0] is not None:
                    deps.add(st[0])
                deps.update(st[1])
        deps.discard(oid)
        self.recs.append(dict(id=oid, eng=eng, fn=fn, deps=sorted(deps), dma=dma, signal=dma))
        for k in r:
            base, sub = self._split(k)
            st = self.state.setdefault(base, {}).setdefault(sub, [None, []])
            st[1].append(oid)
        for k in w:
            base, sub = self._split(k)
            d = self.state.setdefault(base, {})
            if sub is None:
                d.clear()
            d[sub] = [oid, []]
        return oid

    def pe(self, fn, r=(), w=()):
        return self.op("pe", fn, r, w)

    def act(self, fn, r=(), w=()):
        return self.op("act", fn, r, w)

    def dve(self, fn, r=(), w=()):
        return self.op("dve", fn, r, w)

    def pool(self, fn, r=(), w=()):
        return self.op("pool", fn, r, w)

    def dma(self, out, in_, r=(), w=(), eng="sp", **kw):
        return self.op(eng, lambda e: e.dma_start(out=out, in_=in_, **kw), r, w, dma=True)

    def emit(self):
        nc = self.nc
        recs = self.recs
        for rec in recs:
            for d in rec["deps"]:
                recs[d]["signal"] = True
        cnt = {e: 0 for e in self.ENGS}
        sems = {}

        def get_sem(name):
            if name not in sems:
                sems[name] = nc.alloc_semaphore(name)
            return sems[name]

        dma_tot = [0] * self.n_dma_sems
        dma_rr = 0
        per_eng = {e: [] for e in self.ENGS}
        for rec in recs:
            e = rec["eng"]
            per_eng[e].append(rec)
            if rec["dma"]:
                i = dma_rr % self.n_dma_sems
                dma_rr += 1
                rec["prev_ev"] = (f"dq{i}", dma_tot[i]) if dma_tot[i] > 0 else None
                dma_tot[i] += 16
                rec["ev"] = (f"dq{i}", dma_tot[i])
                rec["inc"] = 16
            elif rec["signal"]:
                c = cnt[e]
                cnt[e] += 1
                rec["ev"] = (f"c_{e}_{c // SEM_ROT}", c % SEM_ROT + 1)
                rec["inc"] = 1
            else:
                rec["ev"] = None
        final_dma = [(f"dq{i}", dma_tot[i]) for i in range(self.n_dma_sems) if dma_tot[i] > 0]
        for name, _ in final_dma:
            get_sem(name)
        for rec in recs:
            if rec["ev"] is not None:
                get_sem(rec["ev"][0])

        def run_engine(ename, eng_obj, extra_final=False):
            waited = {}
            for rec in per_eng[ename]:
                evs = [recs[d]["ev"] for d in rec["deps"]]
                if rec["dma"] and rec["prev_ev"] is not None:
                    evs.append(rec["prev_ev"])
                for (sn, v) in evs:
                    if waited.get(sn, 0) < v:
                        eng_obj.wait_ge(sems[sn], v)
                        waited[sn] = v
                ins = rec["fn"](eng_obj)
                if rec["ev"] is not None:
                    ins.then_inc(sems[rec["ev"][0]], rec["inc"])
            if extra_final:
                for (sn, v) in final_dma:
                    if waited.get(sn, 0) < v:
                        eng_obj.wait_ge(sems[sn], v)
                        waited[sn] = v

        with nc.Block() as block:
            @block.tensor
            def _(eng):
                run_engine("pe", eng)

            @block.scalar
            def _(eng):
                run_engine("act", eng)

            @block.vector
            def _(eng):
                run_engine("dve", eng)

            @block.gpsimd
            def _(eng):
                run_engine("pool", eng)

            @block.sync
            def _(eng):
                run_engine("sp", eng, extra_final=True)
        return nc
import ml_dtypes

D = 2048; DFF = 5632; S = 8192; TS = 512; HALO = 2; TT = TS + HALO; NSH = 2; NCORE = 8
LN_EPS = 1e-5; RMS_EPS = 1e-6
DEPTH = 4
DN_ALPHA = (2 * DEPTH) ** 0.25
KC = D // 128
R_GAB = 4096; R_AQ = 4112; R_IQ = 5136; R_IK = 6160; R_IW = 6224; R_K = 6240; R_V = 7264; R_TOT = 8288


class Rot:
    def __init__(self, items):
        self.items = items; self.i = 0
    def next(self):
        it = self.items[self.i % len(self.items)]; self.i += 1
        return it


def build_A(do_post, do_proj):
    nc = bass.Bass("TRN2", target_bir_lowering=False)
    P = Prog(nc)
    def din(name, shape, dt=F32):
        return nc.dram_tensor(name, shape, dt, kind="ExternalInput").ap()
    def dout(name, shape, dt=F32):
        return nc.dram_tensor(name, shape, dt, kind="ExternalOutput").ap()
    def sb(name, shape, dt):
        return nc.alloc_sbuf_tensor("s_" + name, shape, dt)

    T_in = TT if do_post else TS
    xT_d = din("xT", [NSH, D, T_in])
    if do_post:
        mixT_d = din("mixT", [NSH, D, TT], BF16)
        wout_d = din("w_out", [D, D], BF16)
        wup_d = din("ffn_up", [D, 2 * DFF], BF16)
        wdn_d = din("ffn_down", [DFF, D], BF16)
        lnp_d = din("lnp", [128, 4, KC])
        cw_d = din("convw", [128, 88, 4])
        hf_d = din("haloflag", [128, NSH])
        x2T_o = dout("x2T", [NSH, D, TS])
    if do_proj:
        win_d = din("w_in", [D, 6496], BF16)
        wukv_d = din("w_ukv", [256, 2048], BF16)
        kvnw_d = din("kvnw", [128, 2])
        idxgb_d = din("idxgb", [64, 2])
        rope_d = din("rope", [NSH, 4, 128, TS])
        rmat_d = din("rmat", [2, 128, 128])
        pT_o = dout("pT", [NSH, R_TOT, TS])

    xT = sb("xT", [128, KC, TT], F32)
    xb = sb("xb", [128, KC, TT], BF16)
    wp = [sb(f"wp{i}", [128, 8192], BF16) for i in range(2)]
    ones = sb("ones", [128, 128], F32)
    st = [sb(f"st{i}", [128, 512], F32) for i in range(6)]
    strot = Rot(list(range(6)))
    PS = [nc.alloc_psum_tensor(f"ps{i}", [128, 512], F32) for i in range(8)]
    mmrot = Rot([0, 1, 2, 3])
    auxrot = Rot([4, 5])
    P.pool(lambda e: e.memset(ones[:], 1.0), w=["ones"])
    epsb = sb("epsb", [128, 2], F32)
    P.pool(lambda e: e.memset(epsb[:, 0:1], LN_EPS), w=[("epsb", 0)])
    P.pool(lambda e: e.memset(epsb[:, 1:2], RMS_EPS), w=[("epsb", 1)])
    if do_post:
        aT = sb("aT", [128, DFF // 128, TS], BF16)
        lnp = sb("lnp", [128, 4, KC], F32)
        cw = sb("cw", [128, 88, 4], F32)
        hf = sb("hf", [128, NSH], F32)
        hfull = [sb(f"hfull{i}", [128, TT], F32) for i in range(6)]
        uu = [sb(f"uu{i}", [128, TS], F32) for i in range(6)]
        P.dma(lnp[:], lnp_d, w=["lnp"])
        P.dma(cw[:], cw_d, w=["cw"])
        P.dma(hf[:], hf_d, w=["hf"])
        stat = [sb(f"stat{i}", [128, 512], F32) for i in range(4)]
    if do_proj:
        kvnw = sb("kvnw", [128, 2], F32)
        idxgb = sb("idxgb", [64, 2], F32)
        rope = sb("rope", [128, 4, TS], F32)
        rmat = sb("rmat", [128, 2, 128], F32)
        ckv = sb("ckv", [128, 2, TS], F32)
        ckvn = sb("ckvn", [128, 2, TS], BF16)
        wukv = sb("wukv", [128, 2, 2048], BF16)
        P.dma(kvnw[:], kvnw_d, w=["kvnw"])
        P.dma(idxgb[:], idxgb_d, w=["idxgb"])
        P.dma(rmat[:], rmat_d.rearrange("r p m -> p r m"), w=["rmat"])
        P.dma(wukv[:], wukv_d.rearrange("(c p) n -> p c n", p=128), w=["wukv"])
        if not do_post:
            stat = [sb(f"stat{i}", [128, 512], F32) for i in range(4)]

    wp_i = [0]

    def gemm(W_d, Kc, groups, rhs_fn, chunks, epilogue, pre=None):
        pw = 512 if Kc <= 16 else 128
        panels = []
        cur = []
        for gi, (c0, m) in enumerate(groups):
            if cur and (c0 + m - groups[cur[0]][0] > pw or c0 != groups[cur[-1]][0] + groups[cur[-1]][1]):
                panels.append(cur); cur = []
            cur.append(gi)
        if cur:
            panels.append(cur)
        Wv = W_d.rearrange("(c p) n -> p c n", p=128)

        def load(pi):
            b = wp_i[0] % 2; wp_i[0] += 1
            g0 = groups[panels[pi][0]][0]
            g1 = groups[panels[pi][-1]][0] + groups[panels[pi][-1]][1]
            wdt = g1 - g0
            view = wp[b][:, 0:Kc * wdt].rearrange("p (c n) -> p c n", c=Kc)
            P.dma(view, Wv[:, :, g0:g1], w=[f"wp{b}"])
            return b, g0, wdt

        nxt = load(0)
        for pi, pan in enumerate(panels):
            b, g0, wdt = nxt
            if pi + 1 < len(panels):
                nxt = load(pi + 1)
            view = wp[b][:, 0:Kc * wdt].rearrange("p (c n) -> p c n", c=Kc)
            for gi in pan:
                c0, m = groups[gi]
                for ci, (t0, n) in enumerate(chunks):
                    pidx = mmrot.next()
                    for k in range(Kc):
                        rap, rkeys = rhs_fn(k, t0, n)
                        P.pe(lambda e, pidx=pidx, k=k, rap=rap, c0=c0, m=m, n=n, view=view, g0=g0, Kc=Kc:
                             e.matmul(PS[pidx][0:m, 0:n], lhsT=view[:, k, c0 - g0:c0 - g0 + m], rhs=rap,
                                      start=(k == 0), stop=(k == Kc - 1)),
                             r=[f"wp{b}"] + rkeys, w=[f"ps{pidx}"])
                    epilogue(gi, ci, pidx, m, t0, n)

    def ln_feature_major(shard, which, t0, n):
        gi_, bi_ = (0, 1) if which == 1 else (2, 3)
        p1 = auxrot.next(); p2 = auxrot.next()
        for c in range(KC):
            s = strot.next()
            P.act(lambda e, c=c, s=s: e.activation(out=st[s][:, 0:n], in_=xT[:, c, t0:t0 + n], func=AF.Square),
                  r=[("xT", c)], w=[f"st{s}"])
            P.pe(lambda e, c=c: e.matmul(PS[p1][:, 0:n], lhsT=ones[:], rhs=xT[:, c, t0:t0 + n], start=(c == 0), stop=(c == KC - 1)),
                 r=["ones", ("xT", c)], w=[f"ps{p1}"])
            P.pe(lambda e, c=c, s=s: e.matmul(PS[p2][:, 0:n], lhsT=ones[:], rhs=st[s][:, 0:n], start=(c == 0), stop=(c == KC - 1)),
                 r=["ones", f"st{s}"], w=[f"ps{p2}"])
        mean, ex2, rstd, mr = stat
        P.act(lambda e: e.mul(out=mean[:, 0:n], in_=PS[p1][:, 0:n], mul=1.0 / D), r=[f"ps{p1}"], w=["stat0"])
        P.act(lambda e: e.mul(out=ex2[:, 0:n], in_=PS[p2][:, 0:n], mul=1.0 / D), r=[f"ps{p2}"], w=["stat1"])
        P.dve(lambda e: e.tensor_tensor(out=mr[:, 0:n], in0=mean[:, 0:n], in1=mean[:, 0:n], op=ALU.mult), r=["stat0"], w=["stat3"])
        P.dve(lambda e: e.tensor_tensor(out=ex2[:, 0:n], in0=ex2[:, 0:n], in1=mr[:, 0:n], op=ALU.subtract), r=["stat1", "stat3"], w=["stat1"])
        P.act(lambda e: e.activation(out=rstd[:, 0:n], in_=ex2[:, 0:n], func=AF.Sqrt, bias=epsb[:, 0:1], scale=1.0), r=["stat1", "epsb"], w=["stat2"])
        P.dve(lambda e: e.reciprocal(out=rstd[:, 0:n], in_=rstd[:, 0:n]), r=["stat2"], w=["stat2"])
        P.dve(lambda e: e.tensor_tensor(out=mr[:, 0:n], in0=mean[:, 0:n], in1=rstd[:, 0:n], op=ALU.mult), r=["stat0", "stat2"], w=["stat3"])
        for c in range(KC):
            s = strot.next()
            P.pool(lambda e, c=c, s=s: e.tensor_tensor(out=st[s][:, 0:n], in0=xT[:, c, t0:t0 + n], in1=mean[:, 0:n], op=ALU.subtract),
                   r=[("xT", c), "stat0"], w=[f"st{s}"])
            P.dve(lambda e, c=c, s=s: e.scalar_tensor_tensor(out=st[s][:, 0:n], in0=st[s][:, 0:n], scalar=lnp[:, gi_, c:c + 1], in1=rstd[:, 0:n], op0=ALU.mult, op1=ALU.mult),
                  r=[f"st{s}", "lnp", "stat2"], w=[f"st{s}"])
            P.act(lambda e, c=c, s=s: e.activation(out=xT[:, c, t0:t0 + n], in_=st[s][:, 0:n], func=AF.Identity, bias=lnp[:, bi_, c:c + 1], scale=1.0),
                  r=[f"st{s}", "lnp"], w=[("xT", c)])
            P.act(lambda e, c=c, s=s: e.activation(out=xb[:, c, t0:t0 + n], in_=st[s][:, 0:n], func=AF.Identity, bias=lnp[:, bi_, c:c + 1], scale=1.0),
                  r=[f"st{s}", "lnp"], w=[("xb", c)])

    def out_rows(shard, row0, m, src_ap, keys):
        P.dma(pT_o[shard, row0:row0 + m, :], src_ap, r=keys)

    for sh in range(NSH):
        for c4 in range(4):
            P.dma(xT[:, 4 * c4:4 * c4 + 4, 0:T_in], xT_d[sh, 512 * c4:512 * (c4 + 1), :].rearrange("(c p) t -> p c t", p=128),
                  w=[("xT", 4 * c4 + j) for j in range(4)])
        if do_post:
            for c4 in range(4):
                P.dma(xb[:, 4 * c4:4 * c4 + 4, :], mixT_d[sh, 512 * c4:512 * (c4 + 1), :].rearrange("(c p) t -> p c t", p=128),
                      w=[("xb", 4 * c4 + j) for j in range(4)])
            chunks_h = [(0, HALO), (HALO, TS)]
            def ep2(gi, ci, pidx, m, t0, n):
                P.dve(lambda e: e.scalar_tensor_tensor(out=xT[:, gi, t0:t0 + n], in0=xT[:, gi, t0:t0 + n], scalar=DN_ALPHA, in1=PS[pidx][:, 0:n], op0=ALU.mult, op1=ALU.add),
                      r=[("xT", gi), f"ps{pidx}"], w=[("xT", gi)])
            gemm(wout_d, KC, [(g * 128, 128) for g in range(KC)], lambda k, t0, n: (xb[:, k, t0:t0 + n], [("xb", k)]), chunks_h, ep2)
            ln_feature_major(sh, 1, 0, HALO)
            ln_feature_major(sh, 1, HALO, TS)
            NG = DFF // 128
            groups3 = []
            for j in range(NG):
                groups3.append((j * 128, 128))
            for j in range(NG):
                groups3.append((DFF + j * 128, 128))
            order = []
            for j0 in range(0, NG, 4):
                order += [(j * 128, 128) for j in range(j0, j0 + 4)] + [(DFF + j * 128, 128) for j in range(j0, j0 + 4)]
            hmap = {}
            def ep3(gi, ci, pidx, m, t0, n, order=order, sh=sh):
                c0 = order[gi][0]
                isval = c0 >= DFF
                j = (c0 - DFF) // 128 if isval else c0 // 128
                hb = (4 + j % 2) if isval else (j % 4)
                gb_ = j % 4
                ch = j + (NG if isval else 0)
                if ci == 0:
                    P.act(lambda e: e.activation(out=hfull[hb][:, 0:HALO], in_=PS[pidx][:, 0:HALO], func=AF.Copy, scale=hf[:, sh:sh + 1]),
                          r=[f"ps{pidx}", "hf"], w=[(f"hfull{hb}", 0)])
                    return
                P.act(lambda e: e.copy(out=hfull[hb][:, HALO:TT], in_=PS[pidx][:, 0:TS]), r=[f"ps{pidx}"], w=[(f"hfull{hb}", 1)])
                eng = P.dve
                u = uu[hb]
                eng(lambda e: e.tensor_scalar(out=u[:], in0=hfull[hb][:, 2:2 + TS], scalar1=cw[:, ch, 2:3], scalar2=cw[:, ch, 3:4], op0=ALU.mult, op1=ALU.add),
                    r=[f"hfull{hb}", "cw"], w=[f"uu{hb}"])
                eng(lambda e: e.scalar_tensor_tensor(out=u[:], in0=hfull[hb][:, 1:1 + TS], scalar=cw[:, ch, 1:2], in1=u[:], op0=ALU.mult, op1=ALU.add),
                    r=[f"hfull{hb}", "cw", f"uu{hb}"], w=[f"uu{hb}"])
                eng(lambda e: e.scalar_tensor_tensor(out=u[:], in0=hfull[hb][:, 0:TS], scalar=cw[:, ch, 0:1], in1=u[:], op0=ALU.mult, op1=ALU.add),
                    r=[f"hfull{hb}", "cw", f"uu{hb}"], w=[f"uu{hb}"])
                if isval:
                    ug = uu[gb_]
                    P.act(lambda e: e.activation(out=ug[:], in_=ug[:], func=AF.Silu), r=[f"uu{gb_}"], w=[f"uu{gb_}"])
                    P.pool(lambda e: e.tensor_tensor(out=aT[:, j, :], in0=ug[:], in1=u[:], op=ALU.mult), r=[f"uu{gb_}", f"uu{hb}"], w=[("aT", j)])
            gemm(wup_d, KC, order, lambda k, t0, n: (xb[:, k, t0:t0 + n], [("xb", k)]), chunks_h, ep3)
            def ep4(gi, ci, pidx, m, t0, n):
                P.dve(lambda e: e.scalar_tensor_tensor(out=xT[:, gi, HALO:TT], in0=xT[:, gi, HALO:TT], scalar=DN_ALPHA, in1=PS[pidx][:, 0:TS], op0=ALU.mult, op1=ALU.add),
                      r=[("xT", gi), f"ps{pidx}"], w=[("xT", gi)])
            gemm(wdn_d, DFF // 128, [(g * 128, 128) for g in range(KC)], lambda k, t0, n: (aT[:, k, :], [("aT", k)]), [(0, TS)], ep4)
            ln_feature_major(sh, 2, HALO, TS)
            for c4 in range(4):
                P.dma(x2T_o[sh, 512 * c4:512 * (c4 + 1), :].rearrange("(c p) t -> p c t", p=128), xT[:, 4 * c4:4 * c4 + 4, HALO:TT],
                      r=[("xT", 4 * c4 + j) for j in range(4)])
            xoff = HALO
        else:
            for c in range(KC):
                P.act(lambda e, c=c: e.copy(out=xb[:, c, 0:TS], in_=xT[:, c, 0:TS]), r=[("xT", c)], w=[("xb", c)])
            xoff = 0
        if not do_proj:
            continue
        P.dma(rope[:], rope_d[sh].rearrange("f p t -> p f t"), w=["rope"])
        segs = []
        for g in range(32):
            segs.append((g * 128, 128, "plain", g * 128))
        segs.append((4096, 16, "plain", R_GAB))
        for g in range(2):
            segs.append((5136 + g * 128, 128, "ckv", g))
        for g in range(8):
            segs.append((4112 + g * 128, 128, "rope128", R_AQ + g * 128))
        for g in range(8):
            segs.append((5392 + g * 128, 128, "rope64", R_IQ + g * 128))
        segs.append((6416, 64, "ik", R_IK))
        segs.append((6480, 16, "plain", R_IW))

        def rope_ep(src_sb, skey, m, ridx, tabc, tabs, outrow, sh=sh):
            pa = auxrot.next()
            P.pe(lambda e: e.matmul(PS[pa][0:m, 0:TS], lhsT=rmat[0:m, ridx, 0:m], rhs=src_sb, start=True, stop=True), r=["rmat", skey], w=[f"ps{pa}"])
            s1 = strot.next(); s2 = strot.next()
            P.dve(lambda e: e.tensor_tensor(out=st[s1][0:m, :], in0=src_sb, in1=rope[0:m, tabc, :], op=ALU.mult), r=[skey, "rope"], w=[f"st{s1}"])
            P.dve(lambda e: e.tensor_tensor(out=st[s2][0:m, :], in0=PS[pa][0:m, 0:TS], in1=rope[0:m, tabs, :], op=ALU.mult), r=[f"ps{pa}", "rope"], w=[f"st{s2}"])
            P.pool(lambda e: e.tensor_tensor(out=st[s1][0:m, :], in0=st[s1][0:m, :], in1=st[s2][0:m, :], op=ALU.add), r=[f"st{s1}", f"st{s2}"], w=[f"st{s1}"])
            out_rows(sh, outrow, m, st[s1][0:m, :], [f"st{s1}"])

        def ep1(gi, ci, pidx, m, t0, n, sh=sh):
            c0, m_, kind, orow = segs[gi]
            if kind == "plain":
                s = strot.next()
                P.act(lambda e: e.copy(out=st[s][0:m, :], in_=PS[pidx][0:m, 0:TS]), r=[f"ps{pidx}"], w=[f"st{s}"])
                out_rows(sh, orow, m, st[s][0:m, :], [f"st{s}"])
            elif kind in ("rope128", "rope64"):
                s = strot.next()
                P.act(lambda e: e.copy(out=st[s][0:m, :], in_=PS[pidx][0:m, 0:TS]), r=[f"ps{pidx}"], w=[f"st{s}"])
                if kind == "rope128":
                    rope_ep(st[s][0:m, :], f"st{s}", m, 0, 0, 1, orow)
                else:
                    rope_ep(st[s][0:m, :], f"st{s}", m, 1, 2, 3, orow)
            elif kind == "ckv":
                g = orow
                P.act(lambda e: e.copy(out=ckv[:, g, :], in_=PS[pidx][:, 0:TS]), r=[f"ps{pidx}"], w=[("ckv", g)])
                if g == 1:
                    pa = auxrot.next()
                    for j in range(2):
                        s = strot.next()
                        P.act(lambda e, j=j, s=s: e.activation(out=st[s][:], in_=ckv[:, j, :], func=AF.Square), r=[("ckv", j)], w=[f"st{s}"])
                        P.pe(lambda e, j=j, s=s: e.matmul(PS[pa][:, 0:TS], lhsT=ones[:], rhs=st[s][:], start=(j == 0), stop=(j == 1)), r=["ones", f"st{s}"], w=[f"ps{pa}"])
                    rs = stat[2]
                    P.act(lambda e: e.activation(out=rs[:], in_=PS[pa][:, 0:TS], func=AF.Sqrt, bias=epsb[:, 1:2], scale=1.0 / 256), r=[f"ps{pa}", "epsb"], w=["stat2"])
                    P.dve(lambda e: e.reciprocal(out=rs[:], in_=rs[:]), r=["stat2"], w=["stat2"])
                    for j in range(2):
                        P.dve(lambda e, j=j: e.scalar_tensor_tensor(out=ckvn[:, j, :], in0=ckv[:, j, :], scalar=kvnw[:, j:j + 1], in1=rs[:], op0=ALU.mult, op1=ALU.mult),
                              r=[("ckv", j), "kvnw", "stat2"], w=[("ckvn", j)])
                    for hg in range(16):
                        pidx2 = mmrot.next()
                        for k in range(2):
                            P.pe(lambda e, k=k, hg=hg, pidx2=pidx2: e.matmul(PS[pidx2][:, 0:TS], lhsT=wukv[:, k, hg * 128:(hg + 1) * 128], rhs=ckvn[:, k, :], start=(k == 0), stop=(k == 1)),
                                 r=["wukv", ("ckvn", k)], w=[f"ps{pidx2}"])
                        s = strot.next()
                        P.act(lambda e, s=s, pidx2=pidx2: e.copy(out=st[s][:], in_=PS[pidx2][:, 0:TS]), r=[f"ps{pidx2}"], w=[f"st{s}"])
                        if hg < 8:
                            rope_ep(st[s][:], f"st{s}", 128, 0, 0, 1, R_K + hg * 128)
                        else:
                            out_rows(sh, R_V + (hg - 8) * 128, 128, st[s][:], [f"st{s}"])
            elif kind == "ik":
                s = strot.next(); s2 = strot.next()
                P.act(lambda e: e.copy(out=st[s][0:64, :], in_=PS[pidx][0:64, 0:TS]), r=[f"ps{pidx}"], w=[f"st{s}"])
                P.act(lambda e: e.activation(out=st[s2][0:64, :], in_=st[s][0:64, :], func=AF.Square), r=[f"st{s}"], w=[f"st{s2}"])
                p1 = auxrot.next(); p2 = auxrot.next()
                P.pe(lambda e: e.matmul(PS[p1][0:64, 0:TS], lhsT=ones[0:64, 0:64], rhs=st[s][0:64, :], start=True, stop=True), r=["ones", f"st{s}"], w=[f"ps{p1}"])
                P.pe(lambda e: e.matmul(PS[p2][0:64, 0:TS], lhsT=ones[0:64, 0:64], rhs=st[s2][0:64, :], start=True, stop=True), r=["ones", f"st{s2}"], w=[f"ps{p2}"])
                mean, ex2, rstd, mr = stat
                P.act(lambda e: e.mul(out=mean[0:64, :], in_=PS[p1][0:64, 0:TS], mul=1.0 / 64), r=[f"ps{p1}"], w=["stat0"])
                P.act(lambda e: e.mul(out=ex2[0:64, :], in_=PS[p2][0:64, 0:TS], mul=1.0 / 64), r=[f"ps{p2}"], w=["stat1"])
                P.dve(lambda e: e.tensor_tensor(out=mr[0:64, :], in0=mean[0:64, :], in1=mean[0:64, :], op=ALU.mult), r=["stat0"], w=["stat3"])
                P.dve(lambda e: e.tensor_tensor(out=ex2[0:64, :], in0=ex2[0:64, :], in1=mr[0:64, :], op=ALU.subtract), r=["stat1", "stat3"], w=["stat1"])
                P.act(lambda e: e.activation(out=rstd[0:64, :], in_=ex2[0:64, :], func=AF.Sqrt, bias=epsb[0:64, 0:1], scale=1.0), r=["stat1", "epsb"], w=["stat2"])
                P.dve(lambda e: e.reciprocal(out=rstd[0:64, :], in_=rstd[0:64, :]), r=["stat2"], w=["stat2"])
                P.dve(lambda e: e.tensor_tensor(out=st[s][0:64, :], in0=st[s][0:64, :], in1=mean[0:64, :], op=ALU.subtract), r=[f"st{s}", "stat0"], w=[f"st{s}"])
                P.dve(lambda e: e.tensor_tensor(out=st[s][0:64, :], in0=st[s][0:64, :], in1=rstd[0:64, :], op=ALU.mult), r=[f"st{s}", "stat2"], w=[f"st{s}"])
                P.dve(lambda e: e.tensor_scalar(out=st[s][0:64, :], in0=st[s][0:64, :], scalar1=idxgb[:, 0:1], scalar2=idxgb[:, 1:2], op0=ALU.mult, op1=ALU.add), r=[f"st{s}", "idxgb"], w=[f"st{s}"])
                rope_ep(st[s][0:64, :], f"st{s}", 64, 1, 2, 3, orow)

        gemm(win_d, KC, [(c0, m) for (c0, m, _, _) in segs], lambda k, t0, n: (xb[:, k, xoff + t0:xoff + t0 + n], [("xb", k)]), [(0, TS)], ep1)
    P.emit()
    return nc

S = 8192; NSLOT = 8; NIT = 22; TOPK = 256
NEG = -1.0e30


def slot_qbs(core):
    return [core, 15 - core, 16 + core, 31 - core, 32 + core, 47 - core, 48 + core, 63 - core]


def build_I(nslot=NSLOT, nit=NIT):
    nc = bass.Bass("TRN2", target_bir_lowering=False)
    P = Prog(nc)
    def din(name, shape, dt=F32):
        return nc.dram_tensor(name, shape, dt, kind="ExternalInput").ap()
    def dout(name, shape, dt=F32):
        return nc.dram_tensor(name, shape, dt, kind="ExternalOutput").ap()
    def sb(name, shape, dt):
        return nc.alloc_sbuf_tensor("s_" + name, shape, dt)

    iqT_d = din("iqT", [NSLOT, 1024, 128])
    kiT_d = din("kiT2", [128, S])
    iw_d = din("iw", [128, NSLOT, 16])
    qpos_d = din("qpos", [128, NSLOT])
    m_o = [dout(f"m{j}", [128, 8 * (j + 1), 128], BF16) for j in range(nslot)]

    kis = sb("kis", [128, 2048], F32)
    kif = sb("kif", [128, 2048], F32)
    ki = sb("ki", [128, S], BF16)
    kl = sb("kl", [64, S], BF16)
    qis = sb("qis", [128, 16, 128], F32)
    qif = sb("qif", [128, 16, 128], F32)
    qh = sb("qh", [128, 16, 128], BF16)
    qi = sb("qi", [128, 16, 128], BF16)
    iw = sb("iw", [128, NSLOT, 16], F32)
    qpos = sb("qpos", [128, NSLOT], F32)
    acc = sb("acc", [128, S], F32)
    junk = sb("junk", [128, S], BF16)
    msk = sb("msk", [128, S], BF16)
    rl = [sb(f"rl{i}", [128, 512], F32) for i in range(4)]
    kpos = sb("kpos", [128, 1024], F32)
    pen = sb("pen", [128, 1024], F32)
    identf = sb("identf", [128, 128], F32)
    ident = sb("ident", [128, 128], BF16)
    sm = sb("sm", [128, 8], F32)
    mst = [sb(f"mst{i}", [128, 4, 128], BF16) for i in range(3)]
    PS = [nc.alloc_psum_tensor(f"ps{i}", [128, 512], F32) for i in range(6)]
    PT = [nc.alloc_psum_tensor(f"pt{i}", [128, 4, 128], BF16) for i in range(2)]

    P.pool(lambda e: e.memset(identf[:], 0.0), w=["identf"])
    P.pool(lambda e: e.affine_select(out=identf[:], in_=identf[:], pattern=[[-1, 128]], compare_op=ALU.not_equal, fill=1.0, base=0, channel_multiplier=1), r=["identf"], w=["identf"])
    P.pool(lambda e: e.tensor_copy(out=ident[:], in_=identf[:]), r=["identf"], w=["ident"])
    P.dma(iw[:], iw_d, w=["iw"])
    P.dma(qpos[:], qpos_d, w=["qpos"])
    for c in range(4):
        P.dma(kis[:], kiT_d[:, c * 2048:(c + 1) * 2048], w=["kis"])
        P.act(lambda e, c=c: e.copy(out=ki[:, c * 2048:(c + 1) * 2048], in_=kis[:]), r=["kis"], w=[("ki", c)])
        P.dve(lambda e, c=c: e.tensor_copy(out=kif[0:64, :], in_=ki[0:64, c * 2048:(c + 1) * 2048]), r=[("ki", c)], w=["kif"])
        P.dve(lambda e, c=c: e.tensor_tensor(out=kl[:, c * 2048:(c + 1) * 2048], in0=kis[0:64, :], in1=kif[0:64, :], op=ALU.subtract), r=["kis", "kif"], w=[("kl", c)])

    ri = 0; pi = 0; ti = 0; mi = 0
    for j in range(nslot):
        L = 1024 * (j + 1)
        P.dma(qis[0:64, :, :], iqT_d[j].rearrange("(h d) t -> d h t", d=64), w=[("qis", 0)])
        P.dma(qis[64:128, :, :], iqT_d[j].rearrange("(h d) t -> d h t", d=64), w=[("qis", 1)])
        P.act(lambda e: e.copy(out=qh[:], in_=qis[:]), r=["qis"], w=["qh"])
        P.act(lambda e: e.copy(out=qi[0:64, :, :], in_=qh[0:64, :, :]), r=["qh"], w=[("qi", 0)])
        P.pool(lambda e: e.tensor_copy(out=qif[64:128, :, :], in_=qh[64:128, :, :]), r=["qh"], w=["qif"])
        P.pool(lambda e: e.tensor_tensor(out=qi[64:128, :, :], in0=qis[64:128, :, :], in1=qif[64:128, :, :], op=ALU.subtract), r=["qis", "qif"], w=[("qi", 1)])
        for kt in range(L // 512):
            for h in range(16):
                pidx = pi % 6; pi += 1
                P.pe(lambda e, h=h, kt=kt, pidx=pidx: e.matmul(PS[pidx][:, :], lhsT=qi[:, h, :], rhs=ki[:, kt * 512:(kt + 1) * 512], start=True, stop=False),
                     r=["qi", ("ki", kt // 4)], w=[f"ps{pidx}"])
                P.pe(lambda e, h=h, kt=kt, pidx=pidx: e.matmul(PS[pidx][:, :], lhsT=qi[0:64, h, :], rhs=kl[0:64, kt * 512:(kt + 1) * 512], start=False, stop=True),
                     r=["qi", ("kl", kt // 4)], w=[f"ps{pidx}"])
                r_ = ri % 4; ri += 1
                P.act(lambda e, r_=r_, pidx=pidx: e.activation(out=rl[r_][:], in_=PS[pidx][:, :], func=AF.Relu), r=[f"ps{pidx}"], w=[f"rl{r_}"])
                if h == 0:
                    P.dve(lambda e, r_=r_, kt=kt, j=j: e.tensor_scalar(out=acc[:, kt * 512:(kt + 1) * 512], in0=rl[r_][:], scalar1=iw[:, j, 0:1], scalar2=None, op0=ALU.mult),
                          r=[f"rl{r_}", "iw"], w=[("acc", kt)])
                else:
                    P.dve(lambda e, r_=r_, kt=kt, j=j, h=h: e.scalar_tensor_tensor(out=acc[:, kt * 512:(kt + 1) * 512], in0=rl[r_][:], scalar=iw[:, j, h:h + 1], in1=acc[:, kt * 512:(kt + 1) * 512], op0=ALU.mult, op1=ALU.add),
                          r=[f"rl{r_}", "iw", ("acc", kt)], w=[("acc", kt)])
        P.dve(lambda e, L=L: e.tensor_reduce(out=sm[:, 0:1], in_=acc[:, 0:L], axis=AX.X, op=ALU.min), r=["acc"], w=[("sm", 0)])
        P.dve(lambda e, L=L: e.tensor_reduce(out=sm[:, 5:6], in_=acc[:, 0:L], axis=AX.X, op=ALU.max), r=["acc"], w=[("sm", 5)])
        P.dve(lambda e: e.tensor_tensor(out=sm[:, 1:2], in0=sm[:, 5:6], in1=sm[:, 0:1], op=ALU.subtract), r=[("sm", 5), ("sm", 0)], w=[("sm", 1)])
        P.dve(lambda e: e.tensor_scalar(out=sm[:, 1:2], in0=sm[:, 1:2], scalar1=0.5, scalar2=1e-6, op0=ALU.mult, op1=ALU.add), r=[("sm", 1)], w=[("sm", 1)])
        P.pool(lambda e, j=j: e.iota(kpos[:], pattern=[[1, 1024]], base=1024 * j, channel_multiplier=0, allow_small_or_imprecise_dtypes=True), w=["kpos"])
        P.dve(lambda e, j=j: e.tensor_scalar(out=pen[:], in0=kpos[:], scalar1=qpos[:, j:j + 1], scalar2=NEG, op0=ALU.is_gt, op1=ALU.mult), r=["kpos", "qpos"], w=["pen"])
        P.dve(lambda e, L=L: e.tensor_tensor(out=acc[:, L - 1024:L], in0=acc[:, L - 1024:L], in1=pen[:], op=ALU.add), r=["acc", "pen"], w=["acc"])
        for it in range(nit):
            P.dve(lambda e: e.tensor_tensor(out=sm[:, 2:3], in0=sm[:, 0:1], in1=sm[:, 1:2], op=ALU.add), r=[("sm", 0), ("sm", 1)], w=[("sm", 2)])
            P.dve(lambda e, L=L: e.tensor_scalar(out=junk[:, 0:L], in0=acc[:, 0:L], scalar1=sm[:, 2:3], scalar2=0.0, op0=ALU.is_ge, op1=ALU.add, accum_out=sm[:, 3:4]),
                  r=["acc", ("sm", 2)], w=["junk", ("sm", 3)])
            P.dve(lambda e: e.tensor_scalar(out=sm[:, 4:5], in0=sm[:, 3:4], scalar1=TOPK - 0.5, scalar2=None, op0=ALU.is_ge), r=[("sm", 3)], w=[("sm", 4)])
            P.dve(lambda e: e.scalar_tensor_tensor(out=sm[:, 0:1], in0=sm[:, 4:5], scalar=sm[:, 1:2], in1=sm[:, 0:1], op0=ALU.mult, op1=ALU.add), r=[("sm", 4), ("sm", 1), ("sm", 0)], w=[("sm", 0)])
            P.dve(lambda e: e.tensor_scalar(out=sm[:, 1:2], in0=sm[:, 1:2], scalar1=0.5, scalar2=None, op0=ALU.mult), r=[("sm", 1)], w=[("sm", 1)])
        P.pool(lambda e, L=L: e.tensor_scalar(out=msk[:, 0:L], in0=acc[:, 0:L], scalar1=sm[:, 0:1], scalar2=None, op0=ALU.is_ge), r=["acc", ("sm", 0)], w=["msk"])
        for b4 in range(L // 512):
            t_ = ti % 2; ti += 1
            for q in range(4):
                kb = b4 * 4 + q
                P.pe(lambda e, kb=kb, q=q, t_=t_: e.transpose(out=PT[t_][:, q, :], in_=msk[:, kb * 128:(kb + 1) * 128], identity=ident[:]), r=["msk", "ident"], w=[(f"pt{t_}", q)])
            m_ = mi % 3; mi += 1
            P.act(lambda e, t_=t_, m_=m_: e.copy(out=mst[m_][:], in_=PT[t_][:]), r=[f"pt{t_}"], w=[f"mst{m_}"])
            P.dma(m_o[j][:, b4 * 4:(b4 + 1) * 4, :], mst[m_][:], r=[f"mst{m_}"])
    P.emit()
    return nc

S = 8192; NQB = 64
SCALE = 128 ** -0.5
NBLK = NQB * (NQB + 1) // 2


def blk_off(qb):
    return qb * (qb + 1) // 2


def build_T(nqb=NQB):
    nc = bass.Bass("TRN2", target_bir_lowering=False)
    P = Prog(nc)
    def din(name, shape, dt=F32):
        return nc.dram_tensor(name, shape, dt, kind="ExternalInput").ap()
    def dout(name, shape, dt=F32):
        return nc.dram_tensor(name, shape, dt, kind="ExternalOutput").ap()
    def sb(name, shape, dt):
        return nc.alloc_sbuf_tensor("s_" + name, shape, dt)

    qT_d = din("qT", [128, S])
    kT_d = din("kT", [128, S])
    v_d = din("v", [128, NQB, 128])
    mask_d = din("mask", [128, NBLK, 128], BF16)
    o_o = dout("o", [128, NQB, 128], BF16)

    stg = sb("stg", [128, 2048], F32)
    qb_ = sb("qb", [128, S], BF16)
    kb_ = sb("kb", [128, S], BF16)
    va = sb("va", [128, NQB, 129], BF16)
    ones = sb("ones", [128, 128], F32)
    sq = [sb(f"sq{i}", [128, 512], F32) for i in range(2)]
    mx = sb("mx", [128, 8], F32)
    mk = [sb(f"mk{i}", [128, NQB, 128], BF16) for i in range(2)]
    E = [sb(f"E{i}", [128, 4, 128], BF16) for i in range(3)]
    PTt = [sb(f"PT{i}", [128, 4, 128], BF16) for i in range(3)]
    ost = [sb(f"ost{i}", [128, 8, 128], BF16) for i in range(2)]
    rc = sb("rc", [128, 4], F32)
    PS = [nc.alloc_psum_tensor(f"ps{i}", [128, 512], F32) for i in range(4)]
    PO = [nc.alloc_psum_tensor(f"po{i}", [128, 512], F32) for i in range(2)]
    PX = nc.alloc_psum_tensor("px", [128, 512], F32)

    P.pool(lambda e: e.memset(ones[:], 1.0), w=["ones"])
    P.pool(lambda e: e.memset(va[:, :, 128:129], 1.0), w=[("va", "ones")])
    P.pool(lambda e: e.memset(mx[:], 0.0), w=["mx"])
    for which, (src, dst, dname) in enumerate(((qT_d, qb_, "qb"), (kT_d, kb_, "kb"))):
        for c in range(4):
            P.dma(stg[:], src[:, c * 2048:(c + 1) * 2048], w=["stg"])
            P.act(lambda e, c=c, dst=dst: e.copy(out=dst[:, c * 2048:(c + 1) * 2048], in_=stg[:]), r=["stg"], w=[(dname, c)])
            for c2 in range(4):
                s_ = (c * 4 + c2) % 2
                P.pool(lambda e, c2=c2, s_=s_: e.tensor_tensor(out=sq[s_][:], in0=stg[:, c2 * 512:(c2 + 1) * 512], in1=stg[:, c2 * 512:(c2 + 1) * 512], op=ALU.mult), r=["stg"], w=[f"sq{s_}"])
                P.pe(lambda e, s_=s_: e.matmul(PX[:, :], lhsT=ones[:], rhs=sq[s_][:], start=True, stop=True), r=["ones", f"sq{s_}"], w=["px"])
                P.dve(lambda e, which=which: e.tensor_reduce(out=mx[:, 2 + which:3 + which], in_=PX[:, :], axis=AX.X, op=ALU.max), r=["px"], w=[("mx", 2 + which)])
                P.dve(lambda e, which=which: e.tensor_tensor(out=mx[:, which:which + 1], in0=mx[:, which:which + 1], in1=mx[:, 2 + which:3 + which], op=ALU.max), r=[("mx", which), ("mx", 2 + which)], w=[("mx", which)])
    P.dve(lambda e: e.tensor_tensor(out=mx[:, 4:5], in0=mx[:, 0:1], in1=mx[:, 1:2], op=ALU.mult), r=[("mx", 0), ("mx", 1)], w=[("mx", 4)])
    P.act(lambda e: e.activation(out=mx[:, 5:6], in_=mx[:, 4:5], func=AF.Sqrt), r=[("mx", 4)], w=[("mx", 5)])
    P.dve(lambda e: e.tensor_scalar(out=mx[:, 6:7], in0=mx[:, 5:6], scalar1=-SCALE, scalar2=None, op0=ALU.mult), r=[("mx", 5)], w=[("mx", 6)])
    for c in range(4):
        P.dma(stg[:].rearrange("p (b d) -> p b d", d=128), v_d[:, c * 16:(c + 1) * 16, :], w=["stg"])
        P.act(lambda e, c=c: e.copy(out=va[:, c * 16:(c + 1) * 16, 0:128], in_=stg[:].rearrange("p (b d) -> p b d", d=128)), r=["stg"], w=[("va", c)])

    ei = 0; pi = 0
    for qb in range(nqb):
        mb = qb % 2
        nb = qb + 1
        P.dma(mk[mb][:, 0:nb, :], mask_d[:, blk_off(qb):blk_off(qb) + nb, :], w=[f"mk{mb}"])
        po = qb % 2
        for k0 in range(0, nb, 4):
            n = min(4, nb - k0)
            pidx = pi % 4; pi += 1
            for q in range(n):
                kbi = k0 + q
                P.pe(lambda e, q=q, kbi=kbi, pidx=pidx, qb=qb: e.matmul(PS[pidx][:, q * 128:(q + 1) * 128], lhsT=kb_[:, kbi * 128:(kbi + 1) * 128], rhs=qb_[:, qb * 128:(qb + 1) * 128], start=True, stop=True),
                     r=[("kb", kbi // 16), ("qb", qb // 16)], w=[(f"ps{pidx}", q)])
            e_ = ei % 3; ei += 1
            P.act(lambda e, e_=e_, pidx=pidx, n=n: e.activation(out=E[e_][:, 0:n, :], in_=PS[pidx][:, 0:n * 128].rearrange("p (b t) -> p b t", t=128), func=AF.Exp, bias=mx[:, 6:7], scale=SCALE),
                  r=[f"ps{pidx}", ("mx", 6)], w=[f"E{e_}"])
            eng = P.dve if (ei % 2 == 0) else P.pool
            eng(lambda e, e_=e_, n=n, k0=k0, mb=mb: e.tensor_tensor(out=PTt[e_][:, 0:n, :], in0=E[e_][:, 0:n, :], in1=mk[mb][:, k0:k0 + n, :], op=ALU.mult),
                r=[f"E{e_}", f"mk{mb}"], w=[f"PT{e_}"])
            for q in range(n):
                kbi = k0 + q
                P.pe(lambda e, q=q, kbi=kbi, e_=e_, po=po, nb=nb: e.matmul(PO[po][:, 0:129], lhsT=PTt[e_][:, q, :], rhs=va[:, kbi, :], start=(kbi == 0), stop=(kbi == nb - 1)),
                     r=[f"PT{e_}", ("va", kbi // 16), ("va", "ones")], w=[f"po{po}"])
        ob = (qb // 8) % 2
        P.dve(lambda e, po=po, qb=qb: e.reciprocal(out=rc[:, qb % 4:qb % 4 + 1], in_=PO[po][:, 128:129]), r=[f"po{po}"], w=[("rc", qb % 4)])
        P.dve(lambda e, po=po, qb=qb, ob=ob: e.tensor_scalar(out=ost[ob][:, qb % 8, :], in0=PO[po][:, 0:128], scalar1=rc[:, qb % 4:qb % 4 + 1], scalar2=None, op0=ALU.mult),
              r=[f"po{po}", ("rc", qb % 4)], w=[(f"ost{ob}", qb % 8)])
        if qb % 8 == 7 or qb == nqb - 1:
            q0 = (qb // 8) * 8
            cnt = qb - q0 + 1
            P.dma(o_o[:, q0:q0 + cnt, :], ost[ob][:, 0:cnt, :], r=[f"ost{ob}"])
    P.emit()
    return nc

S = 8192; C = 128; GS = 4
RMS_EPS = 1e-6
QSCALE = 128 ** -0.5


def build_G(nch=64):
    T = nch * C
    NG = nch // GS
    nc = bass.Bass("TRN2", target_bir_lowering=False)
    P = Prog(nc)
    def din(name, shape, dt=F32):
        return nc.dram_tensor(name, shape, dt, kind="ExternalInput").ap()
    def dout(name, shape, dt=F32):
        return nc.dram_tensor(name, shape, dt, kind="ExternalOutput").ap()
    def sb(name, shape, dt):
        return nc.alloc_sbuf_tensor("s_" + name, shape, dt)

    qkv_d = din("qkvT", [3, 128, T])
    cw_d = din("cw", [128, 3, 4])
    z_d = din("z", [128, nch, 128])
    ab_d = din("ab", [2, nch, 128])
    hp_d = din("hp", [128, 2])
    nw_d = din("normw", [1, 128])
    o_o = dout("o", [128, nch, 128], BF16)
    gscr = nc.dram_tensor("gscr", [2, nch * 128], F32).ap()

    X = sb("X", [128, T + 3], F32)
    U = sb("U", [128, T], F32)
    kT = sb("kT", [128, T], BF16)
    qT = sb("qT", [128, T], BF16)
    qdT = sb("qdT", [128, T], BF16)
    vT = sb("vT", [128, T], BF16)
    cw = sb("cw", [128, 3, 4], F32)
    hp = sb("hp", [128, 2], F32)
    nwr = sb("nwr", [128, 128], F32)
    ones = sb("ones", [128, 128], F32)
    identf = sb("identf", [128, 128], F32)
    ident = sb("ident", [128, 128], BF16)
    utri = sb("utri", [128, 128], F32)
    dmask = sb("dmask", [128, 128], F32)
    nstrict = sb("nstrict", [128, 128], F32)
    epsb = sb("epsb", [128, 2], F32)
    tmp = [sb(f"tmp{i}", [128, 512], F32) for i in range(4)]
    a_sb = sb("a_sb", [64, 128], F32)
    b_sb = sb("b_sb", [64, 128], F32)
    gcc = sb("gcc", [64, 128], F32)
    cols = sb("cols", [128, 8, 64], F32)
    nea = sb("nea", [128, 2], F32)
    T1 = sb("T1", [128, 2048], F32)
    def gb(name, dt):
        return [sb(f"{name}{i}", [128, GS, 128], dt) for i in range(2)]
    bek_g = gb("bek", BF16); kdec_g = gb("kdec", BF16); bv_g = gb("bv", BF16)
    attn_g = gb("attn", BF16); u_g = gb("u", F32); wT_g = gb("wT", BF16)
    dc_g = gb("dc", F32); t_g = gb("tt", F32)
    Qb = [sb(f"Q{i}", [128, GS, 128], BF16) for i in range(2)]
    Rb = [sb(f"R{i}", [128, GS, 128], BF16) for i in range(2)]
    Yb = sb("Y", [128, GS, 128], BF16)
    zg = gb("zg", F32); gw = gb("gw", F32); og = gb("og", BF16)
    Sf = sb("Sf", [128, 128], F32)
    Sb = sb("Sb", [128, 128], BF16)
    vnew = sb("vnew", [128, 128], BF16)
    junk = sb("junk", [128, 128], F32)
    ssq = sb("ssq", [128, 2], F32)
    PS = [nc.alloc_psum_tensor(f"ps{i}", [128, 512], F32) for i in range(7)]
    PTb = nc.alloc_psum_tensor("ptb", [128, GS, 128], BF16)
    PQ, PR, PY, PA, PB, PSC_A, PSC_B = range(7)

    P.pool(lambda e: e.memset(ones[:], 1.0), w=["ones"])
    P.pool(lambda e: e.memset(epsb[:], RMS_EPS), w=["epsb"])
    P.pool(lambda e: e.memset(identf[:], 0.0), w=["identf"])
    P.pool(lambda e: e.affine_select(out=identf[:], in_=identf[:], pattern=[[-1, 128]], compare_op=ALU.not_equal, fill=1.0, base=0, channel_multiplier=1), r=["identf"], w=["identf"])
    P.pool(lambda e: e.tensor_copy(out=ident[:], in_=identf[:]), r=["identf"], w=["ident"])
    P.pool(lambda e: e.memset(utri[:], 1.0), w=["utri"])
    P.pool(lambda e: e.affine_select(out=utri[:], in_=utri[:], pattern=[[1, 128]], compare_op=ALU.is_ge, fill=0.0, base=0, channel_multiplier=-1), r=["utri"], w=["utri"])
    P.pool(lambda e: e.memset(dmask[:], 0.0), w=["dmask"])
    P.pool(lambda e: e.affine_select(out=dmask[:], in_=dmask[:], pattern=[[1, 128]], compare_op=ALU.is_ge, fill=-30000.0, base=0, channel_multiplier=-1), r=["dmask"], w=["dmask"])
    P.pool(lambda e: e.memset(nstrict[:], -1.0), w=["nstrict"])
    P.pool(lambda e: e.affine_select(out=nstrict[:], in_=nstrict[:], pattern=[[1, 128]], compare_op=ALU.is_gt, fill=0.0, base=0, channel_multiplier=-1), r=["nstrict"], w=["nstrict"])
    P.pool(lambda e: e.memset(X[:, 0:3], 0.0), w=[("X", "pad")])
    P.pool(lambda e: e.memset(Sf[:], 0.0), w=["Sf"])
    P.pool(lambda e: e.memset(Sb[:], 0.0), w=["Sb"])
    P.dma(cw[:], cw_d, w=["cw"])
    P.dma(hp[:], hp_d, w=["hp"])
    P.dma(nwr[:], nw_d.partition_broadcast(128), w=["nwr"])

    PW = min(2048, T)
    NP = T // PW
    ti_ = 0
    for ti, dst in ((0, qT), (1, kT), (2, vT)):
        dname = ("qT", "kT", "vT")[ti]
        for c in range(NP):
            P.dma(X[:, 3 + c * PW:3 + (c + 1) * PW], qkv_d[ti, :, c * PW:(c + 1) * PW], w=[("X", c)])
        for c in range(NP):
            lo = c * PW
            rk = [("X", c), ("X", "pad")] + ([("X", c - 1)] if c > 0 else [])
            P.dve(lambda e, lo=lo, ti=ti: e.tensor_scalar(out=U[:, lo:lo + PW], in0=X[:, lo + 3:lo + 3 + PW], scalar1=cw[:, ti, 3:4], scalar2=None, op0=ALU.mult), r=rk + ["cw"], w=[("U", c)])
            for jj in (2, 1, 0):
                P.dve(lambda e, lo=lo, ti=ti, jj=jj: e.scalar_tensor_tensor(out=U[:, lo:lo + PW], in0=X[:, lo + jj:lo + jj + PW], scalar=cw[:, ti, jj:jj + 1], in1=U[:, lo:lo + PW], op0=ALU.mult, op1=ALU.add),
                      r=rk + ["cw", ("U", c)], w=[("U", c)])
            P.act(lambda e, lo=lo: e.activation(out=U[:, lo:lo + PW], in_=U[:, lo:lo + PW], func=AF.Silu), r=[("U", c)], w=[("U", c)])
            if ti == 2:
                P.act(lambda e, lo=lo: e.copy(out=vT[:, lo:lo + PW], in_=U[:, lo:lo + PW]), r=[("U", c)], w=[("vT", c)])
                continue
            for c2 in range(PW // 512):
                l2 = lo + c2 * 512
                t_ = ti_ % 4; ti_ += 1
                P.pool(lambda e, l2=l2, t_=t_: e.tensor_tensor(out=tmp[t_][:], in0=U[:, l2:l2 + 512], in1=U[:, l2:l2 + 512], op=ALU.mult), r=[("U", c)], w=[f"tmp{t_}"])
                P.pe(lambda e, t_=t_: e.matmul(PS[PY][:, :], lhsT=ones[:], rhs=tmp[t_][:], start=True, stop=True), r=["ones", f"tmp{t_}"], w=[f"ps{PY}"])
                P.act(lambda e, t_=t_: e.activation(out=tmp[t_][:], in_=PS[PY][:, :], func=AF.Sqrt, bias=epsb[:, 0:1], scale=1.0), r=[f"ps{PY}", "epsb"], w=[f"tmp{t_}"])
                P.dve(lambda e, t_=t_: e.reciprocal(out=tmp[t_][:], in_=tmp[t_][:]), r=[f"tmp{t_}"], w=[f"tmp{t_}"])
                sc = QSCALE if ti == 0 else 1.0
                P.dve(lambda e, l2=l2, t_=t_, dst=dst, sc=sc: e.scalar_tensor_tensor(out=dst[:, l2:l2 + 512], in0=U[:, l2:l2 + 512], scalar=sc, in1=tmp[t_][:], op0=ALU.mult, op1=ALU.mult),
                      r=[("U", c), f"tmp{t_}"], w=[(dname, c)])

    P.dma(a_sb[0:nch, :], ab_d[0], w=["a_sb"])
    P.dma(b_sb[0:nch, :], ab_d[1], w=["b_sb"])
    P.act(lambda e: e.activation(out=a_sb[0:nch, :], in_=a_sb[0:nch, :], func=AF.Exp, bias=hp[0:nch, 1:2], scale=1.0), r=["a_sb", "hp"], w=["a_sb"])
    P.act(lambda e: e.activation(out=a_sb[0:nch, :], in_=a_sb[0:nch, :], func=AF.Ln, bias=ones[0:nch, 0:1], scale=1.0), r=["a_sb", "ones"], w=["a_sb"])
    P.act(lambda e: e.activation(out=nea[:, 0:1], in_=hp[:, 0:1], func=AF.Exp), r=["hp"], w=["nea"])
    P.dve(lambda e: e.tensor_scalar(out=a_sb[0:nch, :], in0=a_sb[0:nch, :], scalar1=nea[0:nch, 0:1], scalar2=-1.0, op0=ALU.mult, op1=ALU.mult), r=["a_sb", "nea"], w=["a_sb"])
    P.act(lambda e: e.activation(out=b_sb[0:nch, :], in_=b_sb[0:nch, :], func=AF.Sigmoid), r=["b_sb"], w=["b_sb"])
    P.dma(gscr[1].rearrange("(n i) -> n i", i=128), b_sb[0:nch, :], r=["b_sb"], w=[("gscr", 1)])
    P.pe(lambda e: e.transpose(out=PS[PQ][:, 0:nch], in_=a_sb[0:nch, :], identity=identf[0:nch, 0:nch]), r=["a_sb", "identf"], w=[f"ps{PQ}"])
    P.act(lambda e: e.copy(out=cols[:, 0, 0:nch], in_=PS[PQ][:, 0:nch]), r=[f"ps{PQ}"], w=[("cols", 0)])
    P.pe(lambda e: e.transpose(out=PS[PR][:, 0:nch], in_=b_sb[0:nch, :], identity=identf[0:nch, 0:nch]), r=["b_sb", "identf"], w=[f"ps{PR}"])
    P.act(lambda e: e.copy(out=cols[:, 1, 0:nch], in_=PS[PR][:, 0:nch]), r=[f"ps{PR}"], w=[("cols", 1)])
    P.pe(lambda e: e.matmul(PS[PA][:, 0:nch], lhsT=utri[:], rhs=cols[:, 0, 0:nch], start=True, stop=True), r=["utri", ("cols", 0)], w=[f"ps{PA}"])
    P.act(lambda e: e.copy(out=cols[:, 2, 0:nch], in_=PS[PA][:, 0:nch]), r=[f"ps{PA}"], w=[("cols", 2)])
    P.pe(lambda e: e.matmul(PS[PB][:, 0:nch], lhsT=ones[:], rhs=cols[:, 0, 0:nch], start=True, stop=True), r=["ones", ("cols", 0)], w=[f"ps{PB}"])
    P.act(lambda e: e.copy(out=cols[:, 3, 0:nch], in_=PS[PB][:, 0:nch]), r=[f"ps{PB}"], w=[("cols", 3)])
    P.act(lambda e: e.activation(out=cols[:, 4, 0:nch], in_=cols[:, 3, 0:nch], func=AF.Exp), r=[("cols", 3)], w=[("cols", 4)])
    P.act(lambda e: e.activation(out=cols[:, 5, 0:nch], in_=cols[:, 2, 0:nch], func=AF.Exp), r=[("cols", 2)], w=[("cols", 5)])
    P.dve(lambda e: e.tensor_tensor(out=cols[:, 5, 0:nch], in0=cols[:, 5, 0:nch], in1=cols[:, 1, 0:nch], op=ALU.mult), r=[("cols", 5), ("cols", 1)], w=[("cols", 5)])
    P.dve(lambda e: e.tensor_tensor(out=cols[:, 6, 0:nch], in0=cols[:, 3, 0:nch], in1=cols[:, 2, 0:nch], op=ALU.subtract), r=[("cols", 3), ("cols", 2)], w=[("cols", 6)])
    P.act(lambda e: e.activation(out=cols[:, 6, 0:nch], in_=cols[:, 6, 0:nch], func=AF.Exp), r=[("cols", 6)], w=[("cols", 6)])
    P.dve(lambda e: e.tensor_scalar(out=cols[:, 7, 0:nch], in0=cols[:, 2, 0:nch], scalar1=-1.0, scalar2=None, op0=ALU.mult), r=[("cols", 2)], w=[("cols", 7)])
    P.pe(lambda e: e.transpose(out=PS[PY][0:nch, 0:128], in_=cols[:, 2, 0:nch], identity=identf[:]), r=[("cols", 2), "identf"], w=[f"ps{PY}"])
    P.act(lambda e: e.copy(out=gcc[0:nch, :], in_=PS[PY][0:nch, 0:128]), r=[f"ps{PY}"], w=["gcc"])
    P.dma(gscr[0].rearrange("(n i) -> n i", i=128), gcc[0:nch, :], r=["gcc"], w=[("gscr", 0)])
    GR = X[:, 3:3 + T]
    BR = U[:, 0:T]
    for c in range(NP):
        P.dma(X[:, 3 + c * PW:3 + (c + 1) * PW], gscr[0:1, c * PW:(c + 1) * PW].partition_broadcast(128), r=[("gscr", 0)], w=[("X", c)])
        P.dma(U[:, c * PW:(c + 1) * PW], gscr[1:2, c * PW:(c + 1) * PW].partition_broadcast(128), r=[("gscr", 1)], w=[("U", c)])
    npc = PW // 128
    for c in range(NP):
        lo = c * PW
        P.act(lambda e, lo=lo: e.activation(out=T1[:, 0:PW], in_=X[:, 3 + lo:3 + lo + PW], func=AF.Exp), r=[("X", c)], w=["T1"])
        P.dve(lambda e, lo=lo: e.tensor_tensor(out=qdT[:, lo:lo + PW], in0=qT[:, lo:lo + PW], in1=T1[:, 0:PW], op=ALU.mult), r=[("qT", c), "T1"], w=[("qdT", c)])
        P.pool(lambda e, lo=lo: e.tensor_tensor(out=X[:, 3 + lo:3 + lo + PW].rearrange("p (n i) -> p n i", i=128), in0=X[:, 3 + lo:3 + lo + PW].rearrange("p (n i) -> p n i", i=128),
                                                in1=dmask[:].unsqueeze(1).to_broadcast([128, npc, 128]), op=ALU.add), r=[("X", c), "dmask"], w=[("X", c)])
        P.pool(lambda e, lo=lo: e.tensor_tensor(out=U[:, lo:lo + PW].rearrange("p (n i) -> p n i", i=128), in0=U[:, lo:lo + PW].rearrange("p (n i) -> p n i", i=128),
                                                in1=nstrict[:].unsqueeze(1).to_broadcast([128, npc, 128]), op=ALU.mult), r=[("U", c), "nstrict"], w=[("U", c)])

    def colb(ci, n0):
        return cols[:, ci, n0:n0 + GS].unsqueeze(2).to_broadcast([128, GS, 128])

    def precompute(g):
        n0 = g * GS
        pb = g % 2
        pc = (n0 * 128) // PW
        flat = lambda ap: ap.rearrange("p g i -> p (g i)")
        for q in range(GS):
            n = n0 + q
            P.pe(lambda e, q=q, n=n: e.transpose(out=PTb[:, q, :], in_=kT[:, n * 128:(n + 1) * 128], identity=ident[:]), r=[("kT", pc), "ident"], w=[("ptb", q)])
        P.dve(lambda e: e.tensor_tensor(out=bek_g[pb][:], in0=PTb[:], in1=colb(5, n0), op=ALU.mult), r=["ptb", ("cols", 5)], w=[f"bek{pb}"])
        P.dve(lambda e: e.tensor_tensor(out=kdec_g[pb][:], in0=PTb[:], in1=colb(6, n0), op=ALU.mult), r=["ptb", ("cols", 6)], w=[f"kdec{pb}"])
        for q in range(GS):
            n = n0 + q
            P.pe(lambda e, q=q, n=n: e.transpose(out=PTb[:, q, :], in_=vT[:, n * 128:(n + 1) * 128], identity=ident[:]), r=[("vT", pc), "ident"], w=[("ptb", q)])
        P.dve(lambda e: e.tensor_tensor(out=bv_g[pb][:], in0=PTb[:], in1=colb(1, n0), op=ALU.mult), r=["ptb", ("cols", 1)], w=[f"bv{pb}"])
        for q in range(GS):
            n = n0 + q
            P.pe(lambda e, q=q, n=n: e.matmul(PS[PA][:, q * 128:(q + 1) * 128], lhsT=kT[:, n * 128:(n + 1) * 128], rhs=kT[:, n * 128:(n + 1) * 128], start=True, stop=True), r=[("kT", pc)], w=[(f"ps{PA}", q)])
            P.pe(lambda e, q=q, n=n: e.matmul(PS[PB][:, q * 128:(q + 1) * 128], lhsT=kT[:, n * 128:(n + 1) * 128], rhs=qT[:, n * 128:(n + 1) * 128], start=True, stop=True), r=[("kT", pc), ("qT", pc)], w=[(f"ps{PB}", q)])
            P.act(lambda e, q=q, n=n: e.activation(out=dc_g[pb][:, q, :], in_=X[:, 3 + n * 128:3 + (n + 1) * 128], func=AF.Exp, bias=cols[:, 7, n:n + 1], scale=1.0), r=[("X", pc), ("cols", 7)], w=[(f"dc{pb}", q)])
        P.dve(lambda e: e.tensor_tensor(out=flat(attn_g[pb][:]), in0=PS[PB][:, :], in1=flat(dc_g[pb][:]), op=ALU.mult), r=[f"ps{PB}", f"dc{pb}"], w=[f"attn{pb}"])
        P.dve(lambda e: e.tensor_tensor(out=flat(t_g[pb][:]), in0=PS[PA][:, :], in1=flat(dc_g[pb][:]), op=ALU.mult), r=[f"ps{PA}", f"dc{pb}"], w=[f"tt{pb}"])
        P.pool(lambda e: e.tensor_tensor(out=flat(Qb[0][:]), in0=flat(t_g[pb][:]), in1=U[:, n0 * 128:(n0 + GS) * 128], op=ALU.mult), r=[f"tt{pb}", ("U", pc)], w=["Q0"])
        for q in range(GS):
            P.pe(lambda e, q=q: e.transpose(out=PTb[:, q, :], in_=Qb[0][:, q, :], identity=ident[:]), r=["Q0", "ident"], w=[("ptb", q)])
        P.act(lambda e: e.copy(out=Rb[0][:], in_=PTb[:]), r=["ptb"], w=["R0"])
        P.pool(lambda e: e.tensor_tensor(out=Yb[:], in0=Qb[0][:], in1=ident[:].unsqueeze(1).to_broadcast([128, GS, 128]), op=ALU.add), r=["Q0", "ident"], w=["Y"])
        cur = 0
        for lvl in range(1, 7):
            nx = 1 - cur
            if lvl <= 5:
                for q in range(GS):
                    P.pe(lambda e, q=q, cur=cur: e.matmul(PS[PQ][:, q * 128:(q + 1) * 128], lhsT=Rb[cur][:, q, :], rhs=Qb[cur][:, q, :], start=True, stop=True), r=[f"R{cur}", f"Q{cur}"], w=[(f"ps{PQ}", q)])
            for q in range(GS):
                P.pe(lambda e, q=q, cur=cur: e.matmul(PS[PR][:, q * 128:(q + 1) * 128], lhsT=Qb[cur][:, q, :], rhs=Rb[cur][:, q, :], start=True, stop=True), r=[f"R{cur}", f"Q{cur}"], w=[(f"ps{PR}", q)])
            if lvl <= 5:
                P.act(lambda e, nx=nx: e.copy(out=flat(Qb[nx][:]), in_=PS[PQ][:, :]), r=[f"ps{PQ}"], w=[f"Q{nx}"])
            P.act(lambda e, nx=nx: e.copy(out=flat(Rb[nx][:]), in_=PS[PR][:, :]), r=[f"ps{PR}"], w=[f"R{nx}"])
            for q in range(GS):
                P.pe(lambda e, q=q, nx=nx: e.matmul(PS[PY][:, q * 128:(q + 1) * 128], lhsT=Rb[nx][:, q, :], rhs=Yb[:, q, :], start=True, stop=True), r=[f"R{nx}", "Y"], w=[(f"ps{PY}", q)])
            P.dve(lambda e: e.tensor_tensor(out=flat(Yb[:]), in0=PS[PY][:, :], in1=flat(Yb[:]), op=ALU.add), r=[f"ps{PY}", "Y"], w=["Y"])
            cur = nx
        for q in range(GS):
            P.pe(lambda e, q=q: e.matmul(PS[PQ][:, q * 128:(q + 1) * 128], lhsT=Yb[:, q, :], rhs=bv_g[pb][:, q, :], start=True, stop=True), r=["Y", f"bv{pb}"], w=[(f"ps{PQ}", q)])
            P.pe(lambda e, q=q: e.matmul(PS[PR][:, q * 128:(q + 1) * 128], lhsT=bek_g[pb][:, q, :], rhs=Yb[:, q, :], start=True, stop=True), r=["Y", f"bek{pb}"], w=[(f"ps{PR}", q)])
        P.act(lambda e: e.copy(out=flat(u_g[pb][:]), in_=PS[PQ][:, :]), r=[f"ps{PQ}"], w=[f"u{pb}"])
        P.act(lambda e: e.copy(out=flat(wT_g[pb][:]), in_=PS[PR][:, :]), r=[f"ps{PR}"], w=[f"wT{pb}"])
        P.dma(zg[pb][:], z_d[:, n0:n0 + GS, :], w=[f"zg{pb}"])
        P.act(lambda e: e.activation(out=zg[pb][:], in_=zg[pb][:], func=AF.Silu), r=[f"zg{pb}"], w=[f"zg{pb}"])
        P.pool(lambda e: e.tensor_tensor(out=gw[pb][:], in0=zg[pb][:], in1=nwr[:].unsqueeze(1).to_broadcast([128, GS, 128]), op=ALU.mult), r=[f"zg{pb}", "nwr"], w=[f"gw{pb}"])

    def scan(g):
        n0 = g * GS
        pb = g % 2
        pc = (n0 * 128) // PW
        for q in range(GS):
            n = n0 + q
            P.pe(lambda e, q=q: e.matmul(PS[PSC_A][:, 0:128], lhsT=wT_g[pb][:, q, :], rhs=Sb[:], start=True, stop=True), r=[f"wT{pb}", "Sb"], w=[(f"ps{PSC_A}", 0)])
            P.pe(lambda e, n=n: e.matmul(PS[PSC_B][:, 0:128], lhsT=qdT[:, n * 128:(n + 1) * 128], rhs=Sb[:], start=True, stop=False), r=[("qdT", pc), "Sb"], w=[f"ps{PSC_B}"])
            P.dve(lambda e, q=q: e.tensor_tensor(out=vnew[:], in0=u_g[pb][:, q, :], in1=PS[PSC_A][:, 0:128], op=ALU.subtract), r=[f"u{pb}", (f"ps{PSC_A}", 0)], w=["vnew"])
            P.pe(lambda e, q=q: e.matmul(PS[PSC_B][:, 0:128], lhsT=attn_g[pb][:, q, :], rhs=vnew[:], start=False, stop=True), r=[f"attn{pb}", "vnew"], w=[f"ps{PSC_B}"])
            P.pe(lambda e, q=q: e.matmul(PS[PSC_A][:, 128:256], lhsT=kdec_g[pb][:, q, :], rhs=vnew[:], start=True, stop=True), r=[f"kdec{pb}", "vnew"], w=[(f"ps{PSC_A}", 1)])
            P.dve(lambda e, n=n: e.scalar_tensor_tensor(out=Sf[:], in0=Sf[:], scalar=cols[:, 4, n:n + 1], in1=PS[PSC_A][:, 128:256], op0=ALU.mult, op1=ALU.add), r=["Sf", ("cols", 4), (f"ps{PSC_A}", 1)], w=["Sf"])
            P.act(lambda e: e.copy(out=Sb[:], in_=Sf[:]), r=["Sf"], w=["Sb"])
            P.act(lambda e: e.activation(out=junk[:], in_=PS[PSC_B][:, 0:128], func=AF.Square, accum_out=ssq[:, 0:1]), r=[f"ps{PSC_B}"], w=["junk", ("ssq", 0)])
            P.act(lambda e: e.activation(out=ssq[:, 1:2], in_=ssq[:, 0:1], func=AF.Sqrt, bias=epsb[:, 0:1], scale=1.0 / 128), r=[("ssq", 0), "epsb"], w=[("ssq", 1)])
            P.dve(lambda e: e.reciprocal(out=ssq[:, 1:2], in_=ssq[:, 1:2]), r=[("ssq", 1)], w=[("ssq", 1)])
            P.dve(lambda e, q=q: e.scalar_tensor_tensor(out=og[pb][:, q, :], in0=PS[PSC_B][:, 0:128], scalar=ssq[:, 1:2], in1=gw[pb][:, q, :], op0=ALU.mult, op1=ALU.mult),
                  r=[f"ps{PSC_B}", ("ssq", 1), f"gw{pb}"], w=[(f"og{pb}", q)])
        P.dma(o_o[:, n0:n0 + GS, :], og[pb][:], r=[f"og{pb}"])

    precompute(0)
    for g in range(NG):
        if g + 1 < NG:
            precompute(g + 1)
        scan(g)
    P.emit()
    return nc
import ml_dtypes

W_NAMES = ("w_in", "ffn_up", "ffn_down", "w_out", "w_ukv")


def build_W(cols):
    nc = bass.Bass("TRN2", target_bir_lowering=False)
    P = Prog(nc)
    CH = 4096
    stg = [nc.alloc_sbuf_tensor(f"s_stg{i}", [128, CH], F32) for i in range(3)]
    ob = [nc.alloc_sbuf_tensor(f"s_ob{i}", [128, CH], BF16) for i in range(3)]
    k = 0
    for i, n in enumerate(cols):
        src = nc.dram_tensor(f"w{i}", [128, n], F32, kind="ExternalInput").ap()
        dst = nc.dram_tensor(f"o{i}", [128, n], BF16, kind="ExternalOutput").ap()
        for c0 in range(0, n, CH):
            w = min(CH, n - c0)
            b = k % 3
            P.dma(stg[b][:, 0:w], src[:, c0:c0 + w], w=[f"stg{b}"])
            if k % 3 == 0:
                P.act(lambda e, b=b, w=w: e.copy(out=ob[b][:, 0:w], in_=stg[b][:, 0:w]), r=[f"stg{b}"], w=[f"ob{b}"])
            elif k % 3 == 1:
                P.dve(lambda e, b=b, w=w: e.tensor_copy(out=ob[b][:, 0:w], in_=stg[b][:, 0:w]), r=[f"stg{b}"], w=[f"ob{b}"])
            else:
                P.pool(lambda e, b=b, w=w: e.tensor_copy(out=ob[b][:, 0:w], in_=stg[b][:, 0:w]), r=[f"stg{b}"], w=[f"ob{b}"])
            P.dma(dst[:, c0:c0 + w], ob[b][:, 0:w], r=[f"ob{b}"])
            k += 1
    P.emit()
    return nc


def _rope_tab(pos, dim):
    inv = (10000.0 ** (-np.arange(0, dim, 2, dtype=np.float32) / dim)).astype(np.float32)
    ang = pos[:, None].astype(np.float32) * inv[None, :]
    ang = np.concatenate([ang, ang], -1)
    return np.cos(ang).astype(np.float32), np.sin(ang).astype(np.float32)


def _rmat(dim, reps):
    R = np.zeros((128, 128), np.float32)
    h = dim // 2
    for b in range(reps):
        for m in range(dim):
            if m < h:
                R[b * dim + m + h, b * dim + m] = -1.0
            else:
                R[b * dim + m - h, b * dim + m] = 1.0
    return R


def _fm(v):
    return np.ascontiguousarray(np.asarray(v, np.float32).reshape(-1, 128).T)


_PROGS = {}


def _prog(key, fn):
    if key not in _PROGS:
        _PROGS[key] = fn()
    return _PROGS[key]


def _run(nc, in_maps):
    res = run_bass_kernel_spmd(nc, in_maps, core_ids=list(range(NCORE)))
    return res.results


def kernel(**inp):
    bf = ml_dtypes.bfloat16
    inp = {k: np.asarray(v) for k, v in inp.items()}
    x = inp["x"][0]
    L = DEPTH
    slices = []
    for c in range(NCORE):
        d = {}
        for i, nm in enumerate(W_NAMES):
            W = inp[nm]
            Rc = W.shape[1] // NCORE
            d[f"w{i}"] = np.ascontiguousarray(W[:, c * Rc:(c + 1) * Rc, :]).reshape(128, -1)
        slices.append(d)
    cols = [slices[0][f"w{i}"].shape[1] for i in range(len(W_NAMES))]
    ncw = _prog("W", lambda: build_W(cols))
    resw = _run(ncw, slices)
    wbf = {}
    for i, nm in enumerate(W_NAMES):
        W = inp[nm]
        Rc = W.shape[1] // NCORE
        parts = [np.asarray(resw[c][f"o{i}"]).reshape(L, Rc, W.shape[2]) for c in range(NCORE)]
        wbf[nm] = np.concatenate(parts, axis=1)
    del slices, resw

    rmat = np.stack([_rmat(128, 1), _rmat(64, 2)])
    ropes = {}
    def rope_for(t0):
        if t0 not in ropes:
            pos = t0 + np.arange(TS)
            c128, s128 = _rope_tab(pos, 128)
            c64, s64 = _rope_tab(pos, 64)
            ropes[t0] = np.stack([c128.T, s128.T, np.concatenate([c64.T, c64.T], 0), np.concatenate([s64.T, s64.T], 0)]).astype(np.float32)
        return ropes[t0]

    def run_A(layer_post, layer_proj, xT_full, mixT_full):
        do_post = layer_post is not None
        do_proj = layer_proj is not None
        nc = _prog(("A", do_post, do_proj), lambda: build_A(do_post, do_proj))
        maps = []
        for c in range(NCORE):
            d = {}
            t0s = [c * NSH * TS + sh * TS for sh in range(NSH)]
            if do_post:
                xs = []; ms = []
                for t0 in t0s:
                    if t0 == 0:
                        xs.append(np.concatenate([np.zeros((D, HALO), np.float32), xT_full[:, 0:TS]], 1))
                        ms.append(np.concatenate([np.zeros((D, HALO), bf), mixT_full[:, 0:TS]], 1))
                    else:
                        xs.append(xT_full[:, t0 - HALO:t0 + TS])
                        ms.append(mixT_full[:, t0 - HALO:t0 + TS])
                d["xT"] = np.ascontiguousarray(np.stack(xs))
                d["mixT"] = np.ascontiguousarray(np.stack(ms))
                i = layer_post
                d["w_out"] = wbf["w_out"][i]; d["ffn_up"] = wbf["ffn_up"][i]; d["ffn_down"] = wbf["ffn_down"][i]
                d["lnp"] = np.ascontiguousarray(np.stack([_fm(inp["ln1_g"][i]), _fm(inp["ln1_b"][i]), _fm(inp["ln2_g"][i]), _fm(inp["ln2_b"][i])], 1))
                cwa = np.concatenate([inp["ffn_conv_w"][i], inp["ffn_conv_b"][i][None]], 0).astype(np.float32)
                d["convw"] = np.ascontiguousarray(cwa.T.reshape(88, 128, 4).transpose(1, 0, 2))
                hf = np.ones((128, NSH), np.float32)
                if c == 0:
                    hf[:, 0] = 0.0
                d["haloflag"] = hf
            else:
                d["xT"] = np.ascontiguousarray(np.stack([xT_full[:, t0:t0 + TS] for t0 in t0s]))
            if do_proj:
                i = layer_proj
                d["w_in"] = wbf["w_in"][i]; d["w_ukv"] = wbf["w_ukv"][i]
                d["kvnw"] = _fm(inp["kv_norm_w"][i])
                d["idxgb"] = np.ascontiguousarray(np.stack([inp["idx_k_norm_g"][i], inp["idx_k_norm_b"][i]], 1).astype(np.float32))
                d["rope"] = np.ascontiguousarray(np.stack([rope_for(t0) for t0 in t0s]))
                d["rmat"] = rmat
            maps.append(d)
        res = _run(nc, maps)
        x2T = None; pT = None
        if do_post:
            x2T = np.concatenate([np.asarray(res[c]["x2T"][sh]) for c in range(NCORE) for sh in range(NSH)], axis=1)
        if do_proj:
            pT = np.concatenate([np.asarray(res[c]["pT"][sh]) for c in range(NCORE) for sh in range(NSH)], axis=1)
        return x2T, pT

    def run_I(pT):
        nc = _prog("I", lambda: build_I())
        kiT2 = np.ascontiguousarray(np.concatenate([pT[R_IK:R_IK + 64], pT[R_IK:R_IK + 64]], 0))
        maps = []
        for c in range(NCORE):
            qbs = slot_qbs(c)
            d = {"kiT2": kiT2}
            d["iqT"] = np.ascontiguousarray(np.stack([pT[R_IQ:R_IQ + 1024, q * 128:(q + 1) * 128] for q in qbs]))
            d["iw"] = np.ascontiguousarray(np.stack([pT[R_IW:R_IW + 16, q * 128:(q + 1) * 128].T for q in qbs], 1))
            d["qpos"] = np.ascontiguousarray(np.stack([np.arange(q * 128, (q + 1) * 128) for q in qbs], 1).astype(np.float32))
            maps.append(d)
        res = _run(nc, maps)
        mask = np.zeros((128, NBLK, 128), bf)
        for c in range(NCORE):
            for j, q in enumerate(slot_qbs(c)):
                mask[:, blk_off(q):blk_off(q) + q + 1, :] = np.asarray(res[c][f"m{j}"])[:, 0:q + 1, :]
        return mask

    def tokmajor(a):
        return np.ascontiguousarray(a.T.reshape(64, 128, 128).transpose(1, 0, 2))

    def run_T(pT, mask):
        nc = _prog("T", lambda: build_T())
        maps = []
        for c in range(NCORE):
            maps.append({"qT": np.ascontiguousarray(pT[R_AQ + c * 128:R_AQ + (c + 1) * 128]),
                         "kT": np.ascontiguousarray(pT[R_K + c * 128:R_K + (c + 1) * 128]),
                         "v": tokmajor(pT[R_V + c * 128:R_V + (c + 1) * 128]),
                         "mask": mask})
        res = _run(nc, maps)
        return np.concatenate([np.asarray(res[c]["o"]).transpose(1, 0, 2).reshape(S, 128) for c in range(NCORE)], axis=1)

    def run_G(pT, i):
        nc = _prog("G", lambda: build_G())
        maps = []
        gcw = inp["gdn_conv_w"][i].astype(np.float32)
        for c in range(NCORE):
            d = {}
            d["qkvT"] = np.ascontiguousarray(np.stack([pT[ti * 1024 + c * 128:ti * 1024 + (c + 1) * 128] for ti in range(3)]))
            d["cw"] = np.ascontiguousarray(np.stack([gcw[:, ti * 1024 + c * 128:ti * 1024 + (c + 1) * 128].T for ti in range(3)], 1))
            d["z"] = tokmajor(pT[3072 + c * 128:3072 + (c + 1) * 128])
            d["ab"] = np.ascontiguousarray(np.stack([pT[R_GAB + c].reshape(64, 128), pT[R_GAB + 8 + c].reshape(64, 128)]))
            d["hp"] = np.ascontiguousarray(np.tile(np.array([[inp["gdn_a_log"][i][c], inp["gdn_dt_bias"][i][c]]], np.float32), (128, 1)))
            d["normw"] = np.ascontiguousarray(inp["gdn_norm_w"][i][None, :].astype(np.float32))
            maps.append(d)
        res = _run(nc, maps)
        return np.concatenate([np.asarray(res[c]["o"]).transpose(1, 0, 2).reshape(S, 128) for c in range(NCORE)], axis=1)

    xT_full = np.ascontiguousarray(x.T.astype(np.float32))
    _, pT = run_A(None, 0, xT_full, None)
    for i in range(L):
        mask = run_I(pT)
        o_att = run_T(pT, mask)
        o_gdn = run_G(pT, i)
        mixT = np.ascontiguousarray(np.concatenate([o_gdn, o_att], axis=1).T)
        xT_full, pT = run_A(i, i + 1 if i + 1 < L else None, xT_full, mixT)
    return np.ascontiguousarray(xT_full.T)[None].astype(np.float32)
```

```python
import numpy as np
import concourse.bass as bass
import concourse.mybir as mybir
from concourse.bass_utils import run_bass_kernel_spmd

F32 = mybir.dt.float32
BF16 = mybir.dt.bfloat16
ALU = mybir.AluOpType
AF = mybir.ActivationFunctionType
AX = mybir.AxisListType

SEM_ROT = 30000


class Prog:
    ENGS = ("pe", "act", "dve", "pool", "sp")

    def __init__(self, nc, n_dma_sems=16):
        self.nc = nc
        self.recs = []
        self.state = {}
        self.n_dma_sems = n_dma_sems

    @staticmethod
    def _split(key):
        if isinstance(key, tuple):
            return key[0], key[1:]
        return key, None

    def _conflicts(self, key):
        base, sub = self._split(key)
        d = self.state.get(base)
        if not d:
            return []
        if sub is None:
            return list(d.values())
        out = []
        if sub in d:
            out.append(d[sub])
        if None in d:
            out.append(d[None])
        return out

    def op(self, eng, fn, r=(), w=(), dma=False):
        oid = len(self.recs)
        deps = set()
        for k in r:
            for st in self._conflicts(k):
                if st[0] is not None:
                    deps.add(st[0])
        for k in w:
            for st in self._conflicts(k):
                if st[0] is not None:
                    deps.add(st[0])
                deps.update(st[1])
        deps.discard(oid)
        if eng == "pe":
            deps = {d for d in deps if self.recs[d]["eng"] != "pe"}
        self.recs.append(dict(id=oid, eng=eng, fn=fn, deps=sorted(deps), dma=dma, signal=dma))
        for k in r:
            base, sub = self._split(k)
            st = self.state.setdefault(base, {}).setdefault(sub, [None, []])
            st[1].append(oid)
        for k in w:
            base, sub = self._split(k)
            d = self.state.setdefault(base, {})
            if sub is None:
                d.clear()
            d[sub] = [oid, []]
        return oid

    def pe(self, fn, r=(), w=()):
        return self.op("pe", fn, r, w)

    def act(self, fn, r=(), w=()):
        return self.op("act", fn, r, w)

    def dve(self, fn, r=(), w=()):
        return self.op("dve", fn, r, w)

    def pool(self, fn, r=(), w=()):
        return self.op("pool", fn, r, w)

    def dma(self, out, in_, r=(), w=(), eng="sp", **kw):
        return self.op(eng, lambda e: e.dma_start(out=out, in_=in_, **kw), r, w, dma=True)

    def emit(self):
        nc = self.nc
        recs = self.recs
        for rec in recs:
            for d in rec["deps"]:
                recs[d]["signal"] = True
        cnt = {e: 0 for e in self.ENGS}
        sems = {}

        def get_sem(name):
            if name not in sems:
                sems[name] = nc.alloc_semaphore(name)
            return sems[name]

        dma_tot = [0] * self.n_dma_sems
        dma_rr = 0
        per_eng = {e: [] for e in self.ENGS}
        for rec in recs:
            e = rec["eng"]
            per_eng[e].append(rec)
            if rec["dma"]:
                i = dma_rr % self.n_dma_sems
                dma_rr += 1
                rec["prev_ev"] = (f"dq{i}", dma_tot[i]) if dma_tot[i] > 0 else None
                dma_tot[i] += 16
                rec["ev"] = (f"dq{i}", dma_tot[i])
                rec["inc"] = 16
            elif rec["signal"]:
                c = cnt[e]
                cnt[e] += 1
                rec["ev"] = (f"c_{e}_{c // SEM_ROT}", c % SEM_ROT + 1)
                rec["inc"] = 1
            else:
                rec["ev"] = None
        final_dma = [(f"dq{i}", dma_tot[i]) for i in range(self.n_dma_sems) if dma_tot[i] > 0]
        for name, _ in final_dma:
            get_sem(name)
        for rec in recs:
            if rec["ev"] is not None:
                get_sem(rec["ev"][0])

        def run_engine(ename, eng_obj, extra_final=False):
            waited = {}
            for rec in per_eng[ename]:
                evs = [recs[d]["ev"] for d in rec["deps"]]
                if rec["dma"] and rec["prev_ev"] is not None:
                    evs.append(rec["prev_ev"])
                for (sn, v) in evs:
                    if waited.get(sn, 0) < v:
                        eng_obj.wait_ge(sems[sn], v)
                        waited[sn] = v
                ins = rec["fn"](eng_obj)
                if rec["ev"] is not None:
                    ins.then_inc(sems[rec["ev"][0]], rec["inc"])
            if extra_final:
                for (sn, v) in final_dma:
                    if waited.get(sn, 0) < v:
                        eng_obj.wait_ge(sems[sn], v)
                        waited[sn] = v

        with nc.Block() as block:
            @block.tensor
            def _(eng):
                run_engine("pe", eng)

            @block.scalar
            def _(eng):
                run_engine("act", eng)

            @block.vector
            def _(eng):
                run_engine("dve", eng)

            @block.gpsimd
            def _(eng):
                run_engine("pool", eng)

            @block.sync
            def _(eng):
                run_engine("sp", eng, extra_final=True)
        return nc
import ml_dtypes

D = 2048; DFF = 5632; S = 8192; TS = 512; HALO = 2; TT = TS + HALO; NSH = 2; NCORE = 8
LN_EPS = 1e-5; RMS_EPS = 1e-6
DEPTH = 4
DN_ALPHA = (2 * DEPTH) ** 0.25
KC = D // 128
R_GAB = 4096; R_AQ = 4112; R_IQ = 5136; R_IK = 6160; R_IW = 6224; R_K = 6240; R_V = 7264; R_TOT = 8288


class Rot:
    def __init__(self, items):
        self.items = items; self.i = 0
    def next(self):
        it = self.items[self.i % len(self.items)]; self.i += 1
        return it


def build_A(do_post, do_proj):
    nc = bass.Bass("TRN2", target_bir_lowering=False)
    P = Prog(nc)
    def din(name, shape, dt=F32):
        return nc.dram_tensor(name, shape, dt, kind="ExternalInput").ap()
    def dout(name, shape, dt=F32):
        return nc.dram_tensor(name, shape, dt, kind="ExternalOutput").ap()
    def sb(name, shape, dt):
        return nc.alloc_sbuf_tensor("s_" + name, shape, dt)

    T_in = TT if do_post else TS
    xT_d = din("xT", [NSH, D, T_in])
    if do_post:
        mixT_d = din("mixT", [NSH, D, TT], BF16)
        wout_d = din("w_out", [D, D], BF16)
        wup_d = din("ffn_up", [D, 2 * DFF], BF16)
        wdn_d = din("ffn_down", [DFF, D], BF16)
        lnp_d = din("lnp", [128, 4, KC])
        cw_d = din("convw", [128, 88, 4])
        hf_d = din("haloflag", [128, NSH])
        x2T_o = dout("x2T", [NSH, D, TS])
    if do_proj:
        win_d = din("w_in", [D, 6496], BF16)
        wukv_d = din("w_ukv", [256, 2048], BF16)
        kvnw_d = din("kvnw", [128, 2])
        idxgb_d = din("idxgb", [64, 2])
        rope_d = din("rope", [NSH, 4, 128, TS])
        rmat_d = din("rmat", [2, 128, 128])
        pT_o = dout("pT", [NSH, R_TOT, TS])

    xT = sb("xT", [128, KC, TT], F32)
    xb = sb("xb", [128, KC, TT], BF16)
    wp = [sb(f"wp{i}", [128, 8192], BF16) for i in range(2)]
    ones = sb("ones", [128, 128], F32)
    st = [sb(f"st{i}", [128, 512], F32) for i in range(6)]
    strot = Rot(list(range(6)))
    PS = [nc.alloc_psum_tensor(f"ps{i}", [128, 512], F32) for i in range(8)]
    mmrot = Rot([0, 1, 2, 3])
    auxrot = Rot([4, 5])
    P.pool(lambda e: e.memset(ones[:], 1.0), w=["ones"])
    epsb = sb("epsb", [128, 2], F32)
    P.pool(lambda e: e.memset(epsb[:, 0:1], LN_EPS), w=[("epsb", 0)])
    P.pool(lambda e: e.memset(epsb[:, 1:2], RMS_EPS), w=[("epsb", 1)])
    if do_post:
        aT = sb("aT", [128, DFF // 128, TS], BF16)
        lnp = sb("lnp", [128, 4, KC], F32)
        cw = sb("cw", [128, 88, 4], F32)
        hf = sb("hf", [128, NSH], F32)
        hfull = [sb(f"hfull{i}", [128, TT], F32) for i in range(6)]
        uu = [sb(f"uu{i}", [128, TS], F32) for i in range(6)]
        P.dma(lnp[:], lnp_d, w=["lnp"])
        P.dma(cw[:], cw_d, w=["cw"])
        P.dma(hf[:], hf_d, w=["hf"])
        stat = [sb(f"stat{i}", [128, 512], F32) for i in range(4)]
    if do_proj:
        kvnw = sb("kvnw", [128, 2], F32)
        idxgb = sb("idxgb", [64, 2], F32)
        rope = sb("rope", [128, 4, TS], F32)
        rmat = sb("rmat", [128, 2, 128], F32)
        ckv = sb("ckv", [128, 2, TS], F32)
        ckvn = sb("ckvn", [128, 2, TS], BF16)
        wukv = sb("wukv", [128, 2, 2048], BF16)
        P.dma(kvnw[:], kvnw_d, w=["kvnw"])
        P.dma(idxgb[:], idxgb_d, w=["idxgb"])
        P.dma(rmat[:], rmat_d.rearrange("r p m -> p r m"), w=["rmat"])
        P.dma(wukv[:], wukv_d.rearrange("(c p) n -> p c n", p=128), w=["wukv"])
        if not do_post:
            stat = [sb(f"stat{i}", [128, 512], F32) for i in range(4)]

    wp_i = [0]

    def gemm(W_d, Kc, groups, rhs_fn, chunks, epilogue, pre=None):
        pw = 512 if Kc <= 16 else 128
        panels = []
        cur = []
        for gi, (c0, m) in enumerate(groups):
            if cur and (c0 + m - groups[cur[0]][0] > pw or c0 != groups[cur[-1]][0] + groups[cur[-1]][1]):
                panels.append(cur); cur = []
            cur.append(gi)
        if cur:
            panels.append(cur)
        Wv = W_d.rearrange("(c p) n -> p c n", p=128)

        def load(pi):
            b = wp_i[0] % 2; wp_i[0] += 1
            g0 = groups[panels[pi][0]][0]
            g1 = groups[panels[pi][-1]][0] + groups[panels[pi][-1]][1]
            wdt = g1 - g0
            view = wp[b][:, 0:Kc * wdt].rearrange("p (c n) -> p c n", c=Kc)
            P.dma(view, Wv[:, :, g0:g1], w=[f"wp{b}"])
            return b, g0, wdt

        nxt = load(0)
        for pi, pan in enumerate(panels):
            b, g0, wdt = nxt
            if pi + 1 < len(panels):
                nxt = load(pi + 1)
            view = wp[b][:, 0:Kc * wdt].rearrange("p (c n) -> p c n", c=Kc)
            for gi in pan:
                c0, m = groups[gi]
                for ci, (t0, n) in enumerate(chunks):
                    pidx = mmrot.next()
                    for k in range(Kc):
                        rap, rkeys = rhs_fn(k, t0, n)
                        P.pe(lambda e, pidx=pidx, k=k, rap=rap, c0=c0, m=m, n=n, view=view, g0=g0, Kc=Kc:
                             e.matmul(PS[pidx][0:m, 0:n], lhsT=view[:, k, c0 - g0:c0 - g0 + m], rhs=rap,
                                      start=(k == 0), stop=(k == Kc - 1)),
                             r=[f"wp{b}"] + rkeys, w=[f"ps{pidx}"])
                    epilogue(gi, ci, pidx, m, t0, n)

    def ln_feature_major(shard, which, t0, n):
        gi_, bi_ = (0, 1) if which == 1 else (2, 3)
        p1 = auxrot.next(); p2 = auxrot.next()
        for c in range(KC):
            s = strot.next()
            P.act(lambda e, c=c, s=s: e.activation(out=st[s][:, 0:n], in_=xT[:, c, t0:t0 + n], func=AF.Square),
                  r=[("xT", c)], w=[f"st{s}"])
            P.pe(lambda e, c=c: e.matmul(PS[p1][:, 0:n], lhsT=ones[:], rhs=xT[:, c, t0:t0 + n], start=(c == 0), stop=(c == KC - 1)),
                 r=["ones", ("xT", c)], w=[f"ps{p1}"])
            P.pe(lambda e, c=c, s=s: e.matmul(PS[p2][:, 0:n], lhsT=ones[:], rhs=st[s][:, 0:n], start=(c == 0), stop=(c == KC - 1)),
                 r=["ones", f"st{s}"], w=[f"ps{p2}"])
        mean, ex2, rstd, mr = stat
        P.act(lambda e: e.mul(out=mean[:, 0:n], in_=PS[p1][:, 0:n], mul=1.0 / D), r=[f"ps{p1}"], w=["stat0"])
        P.act(lambda e: e.mul(out=ex2[:, 0:n], in_=PS[p2][:, 0:n], mul=1.0 / D), r=[f"ps{p2}"], w=["stat1"])
        P.dve(lambda e: e.tensor_tensor(out=mr[:, 0:n], in0=mean[:, 0:n], in1=mean[:, 0:n], op=ALU.mult), r=["stat0"], w=["stat3"])
        P.dve(lambda e: e.tensor_tensor(out=ex2[:, 0:n], in0=ex2[:, 0:n], in1=mr[:, 0:n], op=ALU.subtract), r=["stat1", "stat3"], w=["stat1"])
        P.act(lambda e: e.activation(out=rstd[:, 0:n], in_=ex2[:, 0:n], func=AF.Sqrt, bias=epsb[:, 0:1], scale=1.0), r=["stat1", "epsb"], w=["stat2"])
        P.dve(lambda e: e.reciprocal(out=rstd[:, 0:n], in_=rstd[:, 0:n]), r=["stat2"], w=["stat2"])
        P.dve(lambda e: e.tensor_tensor(out=mr[:, 0:n], in0=mean[:, 0:n], in1=rstd[:, 0:n], op=ALU.mult), r=["stat0", "stat2"], w=["stat3"])
        for c in range(KC):
            s = strot.next()
            P.dve(lambda e, c=c, s=s: e.tensor_tensor(out=st[s][:, 0:n], in0=xT[:, c, t0:t0 + n], in1=mean[:, 0:n], op=ALU.subtract),
                   r=[("xT", c), "stat0"], w=[f"st{s}"])
            P.dve(lambda e, c=c, s=s: e.scalar_tensor_tensor(out=st[s][:, 0:n], in0=st[s][:, 0:n], scalar=lnp[:, gi_, c:c + 1], in1=rstd[:, 0:n], op0=ALU.mult, op1=ALU.mult),
                  r=[f"st{s}", "lnp", "stat2"], w=[f"st{s}"])
            P.act(lambda e, c=c, s=s: e.activation(out=xT[:, c, t0:t0 + n], in_=st[s][:, 0:n], func=AF.Identity, bias=lnp[:, bi_, c:c + 1], scale=1.0),
                  r=[f"st{s}", "lnp"], w=[("xT", c)])
            P.act(lambda e, c=c, s=s: e.activation(out=xb[:, c, t0:t0 + n], in_=st[s][:, 0:n], func=AF.Identity, bias=lnp[:, bi_, c:c + 1], scale=1.0),
                  r=[f"st{s}", "lnp"], w=[("xb", c)])

    def out_rows(shard, row0, m, src_ap, keys):
        P.dma(pT_o[shard, row0:row0 + m, :], src_ap, r=keys)

    for sh in range(NSH):
        for c4 in range(4):
            P.dma(xT[:, 4 * c4:4 * c4 + 4, 0:T_in], xT_d[sh, 512 * c4:512 * (c4 + 1), :].rearrange("(c p) t -> p c t", p=128),
                  w=[("xT", 4 * c4 + j) for j in range(4)])
        if do_post:
            for c4 in range(4):
                P.dma(xb[:, 4 * c4:4 * c4 + 4, :], mixT_d[sh, 512 * c4:512 * (c4 + 1), :].rearrange("(c p) t -> p c t", p=128),
                      w=[("xb", 4 * c4 + j) for j in range(4)])
            chunks_h = [(0, HALO), (HALO, TS)]
            def ep2(gi, ci, pidx, m, t0, n):
                P.dve(lambda e: e.scalar_tensor_tensor(out=xT[:, gi, t0:t0 + n], in0=xT[:, gi, t0:t0 + n], scalar=DN_ALPHA, in1=PS[pidx][:, 0:n], op0=ALU.mult, op1=ALU.add),
                      r=[("xT", gi), f"ps{pidx}"], w=[("xT", gi)])
            gemm(wout_d, KC, [(g * 128, 128) for g in range(KC)], lambda k, t0, n: (xb[:, k, t0:t0 + n], [("xb", k)]), chunks_h, ep2)
            ln_feature_major(sh, 1, 0, HALO)
            ln_feature_major(sh, 1, HALO, TS)
            NG = DFF // 128
            groups3 = []
            for j in range(NG):
                groups3.append((j * 128, 128))
            for j in range(NG):
                groups3.append((DFF + j * 128, 128))
            order = []
            for j0 in range(0, NG, 4):
                order += [(j * 128, 128) for j in range(j0, j0 + 4)] + [(DFF + j * 128, 128) for j in range(j0, j0 + 4)]
            hmap = {}
            def ep3(gi, ci, pidx, m, t0, n, order=order, sh=sh):
                c0 = order[gi][0]
                isval = c0 >= DFF
                j = (c0 - DFF) // 128 if isval else c0 // 128
                hb = (4 + j % 2) if isval else (j % 4)
                gb_ = j % 4
                ch = j + (NG if isval else 0)
                if ci == 0:
                    P.act(lambda e: e.activation(out=hfull[hb][:, 0:HALO], in_=PS[pidx][:, 0:HALO], func=AF.Copy, scale=hf[:, sh:sh + 1]),
                          r=[f"ps{pidx}", "hf"], w=[(f"hfull{hb}", 0)])
                    return
                P.act(lambda e: e.copy(out=hfull[hb][:, HALO:TT], in_=PS[pidx][:, 0:TS]), r=[f"ps{pidx}"], w=[(f"hfull{hb}", 1)])
                if not isval:
                    return
                hg = gb_
                ug = uu[hg]; u = uu[hb]
                chg = j
                def tap0(hbuf, ub, ubk, hk, c_):
                    P.dve(lambda e: e.tensor_scalar(out=ub[:], in0=hbuf[:, 2:2 + TS], scalar1=cw[:, c_, 2:3], scalar2=cw[:, c_, 3:4], op0=ALU.mult, op1=ALU.add),
                          r=[hk, "cw"], w=[ubk])
                def tapk(hbuf, ub, ubk, hk, c_, k):
                    P.dve(lambda e: e.scalar_tensor_tensor(out=ub[:], in0=hbuf[:, k:k + TS], scalar=cw[:, c_, k:k + 1], in1=ub[:], op0=ALU.mult, op1=ALU.add),
                          r=[hk, "cw", ubk], w=[ubk])
                tap0(hfull[hg], ug, f"uu{hg}", f"hfull{hg}", chg)
                tap0(hfull[hb], u, f"uu{hb}", f"hfull{hb}", ch)
                for k in (1, 0):
                    tapk(hfull[hg], ug, f"uu{hg}", f"hfull{hg}", chg, k)
                    tapk(hfull[hb], u, f"uu{hb}", f"hfull{hb}", ch, k)
                P.act(lambda e: e.activation(out=ug[:], in_=ug[:], func=AF.Silu), r=[f"uu{hg}"], w=[f"uu{hg}"])
                P.dve(lambda e: e.tensor_tensor(out=aT[:, j, :], in0=ug[:], in1=u[:], op=ALU.mult), r=[f"uu{hg}", f"uu{hb}"], w=[("aT", j)])
            gemm(wup_d, KC, order, lambda k, t0, n: (xb[:, k, t0:t0 + n], [("xb", k)]), chunks_h, ep3)
            def ep4(gi, ci, pidx, m, t0, n):
                P.dve(lambda e: e.scalar_tensor_tensor(out=xT[:, gi, HALO:TT], in0=xT[:, gi, HALO:TT], scalar=DN_ALPHA, in1=PS[pidx][:, 0:TS], op0=ALU.mult, op1=ALU.add),
                      r=[("xT", gi), f"ps{pidx}"], w=[("xT", gi)])
            gemm(wdn_d, DFF // 128, [(g * 128, 128) for g in range(KC)], lambda k, t0, n: (aT[:, k, :], [("aT", k)]), [(0, TS)], ep4)
            ln_feature_major(sh, 2, HALO, TS)
            for c4 in range(4):
                P.dma(x2T_o[sh, 512 * c4:512 * (c4 + 1), :].rearrange("(c p) t -> p c t", p=128), xT[:, 4 * c4:4 * c4 + 4, HALO:TT],
                      r=[("xT", 4 * c4 + j) for j in range(4)])
            xoff = HALO
        else:
            for c in range(KC):
                P.act(lambda e, c=c: e.copy(out=xb[:, c, 0:TS], in_=xT[:, c, 0:TS]), r=[("xT", c)], w=[("xb", c)])
            xoff = 0
        if not do_proj:
            continue
        P.dma(rope[:], rope_d[sh].rearrange("f p t -> p f t"), w=["rope"])
        segs = []
        for g in range(32):
            segs.append((g * 128, 128, "plain", g * 128))
        segs.append((4096, 16, "plain", R_GAB))
        for g in range(2):
            segs.append((5136 + g * 128, 128, "ckv", g))
        for g in range(8):
            segs.append((4112 + g * 128, 128, "rope128", R_AQ + g * 128))
        for g in range(8):
            segs.append((5392 + g * 128, 128, "rope64", R_IQ + g * 128))
        segs.append((6416, 64, "ik", R_IK))
        segs.append((6480, 16, "plain", R_IW))

        def rope_ep(src_sb, skey, m, ridx, tabc, tabs, outrow, sh=sh):
            pa = auxrot.next()
            P.pe(lambda e: e.matmul(PS[pa][0:m, 0:TS], lhsT=rmat[0:m, ridx, 0:m], rhs=src_sb, start=True, stop=True), r=["rmat", skey], w=[f"ps{pa}"])
            s1 = strot.next(); s2 = strot.next()
            P.dve(lambda e: e.tensor_tensor(out=st[s1][0:m, :], in0=src_sb, in1=rope[0:m, tabc, :], op=ALU.mult), r=[skey, "rope"], w=[f"st{s1}"])
            P.dve(lambda e: e.tensor_tensor(out=st[s2][0:m, :], in0=PS[pa][0:m, 0:TS], in1=rope[0:m, tabs, :], op=ALU.mult), r=[f"ps{pa}", "rope"], w=[f"st{s2}"])
            P.dve(lambda e: e.tensor_tensor(out=st[s1][0:m, :], in0=st[s1][0:m, :], in1=st[s2][0:m, :], op=ALU.add), r=[f"st{s1}", f"st{s2}"], w=[f"st{s1}"])
            out_rows(sh, outrow, m, st[s1][0:m, :], [f"st{s1}"])

        def ep1(gi, ci, pidx, m, t0, n, sh=sh):
            c0, m_, kind, orow = segs[gi]
            if kind == "plain":
                s = strot.next()
                P.act(lambda e: e.copy(out=st[s][0:m, :], in_=PS[pidx][0:m, 0:TS]), r=[f"ps{pidx}"], w=[f"st{s}"])
                out_rows(sh, orow, m, st[s][0:m, :], [f"st{s}"])
            elif kind in ("rope128", "rope64"):
                s = strot.next()
                P.act(lambda e: e.copy(out=st[s][0:m, :], in_=PS[pidx][0:m, 0:TS]), r=[f"ps{pidx}"], w=[f"st{s}"])
                if kind == "rope128":
                    rope_ep(st[s][0:m, :], f"st{s}", m, 0, 0, 1, orow)
                else:
                    rope_ep(st[s][0:m, :], f"st{s}", m, 1, 2, 3, orow)
            elif kind == "ckv":
                g = orow
                P.act(lambda e: e.copy(out=ckv[:, g, :], in_=PS[pidx][:, 0:TS]), r=[f"ps{pidx}"], w=[("ckv", g)])
                if g == 1:
                    pa = auxrot.next()
                    for j in range(2):
                        s = strot.next()
                        P.act(lambda e, j=j, s=s: e.activation(out=st[s][:], in_=ckv[:, j, :], func=AF.Square), r=[("ckv", j)], w=[f"st{s}"])
                        P.pe(lambda e, j=j, s=s: e.matmul(PS[pa][:, 0:TS], lhsT=ones[:], rhs=st[s][:], start=(j == 0), stop=(j == 1)), r=["ones", f"st{s}"], w=[f"ps{pa}"])
                    rs = stat[2]
                    P.act(lambda e: e.activation(out=rs[:], in_=PS[pa][:, 0:TS], func=AF.Sqrt, bias=epsb[:, 1:2], scale=1.0 / 256), r=[f"ps{pa}", "epsb"], w=["stat2"])
                    P.dve(lambda e: e.reciprocal(out=rs[:], in_=rs[:]), r=["stat2"], w=["stat2"])
                    for j in range(2):
                        P.dve(lambda e, j=j: e.scalar_tensor_tensor(out=ckvn[:, j, :], in0=ckv[:, j, :], scalar=kvnw[:, j:j + 1], in1=rs[:], op0=ALU.mult, op1=ALU.mult),
                              r=[("ckv", j), "kvnw", "stat2"], w=[("ckvn", j)])
                    for hg in range(16):
                        pidx2 = mmrot.next()
                        for k in range(2):
                            P.pe(lambda e, k=k, hg=hg, pidx2=pidx2: e.matmul(PS[pidx2][:, 0:TS], lhsT=wukv[:, k, hg * 128:(hg + 1) * 128], rhs=ckvn[:, k, :], start=(k == 0), stop=(k == 1)),
                                 r=["wukv", ("ckvn", k)], w=[f"ps{pidx2}"])
                        s = strot.next()
                        P.act(lambda e, s=s, pidx2=pidx2: e.copy(out=st[s][:], in_=PS[pidx2][:, 0:TS]), r=[f"ps{pidx2}"], w=[f"st{s}"])
                        if hg < 8:
                            rope_ep(st[s][:], f"st{s}", 128, 0, 0, 1, R_K + hg * 128)
                        else:
                            out_rows(sh, R_V + (hg - 8) * 128, 128, st[s][:], [f"st{s}"])
            elif kind == "ik":
                s = strot.next(); s2 = strot.next()
                P.act(lambda e: e.copy(out=st[s][0:64, :], in_=PS[pidx][0:64, 0:TS]), r=[f"ps{pidx}"], w=[f"st{s}"])
                P.act(lambda e: e.activation(out=st[s2][0:64, :], in_=st[s][0:64, :], func=AF.Square), r=[f"st{s}"], w=[f"st{s2}"])
                p1 = auxrot.next(); p2 = auxrot.next()
                P.pe(lambda e: e.matmul(PS[p1][0:64, 0:TS], lhsT=ones[0:64, 0:64], rhs=st[s][0:64, :], start=True, stop=True), r=["ones", f"st{s}"], w=[f"ps{p1}"])
                P.pe(lambda e: e.matmul(PS[p2][0:64, 0:TS], lhsT=ones[0:64, 0:64], rhs=st[s2][0:64, :], start=True, stop=True), r=["ones", f"st{s2}"], w=[f"ps{p2}"])
                mean, ex2, rstd, mr = stat
                P.act(lambda e: e.mul(out=mean[0:64, :], in_=PS[p1][0:64, 0:TS], mul=1.0 / 64), r=[f"ps{p1}"], w=["stat0"])
                P.act(lambda e: e.mul(out=ex2[0:64, :], in_=PS[p2][0:64, 0:TS], mul=1.0 / 64), r=[f"ps{p2}"], w=["stat1"])
                P.dve(lambda e: e.tensor_tensor(out=mr[0:64, :], in0=mean[0:64, :], in1=mean[0:64, :], op=ALU.mult), r=["stat0"], w=["stat3"])
                P.dve(lambda e: e.tensor_tensor(out=ex2[0:64, :], in0=ex2[0:64, :], in1=mr[0:64, :], op=ALU.subtract), r=["stat1", "stat3"], w=["stat1"])
                P.act(lambda e: e.activation(out=rstd[0:64, :], in_=ex2[0:64, :], func=AF.Sqrt, bias=epsb[0:64, 0:1], scale=1.0), r=["stat1", "epsb"], w=["stat2"])
                P.dve(lambda e: e.reciprocal(out=rstd[0:64, :], in_=rstd[0:64, :]), r=["stat2"], w=["stat2"])
                P.dve(lambda e: e.tensor_tensor(out=st[s][0:64, :], in0=st[s][0:64, :], in1=mean[0:64, :], op=ALU.subtract), r=[f"st{s}", "stat0"], w=[f"st{s}"])
                P.dve(lambda e: e.tensor_tensor(out=st[s][0:64, :], in0=st[s][0:64, :], in1=rstd[0:64, :], op=ALU.mult), r=[f"st{s}", "stat2"], w=[f"st{s}"])
                P.dve(lambda e: e.tensor_scalar(out=st[s][0:64, :], in0=st[s][0:64, :], scalar1=idxgb[:, 0:1], scalar2=idxgb[:, 1:2], op0=ALU.mult, op1=ALU.add), r=[f"st{s}", "idxgb"], w=[f"st{s}"])
                rope_ep(st[s][0:64, :], f"st{s}", 64, 1, 2, 3, orow)

        gemm(win_d, KC, [(c0, m) for (c0, m, _, _) in segs], lambda k, t0, n: (xb[:, k, xoff + t0:xoff + t0 + n], [("xb", k)]), [(0, TS)], ep1)
    P.emit()
    return nc

S = 8192; NSLOT = 8; NIT = 22; TOPK = 256
NEG = -1.0e30


def slot_qbs(core):
    return [core, 15 - core, 16 + core, 31 - core, 32 + core, 47 - core, 48 + core, 63 - core]


def build_I(nslot=NSLOT, nit=NIT):
    nc = bass.Bass("TRN2", target_bir_lowering=False)
    P = Prog(nc)
    def din(name, shape, dt=F32):
        return nc.dram_tensor(name, shape, dt, kind="ExternalInput").ap()
    def dout(name, shape, dt=F32):
        return nc.dram_tensor(name, shape, dt, kind="ExternalOutput").ap()
    def sb(name, shape, dt):
        return nc.alloc_sbuf_tensor("s_" + name, shape, dt)

    iqT_d = din("iqT", [NSLOT, 1024, 128])
    kiT_d = din("kiT2", [128, S])
    iw_d = din("iw", [128, NSLOT, 16])
    qpos_d = din("qpos", [128, NSLOT])
    m_o = [dout(f"m{j}", [128, 8 * (j + 1), 128], BF16) for j in range(nslot)]

    kis = sb("kis", [128, 2048], F32)
    kif = sb("kif", [128, 2048], F32)
    ki = sb("ki", [128, S], BF16)
    kl = sb("kl", [64, S], BF16)
    qis = sb("qis", [128, 16, 128], F32)
    qif = sb("qif", [128, 16, 128], F32)
    qh = sb("qh", [128, 16, 128], BF16)
    qi = sb("qi", [128, 16, 128], BF16)
    iw = sb("iw", [128, NSLOT, 16], F32)
    qpos = sb("qpos", [128, NSLOT], F32)
    accs = [sb(f"acc{i}", [128, S], F32) for i in range(2)]
    junk = sb("junk", [128, S], BF16)
    junk2 = sb("junk2", [128, S], BF16)
    msk = sb("msk", [128, S], BF16)
    rl = [sb(f"rl{i}", [128, 512], F32) for i in range(4)]
    kpos = sb("kpos", [128, 1024], F32)
    pen = sb("pen", [128, 1024], F32)
    identf = sb("identf", [128, 128], F32)
    ident = sb("ident", [128, 128], BF16)
    sms = [sb(f"sm{i}", [128, 8], F32) for i in range(2)]
    mst = [sb(f"mst{i}", [128, 4, 128], BF16) for i in range(3)]
    PS = [nc.alloc_psum_tensor(f"ps{i}", [128, 512], F32) for i in range(6)]
    PT = [nc.alloc_psum_tensor(f"pt{i}", [128, 4, 128], BF16) for i in range(2)]

    P.pool(lambda e: e.memset(identf[:], 0.0), w=["identf"])
    P.pool(lambda e: e.affine_select(out=identf[:], in_=identf[:], pattern=[[-1, 128]], compare_op=ALU.not_equal, fill=1.0, base=0, channel_multiplier=1), r=["identf"], w=["identf"])
    P.pool(lambda e: e.tensor_copy(out=ident[:], in_=identf[:]), r=["identf"], w=["ident"])
    P.dma(iw[:], iw_d, w=["iw"])
    P.dma(qpos[:], qpos_d, w=["qpos"])
    for c in range(4):
        P.dma(kis[:], kiT_d[:, c * 2048:(c + 1) * 2048], w=["kis"])
        P.act(lambda e, c=c: e.copy(out=ki[:, c * 2048:(c + 1) * 2048], in_=kis[:]), r=["kis"], w=[("ki", c)])
        P.dve(lambda e, c=c: e.tensor_copy(out=kif[0:64, :], in_=ki[0:64, c * 2048:(c + 1) * 2048]), r=[("ki", c)], w=["kif"])
        P.dve(lambda e, c=c: e.tensor_tensor(out=kl[:, c * 2048:(c + 1) * 2048], in0=kis[0:64, :], in1=kif[0:64, :], op=ALU.subtract), r=["kis", "kif"], w=[("kl", c)])

    cnts = dict(ri=0, pi=0, ti=0, mi=0)

    def do_slot(j):
        L = 1024 * (j + 1)
        acc = accs[j % 2]; AK = f"acc{j % 2}"
        sm = sms[j % 2]; SK = f"sm{j % 2}"
        use_act = (j % 2 == 1)
        P.dma(qis[0:64, :, :], iqT_d[j].rearrange("(h d) t -> d h t", d=64), w=[("qis", 0)])
        P.dma(qis[64:128, :, :], iqT_d[j].rearrange("(h d) t -> d h t", d=64), w=[("qis", 1)])
        P.act(lambda e: e.copy(out=qh[:], in_=qis[:]), r=["qis"], w=["qh"])
        P.act(lambda e: e.copy(out=qi[0:64, :, :], in_=qh[0:64, :, :]), r=["qh"], w=[("qi", 0)])
        P.dve(lambda e: e.tensor_copy(out=qif[64:128, :, :], in_=qh[64:128, :, :]), r=["qh"], w=["qif"])
        P.dve(lambda e: e.tensor_tensor(out=qi[64:128, :, :], in0=qis[64:128, :, :], in1=qif[64:128, :, :], op=ALU.subtract), r=["qis", "qif"], w=[("qi", 1)])
        for h in range(16):
            for kt in range(L // 512):
                pidx = cnts['pi'] % 6; cnts['pi'] += 1
                P.pe(lambda e, h=h, kt=kt, pidx=pidx: e.matmul(PS[pidx][:, :], lhsT=qi[:, h, :], rhs=ki[:, kt * 512:(kt + 1) * 512], start=True, stop=False),
                     r=["qi", ("ki", kt // 4)], w=[f"ps{pidx}"])
                P.pe(lambda e, h=h, kt=kt, pidx=pidx: e.matmul(PS[pidx][:, :], lhsT=qi[0:64, h, :], rhs=kl[0:64, kt * 512:(kt + 1) * 512], start=False, stop=True),
                     r=["qi", ("kl", kt // 4)], w=[f"ps{pidx}"])
                r_ = cnts['ri'] % 4; cnts['ri'] += 1
                P.act(lambda e, r_=r_, pidx=pidx: e.activation(out=rl[r_][:], in_=PS[pidx][:, :], func=AF.Relu), r=[f"ps{pidx}"], w=[f"rl{r_}"])
                if h == 0:
                    P.dve(lambda e, r_=r_, kt=kt, j=j: e.tensor_scalar(out=acc[:, kt * 512:(kt + 1) * 512], in0=rl[r_][:], scalar1=iw[:, j, 0:1], scalar2=None, op0=ALU.mult),
                          r=[f"rl{r_}", "iw"], w=[(AK, kt)])
                else:
                    P.dve(lambda e, r_=r_, kt=kt, j=j, h=h: e.scalar_tensor_tensor(out=acc[:, kt * 512:(kt + 1) * 512], in0=rl[r_][:], scalar=iw[:, j, h:h + 1], in1=acc[:, kt * 512:(kt + 1) * 512], op0=ALU.mult, op1=ALU.add),
                          r=[f"rl{r_}", "iw", (AK, kt)], w=[(AK, kt)])
        P.dve(lambda e, L=L: e.tensor_reduce(out=sm[:, 0:1], in_=acc[:, 0:L], axis=AX.X, op=ALU.min), r=[AK], w=[(SK, 0)])
        P.dve(lambda e, L=L: e.tensor_reduce(out=sm[:, 5:6], in_=acc[:, 0:L], axis=AX.X, op=ALU.max), r=[AK], w=[(SK, 5)])
        P.dve(lambda e: e.tensor_tensor(out=sm[:, 1:2], in0=sm[:, 5:6], in1=sm[:, 0:1], op=ALU.subtract), r=[(SK, 5), (SK, 0)], w=[(SK, 1)])
        P.dve(lambda e: e.tensor_scalar(out=sm[:, 1:2], in0=sm[:, 1:2], scalar1=0.5, scalar2=1e-6, op0=ALU.mult, op1=ALU.add), r=[(SK, 1)], w=[(SK, 1)])
        P.pool(lambda e, j=j: e.iota(kpos[:], pattern=[[1, 1024]], base=1024 * j, channel_multiplier=0, allow_small_or_imprecise_dtypes=True), w=["kpos"])
        P.dve(lambda e, j=j: e.tensor_scalar(out=pen[:], in0=kpos[:], scalar1=qpos[:, j:j + 1], scalar2=NEG, op0=ALU.is_gt, op1=ALU.mult), r=["kpos", "qpos"], w=["pen"])
        P.dve(lambda e, L=L: e.tensor_tensor(out=acc[:, L - 1024:L], in0=acc[:, L - 1024:L], in1=pen[:], op=ALU.add), r=[AK, "pen"], w=[AK])
        if use_act:
            P.dve(lambda e, sm=sm: e.tensor_scalar(out=sm[:, 0:2], in0=sm[:, 0:2], scalar1=-1.0, scalar2=None, op0=ALU.mult), r=[(SK, 0), (SK, 1)], w=[(SK, 0), (SK, 1)])
        for it in range(nit):
            P.dve(lambda e, sm=sm: e.tensor_tensor(out=sm[:, 2:3], in0=sm[:, 0:1], in1=sm[:, 1:2], op=ALU.add), r=[(SK, 0), (SK, 1)], w=[(SK, 2)])
            if use_act:
                P.act(lambda e, L=L, sm=sm, acc=acc: e.activation(out=junk[:, 0:L], in_=acc[:, 0:L], func=AF.Sign, bias=sm[:, 2:3], scale=1.0, accum_out=sm[:, 3:4]),
                      r=[AK, (SK, 2)], w=["junk", (SK, 3)])
                P.dve(lambda e, sm=sm, L=L: e.tensor_scalar(out=sm[:, 4:5], in0=sm[:, 3:4], scalar1=float(2 * TOPK - 1 - L), scalar2=None, op0=ALU.is_ge), r=[(SK, 3)], w=[(SK, 4)])
            else:
                P.dve(lambda e, L=L, sm=sm, acc=acc: e.tensor_scalar(out=junk2[:, 0:L], in0=acc[:, 0:L], scalar1=sm[:, 2:3], scalar2=0.0, op0=ALU.is_ge, op1=ALU.add, accum_out=sm[:, 3:4]),
                      r=[AK, (SK, 2)], w=["junk2", (SK, 3)])
                P.dve(lambda e, sm=sm: e.tensor_scalar(out=sm[:, 4:5], in0=sm[:, 3:4], scalar1=TOPK - 0.5, scalar2=None, op0=ALU.is_ge), r=[(SK, 3)], w=[(SK, 4)])
            P.dve(lambda e, sm=sm: e.scalar_tensor_tensor(out=sm[:, 0:1], in0=sm[:, 4:5], scalar=sm[:, 1:2], in1=sm[:, 0:1], op0=ALU.mult, op1=ALU.add), r=[(SK, 4), (SK, 1), (SK, 0)], w=[(SK, 0)])
            P.dve(lambda e, sm=sm: e.tensor_scalar(out=sm[:, 1:2], in0=sm[:, 1:2], scalar1=0.5, scalar2=None, op0=ALU.mult), r=[(SK, 1)], w=[(SK, 1)])
        if use_act:
            P.dve(lambda e, sm=sm: e.tensor_scalar(out=sm[:, 0:1], in0=sm[:, 0:1], scalar1=-1.0, scalar2=None, op0=ALU.mult), r=[(SK, 0)], w=[(SK, 0)])
        P.dve(lambda e, L=L: e.tensor_scalar(out=msk[:, 0:L], in0=acc[:, 0:L], scalar1=sm[:, 0:1], scalar2=None, op0=ALU.is_ge), r=[AK, (SK, 0)], w=["msk"])
        for b4 in range(L // 512):
            t_ = cnts['ti'] % 2; cnts['ti'] += 1
            for q in range(4):
                kb = b4 * 4 + q
                P.pe(lambda e, kb=kb, q=q, t_=t_: e.transpose(out=PT[t_][:, q, :], in_=msk[:, kb * 128:(kb + 1) * 128], identity=ident[:]), r=["msk", "ident"], w=[(f"pt{t_}", q)])
            m_ = cnts['mi'] % 3; cnts['mi'] += 1
            P.act(lambda e, t_=t_, m_=m_: e.copy(out=mst[m_][:], in_=PT[t_][:]), r=[f"pt{t_}"], w=[f"mst{m_}"])
            P.dma(m_o[j][:, b4 * 4:(b4 + 1) * 4, :], mst[m_][:], r=[f"mst{m_}"])

    for j in range(nslot):
        do_slot(j)
    P.emit()
    return nc

S = 8192; NQB = 64
SCALE = 128 ** -0.5
NBLK = NQB * (NQB + 1) // 2


def blk_off(qb):
    return qb * (qb + 1) // 2


def build_T(nqb=NQB):
    nc = bass.Bass("TRN2", target_bir_lowering=False)
    P = Prog(nc)
    def din(name, shape, dt=F32):
        return nc.dram_tensor(name, shape, dt, kind="ExternalInput").ap()
    def dout(name, shape, dt=F32):
        return nc.dram_tensor(name, shape, dt, kind="ExternalOutput").ap()
    def sb(name, shape, dt):
        return nc.alloc_sbuf_tensor("s_" + name, shape, dt)

    qT_d = din("qT", [128, S])
    kT_d = din("kT", [128, S])
    v_d = din("v", [128, NQB, 128])
    mask_d = din("mask", [128, NBLK, 128], BF16)
    o_o = dout("o", [128, NQB, 128], BF16)

    stg = sb("stg", [128, 2048], F32)
    qb_ = sb("qb", [128, S], BF16)
    kb_ = sb("kb", [128, S], BF16)
    va = sb("va", [128, NQB, 129], BF16)
    ones = sb("ones", [128, 128], F32)
    sq = [sb(f"sq{i}", [128, 512], F32) for i in range(2)]
    mx = sb("mx", [128, 8], F32)
    mk = [sb(f"mk{i}", [128, NQB, 128], BF16) for i in range(2)]
    E = [sb(f"E{i}", [128, 4, 128], BF16) for i in range(3)]
    PTt = [sb(f"PT{i}", [128, 4, 128], BF16) for i in range(3)]
    ost = [sb(f"ost{i}", [128, 8, 128], BF16) for i in range(2)]
    rc = sb("rc", [128, 4], F32)
    PS = [nc.alloc_psum_tensor(f"ps{i}", [128, 512], F32) for i in range(4)]
    PO = [nc.alloc_psum_tensor(f"po{i}", [128, 512], F32) for i in range(2)]
    PX = nc.alloc_psum_tensor("px", [128, 512], F32)

    P.pool(lambda e: e.memset(ones[:], 1.0), w=["ones"])
    P.pool(lambda e: e.memset(va[:, :, 128:129], 1.0), w=[("va", "ones")])
    P.pool(lambda e: e.memset(mx[:], 0.0), w=["mx"])
    for which, (src, dst, dname) in enumerate(((qT_d, qb_, "qb"), (kT_d, kb_, "kb"))):
        for c in range(4):
            P.dma(stg[:], src[:, c * 2048:(c + 1) * 2048], w=["stg"])
            P.act(lambda e, c=c, dst=dst: e.copy(out=dst[:, c * 2048:(c + 1) * 2048], in_=stg[:]), r=["stg"], w=[(dname, c)])
            for c2 in range(4):
                s_ = (c * 4 + c2) % 2
                P.dve(lambda e, c2=c2, s_=s_: e.tensor_tensor(out=sq[s_][:], in0=stg[:, c2 * 512:(c2 + 1) * 512], in1=stg[:, c2 * 512:(c2 + 1) * 512], op=ALU.mult), r=["stg"], w=[f"sq{s_}"])
                P.pe(lambda e, s_=s_: e.matmul(PX[:, :], lhsT=ones[:], rhs=sq[s_][:], start=True, stop=True), r=["ones", f"sq{s_}"], w=["px"])
                P.dve(lambda e, which=which: e.tensor_reduce(out=mx[:, 2 + which:3 + which], in_=PX[:, :], axis=AX.X, op=ALU.max), r=["px"], w=[("mx", 2 + which)])
                P.dve(lambda e, which=which: e.tensor_tensor(out=mx[:, which:which + 1], in0=mx[:, which:which + 1], in1=mx[:, 2 + which:3 + which], op=ALU.max), r=[("mx", which), ("mx", 2 + which)], w=[("mx", which)])
    P.dve(lambda e: e.tensor_tensor(out=mx[:, 4:5], in0=mx[:, 0:1], in1=mx[:, 1:2], op=ALU.mult), r=[("mx", 0), ("mx", 1)], w=[("mx", 4)])
    P.act(lambda e: e.activation(out=mx[:, 5:6], in_=mx[:, 4:5], func=AF.Sqrt), r=[("mx", 4)], w=[("mx", 5)])
    P.dve(lambda e: e.tensor_scalar(out=mx[:, 6:7], in0=mx[:, 5:6], scalar1=-SCALE, scalar2=None, op0=ALU.mult), r=[("mx", 5)], w=[("mx", 6)])
    for c in range(4):
        P.dma(stg[:].rearrange("p (b d) -> p b d", d=128), v_d[:, c * 16:(c + 1) * 16, :], w=["stg"])
        P.act(lambda e, c=c: e.copy(out=va[:, c * 16:(c + 1) * 16, 0:128], in_=stg[:].rearrange("p (b d) -> p b d", d=128)), r=["stg"], w=[("va", c)])

    E4 = E + [sb("E3", [128, 4, 128], BF16)]
    PT4 = PTt + [sb("PT3", [128, 4, 128], BF16)]
    groups = []
    for qb in range(nqb):
        nb = qb + 1
        for k0 in range(0, nb, 4):
            groups.append((qb, k0, min(4, nb - k0)))

    def stage1(gi):
        qb, k0, n = groups[gi]
        mb = qb % 2
        if k0 == 0:
            P.dma(mk[mb][:, 0:qb + 1, :], mask_d[:, blk_off(qb):blk_off(qb) + qb + 1, :], w=[f"mk{mb}"])
        pidx = gi % 4
        for q in range(n):
            kbi = k0 + q
            P.pe(lambda e, q=q, kbi=kbi, pidx=pidx, qb=qb: e.matmul(PS[pidx][:, q * 128:(q + 1) * 128], lhsT=kb_[:, kbi * 128:(kbi + 1) * 128], rhs=qb_[:, qb * 128:(qb + 1) * 128], start=True, stop=True),
                 r=[("kb", kbi // 16), ("qb", qb // 16)], w=[(f"ps{pidx}", q)])
        e_ = gi % 4
        P.act(lambda e, e_=e_, pidx=pidx, n=n: e.activation(out=E4[e_][:, 0:n, :], in_=PS[pidx][:, 0:n * 128].rearrange("p (b t) -> p b t", t=128), func=AF.Exp, bias=mx[:, 6:7], scale=SCALE),
              r=[f"ps{pidx}", ("mx", 6)], w=[f"E{e_}"])
        eng = P.dve
        eng(lambda e, e_=e_, n=n, k0=k0, mb=mb: e.tensor_tensor(out=PT4[e_][:, 0:n, :], in0=E4[e_][:, 0:n, :], in1=mk[mb][:, k0:k0 + n, :], op=ALU.mult),
            r=[f"E{e_}", f"mk{mb}"], w=[f"PT{e_}"])

    def stage2(gi):
        qb, k0, n = groups[gi]
        nb = qb + 1
        po = qb % 2
        e_ = gi % 4
        for q in range(n):
            kbi = k0 + q
            P.pe(lambda e, q=q, kbi=kbi, e_=e_, po=po, nb=nb: e.matmul(PO[po][:, 0:129], lhsT=PT4[e_][:, q, :], rhs=va[:, kbi, :], start=(kbi == 0), stop=(kbi == nb - 1)),
                 r=[f"PT{e_}", ("va", kbi // 16), ("va", "ones")], w=[f"po{po}"])
        if k0 + n == nb:
            ob = (qb // 8) % 2
            P.dve(lambda e, po=po, qb=qb: e.reciprocal(out=rc[:, qb % 4:qb % 4 + 1], in_=PO[po][:, 128:129]), r=[f"po{po}"], w=[("rc", qb % 4)])
            P.dve(lambda e, po=po, qb=qb, ob=ob: e.tensor_scalar(out=ost[ob][:, qb % 8, :], in0=PO[po][:, 0:128], scalar1=rc[:, qb % 4:qb % 4 + 1], scalar2=None, op0=ALU.mult),
                  r=[f"po{po}", ("rc", qb % 4)], w=[(f"ost{ob}", qb % 8)])
            if qb % 8 == 7 or qb == nqb - 1:
                q0 = (qb // 8) * 8
                cnt = qb - q0 + 1
                P.dma(o_o[:, q0:q0 + cnt, :], ost[ob][:, 0:cnt, :], r=[f"ost{ob}"])

    LOOK = 2
    for gi in range(len(groups) + LOOK):
        if gi < len(groups):
            stage1(gi)
        if gi - LOOK >= 0:
            stage2(gi - LOOK)
    P.emit()
    return nc

S = 8192; C = 128; GS = 4
RMS_EPS = 1e-6
QSCALE = 128 ** -0.5


def build_G(nch=64):
    T = nch * C
    NG = nch // GS
    nc = bass.Bass("TRN2", target_bir_lowering=False)
    P = Prog(nc)
    def din(name, shape, dt=F32):
        return nc.dram_tensor(name, shape, dt, kind="ExternalInput").ap()
    def dout(name, shape, dt=F32):
        return nc.dram_tensor(name, shape, dt, kind="ExternalOutput").ap()
    def sb(name, shape, dt):
        return nc.alloc_sbuf_tensor("s_" + name, shape, dt)

    qkv_d = din("qkvT", [3, 128, T])
    cw_d = din("cw", [128, 3, 4])
    z_d = din("z", [128, nch, 128])
    ab_d = din("ab", [2, nch, 128])
    hp_d = din("hp", [128, 2])
    nw_d = din("normw", [1, 128])
    o_o = dout("o", [128, nch, 128], BF16)
    gscr = nc.dram_tensor("gscr", [2, nch * 128], F32).ap()

    X = sb("X", [128, T + 3], F32)
    U = sb("U", [128, T], F32)
    kT = sb("kT", [128, T], BF16)
    qT = sb("qT", [128, T], BF16)
    qdT = sb("qdT", [128, T], BF16)
    vT = sb("vT", [128, T], BF16)
    cw = sb("cw", [128, 3, 4], F32)
    hp = sb("hp", [128, 2], F32)
    nwr = sb("nwr", [128, 128], F32)
    ones = sb("ones", [128, 128], F32)
    identf = sb("identf", [128, 128], F32)
    ident = sb("ident", [128, 128], BF16)
    utri = sb("utri", [128, 128], F32)
    dmask = sb("dmask", [128, 128], F32)
    nstrict = sb("nstrict", [128, 128], F32)
    epsb = sb("epsb", [128, 2], F32)
    tmp = [sb(f"tmp{i}", [128, 512], F32) for i in range(4)]
    a_sb = sb("a_sb", [64, 128], F32)
    b_sb = sb("b_sb", [64, 128], F32)
    gcc = sb("gcc", [64, 128], F32)
    cols = sb("cols", [128, 8, 64], F32)
    nea = sb("nea", [128, 2], F32)
    T1 = sb("T1", [128, 2048], F32)
    def gb(name, dt):
        return [sb(f"{name}{i}", [128, GS, 128], dt) for i in range(2)]
    bek_g = gb("bek", BF16); kdec_g = gb("kdec", BF16); bv_g = gb("bv", BF16)
    attn_g = gb("attn", BF16); u_g = gb("u", F32); wT_g = gb("wT", BF16)
    dc_g = gb("dc", F32); t_g = gb("tt", F32)
    Qb = [sb(f"Q{i}", [128, GS, 128], BF16) for i in range(2)]
    Rb = [sb(f"R{i}", [128, GS, 128], BF16) for i in range(2)]
    Yb = sb("Y", [128, GS, 128], BF16)
    zg = gb("zg", F32); gw = gb("gw", F32); og = gb("og", BF16)
    Sf = sb("Sf", [128, 128], F32)
    Sb = sb("Sb", [128, 128], BF16)
    vnew = sb("vnew", [128, 128], BF16)
    junk = sb("junk", [128, 128], F32)
    ssq = sb("ssq", [128, 2], F32)
    PS = [nc.alloc_psum_tensor(f"ps{i}", [128, 512], F32) for i in range(7)]
    PTb = nc.alloc_psum_tensor("ptb", [128, GS, 128], BF16)
    PQ, PR, PY, PA, PB, PSC_A, PSC_B = range(7)

    P.pool(lambda e: e.memset(ones[:], 1.0), w=["ones"])
    P.pool(lambda e: e.memset(epsb[:], RMS_EPS), w=["epsb"])
    P.pool(lambda e: e.memset(identf[:], 0.0), w=["identf"])
    P.pool(lambda e: e.affine_select(out=identf[:], in_=identf[:], pattern=[[-1, 128]], compare_op=ALU.not_equal, fill=1.0, base=0, channel_multiplier=1), r=["identf"], w=["identf"])
    P.pool(lambda e: e.tensor_copy(out=ident[:], in_=identf[:]), r=["identf"], w=["ident"])
    P.pool(lambda e: e.memset(utri[:], 1.0), w=["utri"])
    P.pool(lambda e: e.affine_select(out=utri[:], in_=utri[:], pattern=[[1, 128]], compare_op=ALU.is_ge, fill=0.0, base=0, channel_multiplier=-1), r=["utri"], w=["utri"])
    P.pool(lambda e: e.memset(dmask[:], 0.0), w=["dmask"])
    P.pool(lambda e: e.affine_select(out=dmask[:], in_=dmask[:], pattern=[[1, 128]], compare_op=ALU.is_ge, fill=-30000.0, base=0, channel_multiplier=-1), r=["dmask"], w=["dmask"])
    P.pool(lambda e: e.memset(nstrict[:], -1.0), w=["nstrict"])
    P.pool(lambda e: e.affine_select(out=nstrict[:], in_=nstrict[:], pattern=[[1, 128]], compare_op=ALU.is_gt, fill=0.0, base=0, channel_multiplier=-1), r=["nstrict"], w=["nstrict"])
    P.pool(lambda e: e.memset(X[:, 0:3], 0.0), w=[("X", "pad")])
    P.pool(lambda e: e.memset(Sf[:], 0.0), w=["Sf"])
    P.pool(lambda e: e.memset(Sb[:], 0.0), w=["Sb"])
    P.dma(cw[:], cw_d, w=["cw"])
    P.dma(hp[:], hp_d, w=["hp"])
    P.dma(nwr[:], nw_d.partition_broadcast(128), w=["nwr"])

    PW = min(2048, T)
    NP = T // PW
    ti_ = 0
    for ti, dst in ((0, qT), (1, kT), (2, vT)):
        dname = ("qT", "kT", "vT")[ti]
        for c in range(NP):
            P.dma(X[:, 3 + c * PW:3 + (c + 1) * PW], qkv_d[ti, :, c * PW:(c + 1) * PW], w=[("X", c)])
        for c in range(NP):
            lo = c * PW
            rk = [("X", c), ("X", "pad")] + ([("X", c - 1)] if c > 0 else [])
            P.dve(lambda e, lo=lo, ti=ti: e.tensor_scalar(out=U[:, lo:lo + PW], in0=X[:, lo + 3:lo + 3 + PW], scalar1=cw[:, ti, 3:4], scalar2=None, op0=ALU.mult), r=rk + ["cw"], w=[("U", c)])
            for jj in (2, 1, 0):
                P.dve(lambda e, lo=lo, ti=ti, jj=jj: e.scalar_tensor_tensor(out=U[:, lo:lo + PW], in0=X[:, lo + jj:lo + jj + PW], scalar=cw[:, ti, jj:jj + 1], in1=U[:, lo:lo + PW], op0=ALU.mult, op1=ALU.add),
                      r=rk + ["cw", ("U", c)], w=[("U", c)])
            P.act(lambda e, lo=lo: e.activation(out=U[:, lo:lo + PW], in_=U[:, lo:lo + PW], func=AF.Silu), r=[("U", c)], w=[("U", c)])
            if ti == 2:
                P.act(lambda e, lo=lo: e.copy(out=vT[:, lo:lo + PW], in_=U[:, lo:lo + PW]), r=[("U", c)], w=[("vT", c)])
                continue
            for c2 in range(PW // 512):
                l2 = lo + c2 * 512
                t_ = ti_ % 4; ti_ += 1
                P.dve(lambda e, l2=l2, t_=t_: e.tensor_tensor(out=tmp[t_][:], in0=U[:, l2:l2 + 512], in1=U[:, l2:l2 + 512], op=ALU.mult), r=[("U", c)], w=[f"tmp{t_}"])
                P.pe(lambda e, t_=t_: e.matmul(PS[PY][:, :], lhsT=ones[:], rhs=tmp[t_][:], start=True, stop=True), r=["ones", f"tmp{t_}"], w=[f"ps{PY}"])
                P.act(lambda e, t_=t_: e.activation(out=tmp[t_][:], in_=PS[PY][:, :], func=AF.Sqrt, bias=epsb[:, 0:1], scale=1.0), r=[f"ps{PY}", "epsb"], w=[f"tmp{t_}"])
                P.dve(lambda e, t_=t_: e.reciprocal(out=tmp[t_][:], in_=tmp[t_][:]), r=[f"tmp{t_}"], w=[f"tmp{t_}"])
                sc = QSCALE if ti == 0 else 1.0
                P.dve(lambda e, l2=l2, t_=t_, dst=dst, sc=sc: e.scalar_tensor_tensor(out=dst[:, l2:l2 + 512], in0=U[:, l2:l2 + 512], scalar=sc, in1=tmp[t_][:], op0=ALU.mult, op1=ALU.mult),
                      r=[("U", c), f"tmp{t_}"], w=[(dname, c)])

    P.dma(a_sb[0:nch, :], ab_d[0], w=["a_sb"])
    P.dma(b_sb[0:nch, :], ab_d[1], w=["b_sb"])
    P.act(lambda e: e.activation(out=a_sb[0:nch, :], in_=a_sb[0:nch, :], func=AF.Exp, bias=hp[0:nch, 1:2], scale=1.0), r=["a_sb", "hp"], w=["a_sb"])
    P.act(lambda e: e.activation(out=a_sb[0:nch, :], in_=a_sb[0:nch, :], func=AF.Ln, bias=ones[0:nch, 0:1], scale=1.0), r=["a_sb", "ones"], w=["a_sb"])
    P.act(lambda e: e.activation(out=nea[:, 0:1], in_=hp[:, 0:1], func=AF.Exp), r=["hp"], w=["nea"])
    P.dve(lambda e: e.tensor_scalar(out=a_sb[0:nch, :], in0=a_sb[0:nch, :], scalar1=nea[0:nch, 0:1], scalar2=-1.0, op0=ALU.mult, op1=ALU.mult), r=["a_sb", "nea"], w=["a_sb"])
    P.act(lambda e: e.activation(out=b_sb[0:nch, :], in_=b_sb[0:nch, :], func=AF.Sigmoid), r=["b_sb"], w=["b_sb"])
    P.dma(gscr[1].rearrange("(n i) -> n i", i=128), b_sb[0:nch, :], r=["b_sb"], w=[("gscr", 1)])
    P.pe(lambda e: e.transpose(out=PS[PQ][:, 0:nch], in_=a_sb[0:nch, :], identity=identf[0:nch, 0:nch]), r=["a_sb", "identf"], w=[f"ps{PQ}"])
    P.act(lambda e: e.copy(out=cols[:, 0, 0:nch], in_=PS[PQ][:, 0:nch]), r=[f"ps{PQ}"], w=[("cols", 0)])
    P.pe(lambda e: e.transpose(out=PS[PR][:, 0:nch], in_=b_sb[0:nch, :], identity=identf[0:nch, 0:nch]), r=["b_sb", "identf"], w=[f"ps{PR}"])
    P.act(lambda e: e.copy(out=cols[:, 1, 0:nch], in_=PS[PR][:, 0:nch]), r=[f"ps{PR}"], w=[("cols", 1)])
    P.pe(lambda e: e.matmul(PS[PA][:, 0:nch], lhsT=utri[:], rhs=cols[:, 0, 0:nch], start=True, stop=True), r=["utri", ("cols", 0)], w=[f"ps{PA}"])
    P.act(lambda e: e.copy(out=cols[:, 2, 0:nch], in_=PS[PA][:, 0:nch]), r=[f"ps{PA}"], w=[("cols", 2)])
    P.pe(lambda e: e.matmul(PS[PB][:, 0:nch], lhsT=ones[:], rhs=cols[:, 0, 0:nch], start=True, stop=True), r=["ones", ("cols", 0)], w=[f"ps{PB}"])
    P.act(lambda e: e.copy(out=cols[:, 3, 0:nch], in_=PS[PB][:, 0:nch]), r=[f"ps{PB}"], w=[("cols", 3)])
    P.act(lambda e: e.activation(out=cols[:, 4, 0:nch], in_=cols[:, 3, 0:nch], func=AF.Exp), r=[("cols", 3)], w=[("cols", 4)])
    P.act(lambda e: e.activation(out=cols[:, 5, 0:nch], in_=cols[:, 2, 0:nch], func=AF.Exp), r=[("cols", 2)], w=[("cols", 5)])
    P.dve(lambda e: e.tensor_tensor(out=cols[:, 5, 0:nch], in0=cols[:, 5, 0:nch], in1=cols[:, 1, 0:nch], op=ALU.mult), r=[("cols", 5), ("cols", 1)], w=[("cols", 5)])
    P.dve(lambda e: e.tensor_tensor(out=cols[:, 6, 0:nch], in0=cols[:, 3, 0:nch], in1=cols[:, 2, 0:nch], op=ALU.subtract), r=[("cols", 3), ("cols", 2)], w=[("cols", 6)])
    P.act(lambda e: e.activation(out=cols[:, 6, 0:nch], in_=cols[:, 6, 0:nch], func=AF.Exp), r=[("cols", 6)], w=[("cols", 6)])
    P.dve(lambda e: e.tensor_scalar(out=cols[:, 7, 0:nch], in0=cols[:, 2, 0:nch], scalar1=-1.0, scalar2=None, op0=ALU.mult), r=[("cols", 2)], w=[("cols", 7)])
    P.pe(lambda e: e.transpose(out=PS[PY][0:nch, 0:128], in_=cols[:, 2, 0:nch], identity=identf[:]), r=[("cols", 2), "identf"], w=[f"ps{PY}"])
    P.act(lambda e: e.copy(out=gcc[0:nch, :], in_=PS[PY][0:nch, 0:128]), r=[f"ps{PY}"], w=["gcc"])
    P.dma(gscr[0].rearrange("(n i) -> n i", i=128), gcc[0:nch, :], r=["gcc"], w=[("gscr", 0)])
    GR = X[:, 3:3 + T]
    BR = U[:, 0:T]
    for c in range(NP):
        P.dma(X[:, 3 + c * PW:3 + (c + 1) * PW], gscr[0:1, c * PW:(c + 1) * PW].partition_broadcast(128), r=[("gscr", 0)], w=[("X", c)])
        P.dma(U[:, c * PW:(c + 1) * PW], gscr[1:2, c * PW:(c + 1) * PW].partition_broadcast(128), r=[("gscr", 1)], w=[("U", c)])
    npc = PW // 128
    for c in range(NP):
        lo = c * PW
        P.act(lambda e, lo=lo: e.activation(out=T1[:, 0:PW], in_=X[:, 3 + lo:3 + lo + PW], func=AF.Exp), r=[("X", c)], w=["T1"])
        P.dve(lambda e, lo=lo: e.tensor_tensor(out=qdT[:, lo:lo + PW], in0=qT[:, lo:lo + PW], in1=T1[:, 0:PW], op=ALU.mult), r=[("qT", c), "T1"], w=[("qdT", c)])
        P.dve(lambda e, lo=lo: e.tensor_tensor(out=X[:, 3 + lo:3 + lo + PW].rearrange("p (n i) -> p n i", i=128), in0=X[:, 3 + lo:3 + lo + PW].rearrange("p (n i) -> p n i", i=128),
                                                in1=dmask[:].unsqueeze(1).to_broadcast([128, npc, 128]), op=ALU.add), r=[("X", c), "dmask"], w=[("X", c)])
        P.dve(lambda e, lo=lo: e.tensor_tensor(out=U[:, lo:lo + PW].rearrange("p (n i) -> p n i", i=128), in0=U[:, lo:lo + PW].rearrange("p (n i) -> p n i", i=128),
                                                in1=nstrict[:].unsqueeze(1).to_broadcast([128, npc, 128]), op=ALU.mult), r=[("U", c), "nstrict"], w=[("U", c)])

    def colb(ci, n0):
        return cols[:, ci, n0:n0 + GS].unsqueeze(2).to_broadcast([128, GS, 128])

    def precompute(g):
        n0 = g * GS
        pb = g % 2
        pc = (n0 * 128) // PW
        flat = lambda ap: ap.rearrange("p g i -> p (g i)")
        for q in range(GS):
            n = n0 + q
            P.pe(lambda e, q=q, n=n: e.transpose(out=PTb[:, q, :], in_=kT[:, n * 128:(n + 1) * 128], identity=ident[:]), r=[("kT", pc), "ident"], w=[("ptb", q)])
        P.dve(lambda e: e.tensor_tensor(out=bek_g[pb][:], in0=PTb[:], in1=colb(5, n0), op=ALU.mult), r=["ptb", ("cols", 5)], w=[f"bek{pb}"])
        P.dve(lambda e: e.tensor_tensor(out=kdec_g[pb][:], in0=PTb[:], in1=colb(6, n0), op=ALU.mult), r=["ptb", ("cols", 6)], w=[f"kdec{pb}"])
        for q in range(GS):
            n = n0 + q
            P.pe(lambda e, q=q, n=n: e.transpose(out=PTb[:, q, :], in_=vT[:, n * 128:(n + 1) * 128], identity=ident[:]), r=[("vT", pc), "ident"], w=[("ptb", q)])
        P.dve(lambda e: e.tensor_tensor(out=bv_g[pb][:], in0=PTb[:], in1=colb(1, n0), op=ALU.mult), r=["ptb", ("cols", 1)], w=[f"bv{pb}"])
        for q in range(GS):
            n = n0 + q
            P.pe(lambda e, q=q, n=n: e.matmul(PS[PA][:, q * 128:(q + 1) * 128], lhsT=kT[:, n * 128:(n + 1) * 128], rhs=kT[:, n * 128:(n + 1) * 128], start=True, stop=True), r=[("kT", pc)], w=[(f"ps{PA}", q)])
            P.pe(lambda e, q=q, n=n: e.matmul(PS[PB][:, q * 128:(q + 1) * 128], lhsT=kT[:, n * 128:(n + 1) * 128], rhs=qT[:, n * 128:(n + 1) * 128], start=True, stop=True), r=[("kT", pc), ("qT", pc)], w=[(f"ps{PB}", q)])
            P.act(lambda e, q=q, n=n: e.activation(out=dc_g[pb][:, q, :], in_=X[:, 3 + n * 128:3 + (n + 1) * 128], func=AF.Exp, bias=cols[:, 7, n:n + 1], scale=1.0), r=[("X", pc), ("cols", 7)], w=[(f"dc{pb}", q)])
        P.dve(lambda e: e.tensor_tensor(out=flat(attn_g[pb][:]), in0=PS[PB][:, :], in1=flat(dc_g[pb][:]), op=ALU.mult), r=[f"ps{PB}", f"dc{pb}"], w=[f"attn{pb}"])
        P.dve(lambda e: e.tensor_tensor(out=flat(t_g[pb][:]), in0=PS[PA][:, :], in1=flat(dc_g[pb][:]), op=ALU.mult), r=[f"ps{PA}", f"dc{pb}"], w=[f"tt{pb}"])
        P.dve(lambda e: e.tensor_tensor(out=flat(Qb[0][:]), in0=flat(t_g[pb][:]), in1=U[:, n0 * 128:(n0 + GS) * 128], op=ALU.mult), r=[f"tt{pb}", ("U", pc)], w=["Q0"])
        for q in range(GS):
            P.pe(lambda e, q=q: e.transpose(out=PTb[:, q, :], in_=Qb[0][:, q, :], identity=ident[:]), r=["Q0", "ident"], w=[("ptb", q)])
        P.act(lambda e: e.copy(out=Rb[0][:], in_=PTb[:]), r=["ptb"], w=["R0"])
        P.dve(lambda e: e.tensor_tensor(out=Yb[:], in0=Qb[0][:], in1=ident[:].unsqueeze(1).to_broadcast([128, GS, 128]), op=ALU.add), r=["Q0", "ident"], w=["Y"])
        cur = 0
        for lvl in range(1, 7):
            nx = 1 - cur
            if lvl <= 5:
                for q in range(GS):
                    P.pe(lambda e, q=q, cur=cur: e.matmul(PS[PQ][:, q * 128:(q + 1) * 128], lhsT=Rb[cur][:, q, :], rhs=Qb[cur][:, q, :], start=True, stop=True), r=[f"R{cur}", f"Q{cur}"], w=[(f"ps{PQ}", q)])
            for q in range(GS):
                P.pe(lambda e, q=q, cur=cur: e.matmul(PS[PR][:, q * 128:(q + 1) * 128], lhsT=Qb[cur][:, q, :], rhs=Rb[cur][:, q, :], start=True, stop=True), r=[f"R{cur}", f"Q{cur}"], w=[(f"ps{PR}", q)])
            if lvl <= 5:
                P.act(lambda e, nx=nx: e.copy(out=flat(Qb[nx][:]), in_=PS[PQ][:, :]), r=[f"ps{PQ}"], w=[f"Q{nx}"])
            P.act(lambda e, nx=nx: e.copy(out=flat(Rb[nx][:]), in_=PS[PR][:, :]), r=[f"ps{PR}"], w=[f"R{nx}"])
            for q in range(GS):
                P.pe(lambda e, q=q, nx=nx: e.matmul(PS[PY][:, q * 128:(q + 1) * 128], lhsT=Rb[nx][:, q, :], rhs=Yb[:, q, :], start=True, stop=True), r=[f"R{nx}", "Y"], w=[(f"ps{PY}", q)])
            P.dve(lambda e: e.tensor_tensor(out=flat(Yb[:]), in0=PS[PY][:, :], in1=flat(Yb[:]), op=ALU.add), r=[f"ps{PY}", "Y"], w=["Y"])
            cur = nx
        for q in range(GS):
            P.pe(lambda e, q=q: e.matmul(PS[PQ][:, q * 128:(q + 1) * 128], lhsT=Yb[:, q, :], rhs=bv_g[pb][:, q, :], start=True, stop=True), r=["Y", f"bv{pb}"], w=[(f"ps{PQ}", q)])
            P.pe(lambda e, q=q: e.matmul(PS[PR][:, q * 128:(q + 1) * 128], lhsT=bek_g[pb][:, q, :], rhs=Yb[:, q, :], start=True, stop=True), r=["Y", f"bek{pb}"], w=[(f"ps{PR}", q)])
        P.act(lambda e: e.copy(out=flat(u_g[pb][:]), in_=PS[PQ][:, :]), r=[f"ps{PQ}"], w=[f"u{pb}"])
        P.act(lambda e: e.copy(out=flat(wT_g[pb][:]), in_=PS[PR][:, :]), r=[f"ps{PR}"], w=[f"wT{pb}"])
        P.dma(zg[pb][:], z_d[:, n0:n0 + GS, :], w=[f"zg{pb}"])
        P.act(lambda e: e.activation(out=zg[pb][:], in_=zg[pb][:], func=AF.Silu), r=[f"zg{pb}"], w=[f"zg{pb}"])
        P.dve(lambda e: e.tensor_tensor(out=gw[pb][:], in0=zg[pb][:], in1=nwr[:].unsqueeze(1).to_broadcast([128, GS, 128]), op=ALU.mult), r=[f"zg{pb}", "nwr"], w=[f"gw{pb}"])

    def scan(g):
        n0 = g * GS
        pb = g % 2
        pc = (n0 * 128) // PW
        for q in range(GS):
            n = n0 + q
            P.pe(lambda e, q=q: e.matmul(PS[PSC_A][:, 0:128], lhsT=wT_g[pb][:, q, :], rhs=Sb[:], start=True, stop=True), r=[f"wT{pb}", "Sb"], w=[(f"ps{PSC_A}", 0)])
            P.pe(lambda e, n=n: e.matmul(PS[PSC_B][:, 0:128], lhsT=qdT[:, n * 128:(n + 1) * 128], rhs=Sb[:], start=True, stop=False), r=[("qdT", pc), "Sb"], w=[f"ps{PSC_B}"])
            P.dve(lambda e, q=q: e.tensor_tensor(out=vnew[:], in0=u_g[pb][:, q, :], in1=PS[PSC_A][:, 0:128], op=ALU.subtract), r=[f"u{pb}", (f"ps{PSC_A}", 0)], w=["vnew"])
            P.pe(lambda e, q=q: e.matmul(PS[PSC_B][:, 0:128], lhsT=attn_g[pb][:, q, :], rhs=vnew[:], start=False, stop=True), r=[f"attn{pb}", "vnew"], w=[f"ps{PSC_B}"])
            P.pe(lambda e, q=q: e.matmul(PS[PSC_A][:, 128:256], lhsT=kdec_g[pb][:, q, :], rhs=vnew[:], start=True, stop=True), r=[f"kdec{pb}", "vnew"], w=[(f"ps{PSC_A}", 1)])
            P.dve(lambda e, n=n: e.scalar_tensor_tensor(out=Sf[:], in0=Sf[:], scalar=cols[:, 4, n:n + 1], in1=PS[PSC_A][:, 128:256], op0=ALU.mult, op1=ALU.add), r=["Sf", ("cols", 4), (f"ps{PSC_A}", 1)], w=["Sf"])
            P.act(lambda e: e.copy(out=Sb[:], in_=Sf[:]), r=["Sf"], w=["Sb"])
            P.act(lambda e: e.activation(out=junk[:], in_=PS[PSC_B][:, 0:128], func=AF.Square, accum_out=ssq[:, 0:1]), r=[f"ps{PSC_B}"], w=["junk", ("ssq", 0)])
            P.act(lambda e: e.activation(out=ssq[:, 1:2], in_=ssq[:, 0:1], func=AF.Sqrt, bias=epsb[:, 0:1], scale=1.0 / 128), r=[("ssq", 0), "epsb"], w=[("ssq", 1)])
            P.dve(lambda e: e.reciprocal(out=ssq[:, 1:2], in_=ssq[:, 1:2]), r=[("ssq", 1)], w=[("ssq", 1)])
            P.dve(lambda e, q=q: e.scalar_tensor_tensor(out=og[pb][:, q, :], in0=PS[PSC_B][:, 0:128], scalar=ssq[:, 1:2], in1=gw[pb][:, q, :], op0=ALU.mult, op1=ALU.mult),
                  r=[f"ps{PSC_B}", ("ssq", 1), f"gw{pb}"], w=[(f"og{pb}", q)])
        P.dma(o_o[:, n0:n0 + GS, :], og[pb][:], r=[f"og{pb}"])

    precompute(0)
    for g in range(NG):
        if g + 1 < NG:
            precompute(g + 1)
        scan(g)
    P.emit()
    return nc
import ml_dtypes

W_NAMES = ("w_in", "ffn_up", "ffn_down", "w_out", "w_ukv")


def build_W(cols):
    nc = bass.Bass("TRN2", target_bir_lowering=False)
    P = Prog(nc)
    CH = 4096
    stg = [nc.alloc_sbuf_tensor(f"s_stg{i}", [128, CH], F32) for i in range(3)]
    ob = [nc.alloc_sbuf_tensor(f"s_ob{i}", [128, CH], BF16) for i in range(3)]
    k = 0
    for i, n in enumerate(cols):
        src = nc.dram_tensor(f"w{i}", [128, n], F32, kind="ExternalInput").ap()
        dst = nc.dram_tensor(f"o{i}", [128, n], BF16, kind="ExternalOutput").ap()
        for c0 in range(0, n, CH):
            w = min(CH, n - c0)
            b = k % 3
            P.dma(stg[b][:, 0:w], src[:, c0:c0 + w], w=[f"stg{b}"])
            if k % 2 == 0:
                P.act(lambda e, b=b, w=w: e.copy(out=ob[b][:, 0:w], in_=stg[b][:, 0:w]), r=[f"stg{b}"], w=[f"ob{b}"])
            else:
                P.dve(lambda e, b=b, w=w: e.tensor_copy(out=ob[b][:, 0:w], in_=stg[b][:, 0:w]), r=[f"stg{b}"], w=[f"ob{b}"])
            P.dma(dst[:, c0:c0 + w], ob[b][:, 0:w], r=[f"ob{b}"])
            k += 1
    P.emit()
    return nc


def _rope_tab(pos, dim):
    inv = (10000.0 ** (-np.arange(0, dim, 2, dtype=np.float32) / dim)).astype(np.float32)
    ang = pos[:, None].astype(np.float32) * inv[None, :]
    ang = np.concatenate([ang, ang], -1)
    return np.cos(ang).astype(np.float32), np.sin(ang).astype(np.float32)


def _rmat(dim, reps):
    R = np.zeros((128, 128), np.float32)
    h = dim // 2
    for b in range(reps):
        for m in range(dim):
            if m < h:
                R[b * dim + m + h, b * dim + m] = -1.0
            else:
                R[b * dim + m - h, b * dim + m] = 1.0
    return R


def _fm(v):
    return np.ascontiguousarray(np.asarray(v, np.float32).reshape(-1, 128).T)


_PROGS = {}


def _prog(key, fn):
    if key not in _PROGS:
        _PROGS[key] = fn()
    return _PROGS[key]


def _run(nc, in_maps):
    res = run_bass_kernel_spmd(nc, in_maps, core_ids=list(range(NCORE)))
    return res.results


def kernel(**inp):
    bf = ml_dtypes.bfloat16
    inp = {k: np.asarray(v) for k, v in inp.items()}
    x = inp["x"][0]
    L = DEPTH
    slices = []
    for c in range(NCORE):
        d = {}
        for i, nm in enumerate(W_NAMES):
            W = inp[nm]
            Rc = W.shape[1] // NCORE
            d[f"w{i}"] = np.ascontiguousarray(W[:, c * Rc:(c + 1) * Rc, :]).reshape(128, -1)
        slices.append(d)
    cols = [slices[0][f"w{i}"].shape[1] for i in range(len(W_NAMES))]
    ncw = _prog("W", lambda: build_W(cols))
    resw = _run(ncw, slices)
    wbf = {}
    for i, nm in enumerate(W_NAMES):
        W = inp[nm]
        Rc = W.shape[1] // NCORE
        parts = [np.asarray(resw[c][f"o{i}"]).reshape(L, Rc, W.shape[2]) for c in range(NCORE)]
        wbf[nm] = np.concatenate(parts, axis=1)
    del slices, resw

    rmat = np.stack([_rmat(128, 1), _rmat(64, 2)])
    ropes = {}
    def rope_for(t0):
        if t0 not in ropes:
            pos = t0 + np.arange(TS)
            c128, s128 = _rope_tab(pos, 128)
            c64, s64 = _rope_tab(pos, 64)
            ropes[t0] = np.stack([c128.T, s128.T, np.concatenate([c64.T, c64.T], 0), np.concatenate([s64.T, s64.T], 0)]).astype(np.float32)
        return ropes[t0]

    def run_A(layer_post, layer_proj, xT_full, mixT_full):
        do_post = layer_post is not None
        do_proj = layer_proj is not None
        nc = _prog(("A", do_post, do_proj), lambda: build_A(do_post, do_proj))
        maps = []
        for c in range(NCORE):
            d = {}
            t0s = [c * NSH * TS + sh * TS for sh in range(NSH)]
            if do_post:
                xs = []; ms = []
                for t0 in t0s:
                    if t0 == 0:
                        xs.append(np.concatenate([np.zeros((D, HALO), np.float32), xT_full[:, 0:TS]], 1))
                        ms.append(np.concatenate([np.zeros((D, HALO), bf), mixT_full[:, 0:TS]], 1))
                    else:
                        xs.append(xT_full[:, t0 - HALO:t0 + TS])
                        ms.append(mixT_full[:, t0 - HALO:t0 + TS])
                d["xT"] = np.ascontiguousarray(np.stack(xs))
                d["mixT"] = np.ascontiguousarray(np.stack(ms))
                i = layer_post
                d["w_out"] = wbf["w_out"][i]; d["ffn_up"] = wbf["ffn_up"][i]; d["ffn_down"] = wbf["ffn_down"][i]
                d["lnp"] = np.ascontiguousarray(np.stack([_fm(inp["ln1_g"][i]), _fm(inp["ln1_b"][i]), _fm(inp["ln2_g"][i]), _fm(inp["ln2_b"][i])], 1))
                cwa = np.concatenate([inp["ffn_conv_w"][i], inp["ffn_conv_b"][i][None]], 0).astype(np.float32)
                d["convw"] = np.ascontiguousarray(cwa.T.reshape(88, 128, 4).transpose(1, 0, 2))
                hf = np.ones((128, NSH), np.float32)
                if c == 0:
                    hf[:, 0] = 0.0
                d["haloflag"] = hf
            else:
                d["xT"] = np.ascontiguousarray(np.stack([xT_full[:, t0:t0 + TS] for t0 in t0s]))
            if do_proj:
                i = layer_proj
                d["w_in"] = wbf["w_in"][i]; d["w_ukv"] = wbf["w_ukv"][i]
                d["kvnw"] = _fm(inp["kv_norm_w"][i])
                d["idxgb"] = np.ascontiguousarray(np.stack([inp["idx_k_norm_g"][i], inp["idx_k_norm_b"][i]], 1).astype(np.float32))
                d["rope"] = np.ascontiguousarray(np.stack([rope_for(t0) for t0 in t0s]))
                d["rmat"] = rmat
            maps.append(d)
        res = _run(nc, maps)
        x2T = None; pT = None
        if do_post:
            x2T = np.concatenate([np.asarray(res[c]["x2T"][sh]) for c in range(NCORE) for sh in range(NSH)], axis=1)
        if do_proj:
            pT = np.concatenate([np.asarray(res[c]["pT"][sh]) for c in range(NCORE) for sh in range(NSH)], axis=1)
        return x2T, pT

    def run_I(pT):
        nc = _prog("I", lambda: build_I())
        kiT2 = np.ascontiguousarray(np.concatenate([pT[R_IK:R_IK + 64], pT[R_IK:R_IK + 64]], 0))
        maps = []
        for c in range(NCORE):
            qbs = slot_qbs(c)
            d = {"kiT2": kiT2}
            d["iqT"] = np.ascontiguousarray(np.stack([pT[R_IQ:R_IQ + 1024, q * 128:(q + 1) * 128] for q in qbs]))
            d["iw"] = np.ascontiguousarray(np.stack([pT[R_IW:R_IW + 16, q * 128:(q + 1) * 128].T for q in qbs], 1))
            d["qpos"] = np.ascontiguousarray(np.stack([np.arange(q * 128, (q + 1) * 128) for q in qbs], 1).astype(np.float32))
            maps.append(d)
        res = _run(nc, maps)
        mask = np.zeros((128, NBLK, 128), bf)
        for c in range(NCORE):
            for j, q in enumerate(slot_qbs(c)):
                mask[:, blk_off(q):blk_off(q) + q + 1, :] = np.asarray(res[c][f"m{j}"])[:, 0:q + 1, :]
        return mask

    def tokmajor(a):
        return np.ascontiguousarray(a.T.reshape(64, 128, 128).transpose(1, 0, 2))

    def run_T(pT, mask):
        nc = _prog("T", lambda: build_T())
        maps = []
        for c in range(NCORE):
            maps.append({"qT": np.ascontiguousarray(pT[R_AQ + c * 128:R_AQ + (c + 1) * 128]),
                         "kT": np.ascontiguousarray(pT[R_K + c * 128:R_K + (c + 1) * 128]),
                         "v": tokmajor(pT[R_V + c * 128:R_V + (c + 1) * 128]),
                         "mask": mask})
        res = _run(nc, maps)
        return np.concatenate([np.asarray(res[c]["o"]).transpose(1, 0, 2).reshape(S, 128) for c in range(NCORE)], axis=1)

    def run_G(pT, i):
        nc = _prog("G", lambda: build_G())
        maps = []
        gcw = inp["gdn_conv_w"][i].astype(np.float32)
        for c in range(NCORE):
            d = {}
            d["qkvT"] = np.ascontiguousarray(np.stack([pT[ti * 1024 + c * 128:ti * 1024 + (c + 1) * 128] for ti in range(3)]))
            d["cw"] = np.ascontiguousarray(np.stack([gcw[:, ti * 1024 + c * 128:ti * 1024 + (c + 1) * 128].T for ti in range(3)], 1))
            d["z"] = tokmajor(pT[3072 + c * 128:3072 + (c + 1) * 128])
            d["ab"] = np.ascontiguousarray(np.stack([pT[R_GAB + c].reshape(64, 128), pT[R_GAB + 8 + c].reshape(64, 128)]))
            d["hp"] = np.ascontiguousarray(np.tile(np.array([[inp["gdn_a_log"][i][c], inp["gdn_dt_bias"][i][c]]], np.float32), (128, 1)))
            d["normw"] = np.ascontiguousarray(inp["gdn_norm_w"][i][None, :].astype(np.float32))
            maps.append(d)
        res = _run(nc, maps)
        return np.concatenate([np.asarray(res[c]["o"]).transpose(1, 0, 2).reshape(S, 128) for c in range(NCORE)], axis=1)

    xT_full = np.ascontiguousarray(x.T.astype(np.float32))
    _, pT = run_A(None, 0, xT_full, None)
    for i in range(L):
        mask = run_I(pT)
        o_att = run_T(pT, mask)
        o_gdn = run_G(pT, i)
        mixT = np.ascontiguousarray(np.concatenate([o_gdn, o_att], axis=1).T)
        xT_full, pT = run_A(i, i + 1 if i + 1 < L else None, xT_full, mixT)
    return np.ascontiguousarray(xT_full.T)[None].astype(np.float32)
```

```python
import numpy as np
import concourse.bass as bass
import concourse.mybir as mybir
from concourse.bass_utils import run_bass_kernel_spmd

F32 = mybir.dt.float32
BF16 = mybir.dt.bfloat16
ALU = mybir.AluOpType
AF = mybir.ActivationFunctionType
AX = mybir.AxisListType

SEM_ROT = 30000


class Prog:
    ENGS = ("pe", "act", "dve", "pool", "sp")

    def __init__(self, nc, n_dma_sems=16):
        self.nc = nc
        self.recs = []
        self.state = {}
        self.n_dma_sems = n_dma_sems

    @staticmethod
    def _split(key):
        if isinstance(key, tuple):
            return key[0], key[1:]
        return key, None

    def _conflicts(self, key):
        base, sub = self._split(key)
        d = self.state.get(base)
        if not d:
            return []
        if sub is None:
            return list(d.values())
        out = []
        if sub in d:
            out.append(d[sub])
        if None in d:
            out.append(d[None])
        return out

    def op(self, eng, fn, r=(), w=(), dma=False):
        oid = len(self.recs)
        deps = set()
        for k in r:
            for st in self._conflicts(k):
                if st[0] is not None:
                    deps.add(st[0])
        for k in w:
            for st in self._conflicts(k):
                if st[0] is not None:
                    deps.add(st[0])
                deps.update(st[1])
        deps.discard(oid)
        if eng == "pe":
            deps = {d for d in deps if self.recs[d]["eng"] != "pe"}
        self.recs.append(dict(id=oid, eng=eng, fn=fn, deps=sorted(deps), dma=dma, signal=dma))
        for k in r:
            base, sub = self._split(k)
            st = self.state.setdefault(base, {}).setdefault(sub, [None, []])
            st[1].append(oid)
        for k in w:
            base, sub = self._split(k)
            d = self.state.setdefault(base, {})
            if sub is None:
                d.clear()
            d[sub] = [oid, []]
        return oid

    def pe(self, fn, r=(), w=()):
        return self.op("pe", fn, r, w)

    def act(self, fn, r=(), w=()):
        return self.op("act", fn, r, w)

    def dve(self, fn, r=(), w=()):
        return self.op("dve", fn, r, w)

    def pool(self, fn, r=(), w=()):
        return self.op("pool", fn, r, w)

    def dma(self, out, in_, r=(), w=(), eng="sp", **kw):
        return self.op(eng, lambda e: e.dma_start(out=out, in_=in_, **kw), r, w, dma=True)

    def emit(self):
        nc = self.nc
        recs = self.recs
        for rec in recs:
            for d in rec["deps"]:
                recs[d]["signal"] = True
        cnt = {e: 0 for e in self.ENGS}
        sems = {}

        def get_sem(name):
            if name not in sems:
                sems[name] = nc.alloc_semaphore(name)
            return sems[name]

        dma_tot = [0] * self.n_dma_sems
        dma_rr = 0
        per_eng = {e: [] for e in self.ENGS}
        for rec in recs:
            e = rec["eng"]
            per_eng[e].append(rec)
            if rec["dma"]:
                i = dma_rr % self.n_dma_sems
                dma_rr += 1
                rec["prev_ev"] = (f"dq{i}", dma_tot[i]) if dma_tot[i] > 0 else None
                dma_tot[i] += 16
                rec["ev"] = (f"dq{i}", dma_tot[i])
                rec["inc"] = 16
            elif rec["signal"]:
                c = cnt[e]
                cnt[e] += 1
                rec["ev"] = (f"c_{e}_{c // SEM_ROT}", c % SEM_ROT + 1)
                rec["inc"] = 1
            else:
                rec["ev"] = None
        final_dma = [(f"dq{i}", dma_tot[i]) for i in range(self.n_dma_sems) if dma_tot[i] > 0]
        for name, _ in final_dma:
            get_sem(name)
        for rec in recs:
            if rec["ev"] is not None:
                get_sem(rec["ev"][0])

        def run_engine(ename, eng_obj, extra_final=False):
            waited = {}
            for rec in per_eng[ename]:
                evs = [recs[d]["ev"] for d in rec["deps"]]
                if rec["dma"] and rec["prev_ev"] is not None:
                    evs.append(rec["prev_ev"])
                for (sn, v) in evs:
                    if waited.get(sn, 0) < v:
                        eng_obj.wait_ge(sems[sn], v)
                        waited[sn] = v
                ins = rec["fn"](eng_obj)
                if rec["ev"] is not None:
                    ins.then_inc(sems[rec["ev"][0]], rec["inc"])
            if extra_final:
                for (sn, v) in final_dma:
                    if waited.get(sn, 0) < v:
                        eng_obj.wait_ge(sems[sn], v)
                        waited[sn] = v

        with nc.Block() as block:
            @block.tensor
            def _(eng):
                run_engine("pe", eng)

            @block.scalar
            def _(eng):
                run_engine("act", eng)

            @block.vector
            def _(eng):
                run_engine("dve", eng)

            @block.gpsimd
            def _(eng):
                run_engine("pool", eng)

            @block.sync
            def _(eng):
                run_engine("sp", eng, extra_final=True)
        return nc
import ml_dtypes

D = 2048; DFF = 5632; S = 8192; TS = 512; HALO = 2; TT = TS + HALO; NSH = 2; NCORE = 8
LN_EPS = 1e-5; RMS_EPS = 1e-6
DEPTH = 4
DN_ALPHA = (2 * DEPTH) ** 0.25
KC = D // 128
R_GAB = 4096; R_AQ = 4112; R_IQ = 5136; R_IK = 6160; R_IW = 6224; R_K = 6240; R_V = 7264; R_TOT = 8288


class Rot:
    def __init__(self, items):
        self.items = items; self.i = 0
    def next(self):
        it = self.items[self.i % len(self.items)]; self.i += 1
        return it


def build_A(do_post, do_proj):
    nc = bass.Bass("TRN2", target_bir_lowering=False)
    P = Prog(nc)
    def din(name, shape, dt=F32):
        return nc.dram_tensor(name, shape, dt, kind="ExternalInput").ap()
    def dout(name, shape, dt=F32):
        return nc.dram_tensor(name, shape, dt, kind="ExternalOutput").ap()
    def sb(name, shape, dt):
        return nc.alloc_sbuf_tensor("s_" + name, shape, dt)

    T_in = TT if do_post else TS
    xT_d = din("xT", [NSH, D, T_in])
    if do_post:
        mixT_d = din("mixT", [NSH, D, TT], BF16)
        wout_d = din("w_out", [D, D], BF16)
        wup_d = din("ffn_up", [D, 2 * DFF], BF16)
        wdn_d = din("ffn_down", [DFF, D], BF16)
        lnp_d = din("lnp", [128, 4, KC])
        cw_d = din("convw", [128, 88, 4])
        hf_d = din("haloflag", [128, NSH])
        x2T_o = dout("x2T", [NSH, D, TS])
    if do_proj:
        win_d = din("w_in", [D, 6496], BF16)
        wukv_d = din("w_ukv", [256, 2048], BF16)
        kvnw_d = din("kvnw", [128, 2])
        idxgb_d = din("idxgb", [64, 2])
        rope_d = din("rope", [NSH, 4, 128, TS])
        rmat_d = din("rmat", [2, 128, 128])
        pT_o = dout("pT", [NSH, R_TOT, TS])

    xT = sb("xT", [128, KC, TT], F32)
    xb = sb("xb", [128, KC, TT], BF16)
    wp = [sb(f"wp{i}", [128, 8192], BF16) for i in range(2)]
    ones = sb("ones", [128, 128], F32)
    st = [sb(f"st{i}", [128, 512], F32) for i in range(6)]
    strot = Rot(list(range(6)))
    PS = [nc.alloc_psum_tensor(f"ps{i}", [128, 512], F32) for i in range(8)]
    mmrot = Rot([0, 1, 2, 3])
    auxrot = Rot([4, 5])
    P.pool(lambda e: e.memset(ones[:], 1.0), w=["ones"])
    epsb = sb("epsb", [128, 2], F32)
    P.pool(lambda e: e.memset(epsb[:, 0:1], LN_EPS), w=[("epsb", 0)])
    P.pool(lambda e: e.memset(epsb[:, 1:2], RMS_EPS), w=[("epsb", 1)])
    if do_post:
        aT = sb("aT", [128, DFF // 128, TS], BF16)
        lnp = sb("lnp", [128, 4, KC], F32)
        cw = sb("cw", [128, 88, 4], F32)
        hf = sb("hf", [128, NSH], F32)
        hfull = [sb(f"hfull{i}", [128, TT], F32) for i in range(6)]
        uu = [sb(f"uu{i}", [128, TS], F32) for i in range(6)]
        P.dma(lnp[:], lnp_d, w=["lnp"])
        P.dma(cw[:], cw_d, w=["cw"])
        P.dma(hf[:], hf_d, w=["hf"])
        stat = [sb(f"stat{i}", [128, 512], F32) for i in range(4)]
    if do_proj:
        kvnw = sb("kvnw", [128, 2], F32)
        idxgb = sb("idxgb", [64, 2], F32)
        rope = sb("rope", [128, 4, TS], F32)
        rmat = sb("rmat", [128, 2, 128], F32)
        ckv = sb("ckv", [128, 2, TS], F32)
        ckvn = sb("ckvn", [128, 2, TS], BF16)
        wukv = sb("wukv", [128, 2, 2048], BF16)
        P.dma(kvnw[:], kvnw_d, w=["kvnw"])
        P.dma(idxgb[:], idxgb_d, w=["idxgb"])
        P.dma(rmat[:], rmat_d.rearrange("r p m -> p r m"), w=["rmat"])
        P.dma(wukv[:], wukv_d.rearrange("(c p) n -> p c n", p=128), w=["wukv"])
        if not do_post:
            stat = [sb(f"stat{i}", [128, 512], F32) for i in range(4)]

    wp_i = [0]

    def gemm(W_d, Kc, groups, rhs_fn, chunks, epilogue, pre=None):
        pw = 512 if Kc <= 16 else 128
        panels = []
        cur = []
        for gi, (c0, m) in enumerate(groups):
            if cur and (c0 + m - groups[cur[0]][0] > pw or c0 != groups[cur[-1]][0] + groups[cur[-1]][1]):
                panels.append(cur); cur = []
            cur.append(gi)
        if cur:
            panels.append(cur)
        Wv = W_d.rearrange("(c p) n -> p c n", p=128)

        def load(pi):
            b = wp_i[0] % 2; wp_i[0] += 1
            g0 = groups[panels[pi][0]][0]
            g1 = groups[panels[pi][-1]][0] + groups[panels[pi][-1]][1]
            wdt = g1 - g0
            view = wp[b][:, 0:Kc * wdt].rearrange("p (c n) -> p c n", c=Kc)
            P.dma(view, Wv[:, :, g0:g1], w=[f"wp{b}"])
            return b, g0, wdt

        nxt = load(0)
        for pi, pan in enumerate(panels):
            b, g0, wdt = nxt
            if pi + 1 < len(panels):
                nxt = load(pi + 1)
            view = wp[b][:, 0:Kc * wdt].rearrange("p (c n) -> p c n", c=Kc)
            for gi in pan:
                c0, m = groups[gi]
                for ci, (t0, n) in enumerate(chunks):
                    pidx = mmrot.next()
                    for k in range(Kc):
                        rap, rkeys = rhs_fn(k, t0, n)
                        P.pe(lambda e, pidx=pidx, k=k, rap=rap, c0=c0, m=m, n=n, view=view, g0=g0, Kc=Kc:
                             e.matmul(PS[pidx][0:m, 0:n], lhsT=view[:, k, c0 - g0:c0 - g0 + m], rhs=rap,
                                      start=(k == 0), stop=(k == Kc - 1)),
                             r=[f"wp{b}"] + rkeys, w=[f"ps{pidx}"])
                    epilogue(gi, ci, pidx, m, t0, n)

    def ln_feature_major(shard, which, t0, n):
        gi_, bi_ = (0, 1) if which == 1 else (2, 3)
        p1 = auxrot.next(); p2 = auxrot.next()
        for c in range(KC):
            s = strot.next()
            P.act(lambda e, c=c, s=s: e.activation(out=st[s][:, 0:n], in_=xT[:, c, t0:t0 + n], func=AF.Square),
                  r=[("xT", c)], w=[f"st{s}"])
            P.pe(lambda e, c=c: e.matmul(PS[p1][:, 0:n], lhsT=ones[:], rhs=xT[:, c, t0:t0 + n], start=(c == 0), stop=(c == KC - 1)),
                 r=["ones", ("xT", c)], w=[f"ps{p1}"])
            P.pe(lambda e, c=c, s=s: e.matmul(PS[p2][:, 0:n], lhsT=ones[:], rhs=st[s][:, 0:n], start=(c == 0), stop=(c == KC - 1)),
                 r=["ones", f"st{s}"], w=[f"ps{p2}"])
        mean, ex2, rstd, mr = stat
        P.act(lambda e: e.mul(out=mean[:, 0:n], in_=PS[p1][:, 0:n], mul=1.0 / D), r=[f"ps{p1}"], w=["stat0"])
        P.act(lambda e: e.mul(out=ex2[:, 0:n], in_=PS[p2][:, 0:n], mul=1.0 / D), r=[f"ps{p2}"], w=["stat1"])
        P.dve(lambda e: e.tensor_tensor(out=mr[:, 0:n], in0=mean[:, 0:n], in1=mean[:, 0:n], op=ALU.mult), r=["stat0"], w=["stat3"])
        P.dve(lambda e: e.tensor_tensor(out=ex2[:, 0:n], in0=ex2[:, 0:n], in1=mr[:, 0:n], op=ALU.subtract), r=["stat1", "stat3"], w=["stat1"])
        P.act(lambda e: e.activation(out=rstd[:, 0:n], in_=ex2[:, 0:n], func=AF.Sqrt, bias=epsb[:, 0:1], scale=1.0), r=["stat1", "epsb"], w=["stat2"])
        P.dve(lambda e: e.reciprocal(out=rstd[:, 0:n], in_=rstd[:, 0:n]), r=["stat2"], w=["stat2"])
        P.dve(lambda e: e.tensor_tensor(out=mr[:, 0:n], in0=mean[:, 0:n], in1=rstd[:, 0:n], op=ALU.mult), r=["stat0", "stat2"], w=["stat3"])
        for c in range(KC):
            s = strot.next()
            P.dve(lambda e, c=c, s=s: e.tensor_tensor(out=st[s][:, 0:n], in0=xT[:, c, t0:t0 + n], in1=mean[:, 0:n], op=ALU.subtract),
                   r=[("xT", c), "stat0"], w=[f"st{s}"])
            P.dve(lambda e, c=c, s=s: e.scalar_tensor_tensor(out=st[s][:, 0:n], in0=st[s][:, 0:n], scalar=lnp[:, gi_, c:c + 1], in1=rstd[:, 0:n], op0=ALU.mult, op1=ALU.mult),
                  r=[f"st{s}", "lnp", "stat2"], w=[f"st{s}"])
            P.act(lambda e, c=c, s=s: e.activation(out=xT[:, c, t0:t0 + n], in_=st[s][:, 0:n], func=AF.Identity, bias=lnp[:, bi_, c:c + 1], scale=1.0),
                  r=[f"st{s}", "lnp"], w=[("xT", c)])
            P.act(lambda e, c=c, s=s: e.activation(out=xb[:, c, t0:t0 + n], in_=st[s][:, 0:n], func=AF.Identity, bias=lnp[:, bi_, c:c + 1], scale=1.0),
                  r=[f"st{s}", "lnp"], w=[("xb", c)])

    def out_rows(shard, row0, m, src_ap, keys):
        P.dma(pT_o[shard, row0:row0 + m, :], src_ap, r=keys)

    for sh in range(NSH):
        for c4 in range(4):
            P.dma(xT[:, 4 * c4:4 * c4 + 4, 0:T_in], xT_d[sh, 512 * c4:512 * (c4 + 1), :].rearrange("(c p) t -> p c t", p=128),
                  w=[("xT", 4 * c4 + j) for j in range(4)])
        if do_post:
            for c4 in range(4):
                P.dma(xb[:, 4 * c4:4 * c4 + 4, :], mixT_d[sh, 512 * c4:512 * (c4 + 1), :].rearrange("(c p) t -> p c t", p=128),
                      w=[("xb", 4 * c4 + j) for j in range(4)])
            chunks_h = [(0, HALO), (HALO, TS)]
            def ep2(gi, ci, pidx, m, t0, n):
                P.dve(lambda e: e.scalar_tensor_tensor(out=xT[:, gi, t0:t0 + n], in0=xT[:, gi, t0:t0 + n], scalar=DN_ALPHA, in1=PS[pidx][:, 0:n], op0=ALU.mult, op1=ALU.add),
                      r=[("xT", gi), f"ps{pidx}"], w=[("xT", gi)])
            gemm(wout_d, KC, [(g * 128, 128) for g in range(KC)], lambda k, t0, n: (xb[:, k, t0:t0 + n], [("xb", k)]), chunks_h, ep2)
            ln_feature_major(sh, 1, 0, HALO)
            ln_feature_major(sh, 1, HALO, TS)
            NG = DFF // 128
            groups3 = []
            for j in range(NG):
                groups3.append((j * 128, 128))
            for j in range(NG):
                groups3.append((DFF + j * 128, 128))
            order = []
            for j0 in range(0, NG, 4):
                order += [(j * 128, 128) for j in range(j0, j0 + 4)] + [(DFF + j * 128, 128) for j in range(j0, j0 + 4)]
            hmap = {}
            def ep3(gi, ci, pidx, m, t0, n, order=order, sh=sh):
                c0 = order[gi][0]
                isval = c0 >= DFF
                j = (c0 - DFF) // 128 if isval else c0 // 128
                hb = (4 + j % 2) if isval else (j % 4)
                gb_ = j % 4
                ch = j + (NG if isval else 0)
                if ci == 0:
                    P.act(lambda e: e.activation(out=hfull[hb][:, 0:HALO], in_=PS[pidx][:, 0:HALO], func=AF.Copy, scale=hf[:, sh:sh + 1]),
                          r=[f"ps{pidx}", "hf"], w=[(f"hfull{hb}", 0)])
                    return
                P.act(lambda e: e.copy(out=hfull[hb][:, HALO:TT], in_=PS[pidx][:, 0:TS]), r=[f"ps{pidx}"], w=[(f"hfull{hb}", 1)])
                if not isval:
                    return
                hg = gb_
                ug = uu[hg]; u = uu[hb]
                chg = j
                def tap0(hbuf, ub, ubk, hk, c_):
                    P.dve(lambda e: e.tensor_scalar(out=ub[:], in0=hbuf[:, 2:2 + TS], scalar1=cw[:, c_, 2:3], scalar2=cw[:, c_, 3:4], op0=ALU.mult, op1=ALU.add),
                          r=[hk, "cw"], w=[ubk])
                def tapk(hbuf, ub, ubk, hk, c_, k):
                    P.dve(lambda e: e.scalar_tensor_tensor(out=ub[:], in0=hbuf[:, k:k + TS], scalar=cw[:, c_, k:k + 1], in1=ub[:], op0=ALU.mult, op1=ALU.add),
                          r=[hk, "cw", ubk], w=[ubk])
                tap0(hfull[hg], ug, f"uu{hg}", f"hfull{hg}", chg)
                tap0(hfull[hb], u, f"uu{hb}", f"hfull{hb}", ch)
                for k in (1, 0):
                    tapk(hfull[hg], ug, f"uu{hg}", f"hfull{hg}", chg, k)
                    tapk(hfull[hb], u, f"uu{hb}", f"hfull{hb}", ch, k)
                P.act(lambda e: e.activation(out=ug[:], in_=ug[:], func=AF.Silu), r=[f"uu{hg}"], w=[f"uu{hg}"])
                P.dve(lambda e: e.tensor_tensor(out=aT[:, j, :], in0=ug[:], in1=u[:], op=ALU.mult), r=[f"uu{hg}", f"uu{hb}"], w=[("aT", j)])
            gemm(wup_d, KC, order, lambda k, t0, n: (xb[:, k, t0:t0 + n], [("xb", k)]), chunks_h, ep3)
            def ep4(gi, ci, pidx, m, t0, n):
                P.dve(lambda e: e.scalar_tensor_tensor(out=xT[:, gi, HALO:TT], in0=xT[:, gi, HALO:TT], scalar=DN_ALPHA, in1=PS[pidx][:, 0:TS], op0=ALU.mult, op1=ALU.add),
                      r=[("xT", gi), f"ps{pidx}"], w=[("xT", gi)])
            gemm(wdn_d, DFF // 128, [(g * 128, 128) for g in range(KC)], lambda k, t0, n: (aT[:, k, :], [("aT", k)]), [(0, TS)], ep4)
            ln_feature_major(sh, 2, HALO, TS)
            for c4 in range(4):
                P.dma(x2T_o[sh, 512 * c4:512 * (c4 + 1), :].rearrange("(c p) t -> p c t", p=128), xT[:, 4 * c4:4 * c4 + 4, HALO:TT],
                      r=[("xT", 4 * c4 + j) for j in range(4)])
            xoff = HALO
        else:
            for c in range(KC):
                P.act(lambda e, c=c: e.copy(out=xb[:, c, 0:TS], in_=xT[:, c, 0:TS]), r=[("xT", c)], w=[("xb", c)])
            xoff = 0
        if not do_proj:
            continue
        P.dma(rope[:], rope_d[sh].rearrange("f p t -> p f t"), w=["rope"])
        segs = []
        for g in range(32):
            segs.append((g * 128, 128, "plain", g * 128))
        segs.append((4096, 16, "plain", R_GAB))
        for g in range(2):
            segs.append((5136 + g * 128, 128, "ckv", g))
        for g in range(8):
            segs.append((4112 + g * 128, 128, "rope128", R_AQ + g * 128))
        for g in range(8):
            segs.append((5392 + g * 128, 128, "rope64", R_IQ + g * 128))
        segs.append((6416, 64, "ik", R_IK))
        segs.append((6480, 16, "plain", R_IW))

        def rope_ep(src_sb, skey, m, ridx, tabc, tabs, outrow, sh=sh):
            pa = auxrot.next()
            P.pe(lambda e: e.matmul(PS[pa][0:m, 0:TS], lhsT=rmat[0:m, ridx, 0:m], rhs=src_sb, start=True, stop=True), r=["rmat", skey], w=[f"ps{pa}"])
            s1 = strot.next(); s2 = strot.next()
            P.dve(lambda e: e.tensor_tensor(out=st[s1][0:m, :], in0=src_sb, in1=rope[0:m, tabc, :], op=ALU.mult), r=[skey, "rope"], w=[f"st{s1}"])
            P.dve(lambda e: e.tensor_tensor(out=st[s2][0:m, :], in0=PS[pa][0:m, 0:TS], in1=rope[0:m, tabs, :], op=ALU.mult), r=[f"ps{pa}", "rope"], w=[f"st{s2}"])
            P.dve(lambda e: e.tensor_tensor(out=st[s1][0:m, :], in0=st[s1][0:m, :], in1=st[s2][0:m, :], op=ALU.add), r=[f"st{s1}", f"st{s2}"], w=[f"st{s1}"])
            out_rows(sh, outrow, m, st[s1][0:m, :], [f"st{s1}"])

        def ep1(gi, ci, pidx, m, t0, n, sh=sh):
            c0, m_, kind, orow = segs[gi]
            if kind == "plain":
                s = strot.next()
                P.act(lambda e: e.copy(out=st[s][0:m, :], in_=PS[pidx][0:m, 0:TS]), r=[f"ps{pidx}"], w=[f"st{s}"])
                out_rows(sh, orow, m, st[s][0:m, :], [f"st{s}"])
            elif kind in ("rope128", "rope64"):
                s = strot.next()
                P.act(lambda e: e.copy(out=st[s][0:m, :], in_=PS[pidx][0:m, 0:TS]), r=[f"ps{pidx}"], w=[f"st{s}"])
                if kind == "rope128":
                    rope_ep(st[s][0:m, :], f"st{s}", m, 0, 0, 1, orow)
                else:
                    rope_ep(st[s][0:m, :], f"st{s}", m, 1, 2, 3, orow)
            elif kind == "ckv":
                g = orow
                P.act(lambda e: e.copy(out=ckv[:, g, :], in_=PS[pidx][:, 0:TS]), r=[f"ps{pidx}"], w=[("ckv", g)])
                if g == 1:
                    pa = auxrot.next()
                    for j in range(2):
                        s = strot.next()
                        P.act(lambda e, j=j, s=s: e.activation(out=st[s][:], in_=ckv[:, j, :], func=AF.Square), r=[("ckv", j)], w=[f"st{s}"])
                        P.pe(lambda e, j=j, s=s: e.matmul(PS[pa][:, 0:TS], lhsT=ones[:], rhs=st[s][:], start=(j == 0), stop=(j == 1)), r=["ones", f"st{s}"], w=[f"ps{pa}"])
                    rs = stat[2]
                    P.act(lambda e: e.activation(out=rs[:], in_=PS[pa][:, 0:TS], func=AF.Sqrt, bias=epsb[:, 1:2], scale=1.0 / 256), r=[f"ps{pa}", "epsb"], w=["stat2"])
                    P.dve(lambda e: e.reciprocal(out=rs[:], in_=rs[:]), r=["stat2"], w=["stat2"])
                    for j in range(2):
                        P.dve(lambda e, j=j: e.scalar_tensor_tensor(out=ckvn[:, j, :], in0=ckv[:, j, :], scalar=kvnw[:, j:j + 1], in1=rs[:], op0=ALU.mult, op1=ALU.mult),
                              r=[("ckv", j), "kvnw", "stat2"], w=[("ckvn", j)])
                    for hg in range(16):
                        pidx2 = mmrot.next()
                        for k in range(2):
                            P.pe(lambda e, k=k, hg=hg, pidx2=pidx2: e.matmul(PS[pidx2][:, 0:TS], lhsT=wukv[:, k, hg * 128:(hg + 1) * 128], rhs=ckvn[:, k, :], start=(k == 0), stop=(k == 1)),
                                 r=["wukv", ("ckvn", k)], w=[f"ps{pidx2}"])
                        s = strot.next()
                        P.act(lambda e, s=s, pidx2=pidx2: e.copy(out=st[s][:], in_=PS[pidx2][:, 0:TS]), r=[f"ps{pidx2}"], w=[f"st{s}"])
                        if hg < 8:
                            rope_ep(st[s][:], f"st{s}", 128, 0, 0, 1, R_K + hg * 128)
                        else:
                            out_rows(sh, R_V + (hg - 8) * 128, 128, st[s][:], [f"st{s}"])
            elif kind == "ik":
                s = strot.next(); s2 = strot.next()
                P.act(lambda e: e.copy(out=st[s][0:64, :], in_=PS[pidx][0:64, 0:TS]), r=[f"ps{pidx}"], w=[f"st{s}"])
                P.act(lambda e: e.activation(out=st[s2][0:64, :], in_=st[s][0:64, :], func=AF.Square), r=[f"st{s}"], w=[f"st{s2}"])
                p1 = auxrot.next(); p2 = auxrot.next()
                P.pe(lambda e: e.matmul(PS[p1][0:64, 0:TS], lhsT=ones[0:64, 0:64], rhs=st[s][0:64, :], start=True, stop=True), r=["ones", f"st{s}"], w=[f"ps{p1}"])
                P.pe(lambda e: e.matmul(PS[p2][0:64, 0:TS], lhsT=ones[0:64, 0:64], rhs=st[s2][0:64, :], start=True, stop=True), r=["ones", f"st{s2}"], w=[f"ps{p2}"])
                mean, ex2, rstd, mr = stat
                P.act(lambda e: e.mul(out=mean[0:64, :], in_=PS[p1][0:64, 0:TS], mul=1.0 / 64), r=[f"ps{p1}"], w=["stat0"])
                P.act(lambda e: e.mul(out=ex2[0:64, :], in_=PS[p2][0:64, 0:TS], mul=1.0 / 64), r=[f"ps{p2}"], w=["stat1"])
                P.dve(lambda e: e.tensor_tensor(out=mr[0:64, :], in0=mean[0:64, :], in1=mean[0:64, :], op=ALU.mult), r=["stat0"], w=["stat3"])
                P.dve(lambda e: e.tensor_tensor(out=ex2[0:64, :], in0=ex2[0:64, :], in1=mr[0:64, :], op=ALU.subtract), r=["stat1", "stat3"], w=["stat1"])
                P.act(lambda e: e.activation(out=rstd[0:64, :], in_=ex2[0:64, :], func=AF.Sqrt, bias=epsb[0:64, 0:1], scale=1.0), r=["stat1", "epsb"], w=["stat2"])
                P.dve(lambda e: e.reciprocal(out=rstd[0:64, :], in_=rstd[0:64, :]), r=["stat2"], w=["stat2"])
                P.dve(lambda e: e.tensor_tensor(out=st[s][0:64, :], in0=st[s][0:64, :], in1=mean[0:64, :], op=ALU.subtract), r=[f"st{s}", "stat0"], w=[f"st{s}"])
                P.dve(lambda e: e.tensor_tensor(out=st[s][0:64, :], in0=st[s][0:64, :], in1=rstd[0:64, :], op=ALU.mult), r=[f"st{s}", "stat2"], w=[f"st{s}"])
                P.dve(lambda e: e.tensor_scalar(out=st[s][0:64, :], in0=st[s][0:64, :], scalar1=idxgb[:, 0:1], scalar2=idxgb[:, 1:2], op0=ALU.mult, op1=ALU.add), r=[f"st{s}", "idxgb"], w=[f"st{s}"])
                rope_ep(st[s][0:64, :], f"st{s}", 64, 1, 2, 3, orow)

        gemm(win_d, KC, [(c0, m) for (c0, m, _, _) in segs], lambda k, t0, n: (xb[:, k, xoff + t0:xoff + t0 + n], [("xb", k)]), [(0, TS)], ep1)
    P.emit()
    return nc

S = 8192; NSLOT = 8; NIT = 22; TOPK = 256
NEG = -1.0e30


def slot_qbs(core):
    return [core, 15 - core, 16 + core, 31 - core, 32 + core, 47 - core, 48 + core, 63 - core]


def build_I(nslot=NSLOT, nit=NIT):
    nc = bass.Bass("TRN2", target_bir_lowering=False)
    P = Prog(nc)
    def din(name, shape, dt=F32):
        return nc.dram_tensor(name, shape, dt, kind="ExternalInput").ap()
    def dout(name, shape, dt=F32):
        return nc.dram_tensor(name, shape, dt, kind="ExternalOutput").ap()
    def sb(name, shape, dt):
        return nc.alloc_sbuf_tensor("s_" + name, shape, dt)

    iqT_d = din("iqT", [NSLOT, 1024, 128])
    kiT_d = din("kiT2", [128, S])
    iw_d = din("iw", [128, NSLOT, 16])
    qpos_d = din("qpos", [128, NSLOT])
    m_o = [dout(f"m{j}", [128, 8 * (j + 1), 128], BF16) for j in range(nslot)]

    kis = sb("kis", [128, 2048], F32)
    kif = sb("kif", [128, 2048], F32)
    ki = sb("ki", [128, S], BF16)
    kl = sb("kl", [64, S], BF16)
    qis = sb("qis", [128, 16, 128], F32)
    qif = sb("qif", [128, 16, 128], F32)
    qh = sb("qh", [128, 16, 128], BF16)
    qi = sb("qi", [128, 16, 128], BF16)
    iw = sb("iw", [128, NSLOT, 16], F32)
    qpos = sb("qpos", [128, NSLOT], F32)
    accs = [sb(f"acc{i}", [128, S], F32) for i in range(2)]
    junk = sb("junk", [128, S], BF16)
    msk = sb("msk", [128, S], BF16)
    rl = [sb(f"rl{i}", [128, 512], F32) for i in range(4)]
    kpos = sb("kpos", [128, 1024], F32)
    pen = sb("pen", [128, 1024], F32)
    identf = sb("identf", [128, 128], F32)
    ident = sb("ident", [128, 128], BF16)
    sms = [sb(f"sm{i}", [128, 8], F32) for i in range(2)]
    mst = [sb(f"mst{i}", [128, 4, 128], BF16) for i in range(3)]
    PS = [nc.alloc_psum_tensor(f"ps{i}", [128, 512], F32) for i in range(6)]
    PT = [nc.alloc_psum_tensor(f"pt{i}", [128, 4, 128], BF16) for i in range(2)]

    P.pool(lambda e: e.memset(identf[:], 0.0), w=["identf"])
    P.pool(lambda e: e.affine_select(out=identf[:], in_=identf[:], pattern=[[-1, 128]], compare_op=ALU.not_equal, fill=1.0, base=0, channel_multiplier=1), r=["identf"], w=["identf"])
    P.pool(lambda e: e.tensor_copy(out=ident[:], in_=identf[:]), r=["identf"], w=["ident"])
    P.dma(iw[:], iw_d, w=["iw"])
    P.dma(qpos[:], qpos_d, w=["qpos"])
    for c in range(4):
        P.dma(kis[:], kiT_d[:, c * 2048:(c + 1) * 2048], w=["kis"])
        P.act(lambda e, c=c: e.copy(out=ki[:, c * 2048:(c + 1) * 2048], in_=kis[:]), r=["kis"], w=[("ki", c)])
        P.dve(lambda e, c=c: e.tensor_copy(out=kif[0:64, :], in_=ki[0:64, c * 2048:(c + 1) * 2048]), r=[("ki", c)], w=["kif"])
        P.dve(lambda e, c=c: e.tensor_tensor(out=kl[:, c * 2048:(c + 1) * 2048], in0=kis[0:64, :], in1=kif[0:64, :], op=ALU.subtract), r=["kis", "kif"], w=[("kl", c)])

    cnts = dict(ri=0, pi=0, ti=0, mi=0)
    qis2 = [qis, sb("qisb", [128, 16, 128], F32)]
    qi2 = [qi, sb("qib", [128, 16, 128], BF16)]

    def slot_vars(j):
        return dict(L=1024 * (j + 1), acc=accs[j % 2], AK=f"acc{j % 2}", sm=sms[j % 2], SK=f"sm{j % 2}",
                    qis=qis2[j % 2], QSK=f"qis{j % 2}", qi=qi2[j % 2], QK=f"qi{j % 2}")

    def pre(j):
        v = slot_vars(j); qis_, qi_, QSK, QK = v["qis"], v["qi"], v["QSK"], v["QK"]
        P.dma(qis_[0:64, :, :], iqT_d[j].rearrange("(h d) t -> d h t", d=64), w=[(QSK, 0)])
        P.dma(qis_[64:128, :, :], iqT_d[j].rearrange("(h d) t -> d h t", d=64), w=[(QSK, 1)])
        P.act(lambda e: e.copy(out=qh[:], in_=qis_[:]), r=[QSK], w=["qh"])
        P.act(lambda e: e.copy(out=qi_[0:64, :, :], in_=qh[0:64, :, :]), r=["qh"], w=[(QK, 0)])
        P.dve(lambda e: e.tensor_copy(out=qif[64:128, :, :], in_=qh[64:128, :, :]), r=["qh"], w=["qif"])
        P.dve(lambda e: e.tensor_tensor(out=qi_[64:128, :, :], in0=qis_[64:128, :, :], in1=qif[64:128, :, :], op=ALU.subtract), r=[QSK, "qif"], w=[(QK, 1)])

    def index_steps(j):
        v = slot_vars(j); L, acc, AK, qi_, QK = v["L"], v["acc"], v["AK"], v["qi"], v["QK"]
        steps = []
        for h in range(16):
            for kt in range(L // 512):
                def step(h=h, kt=kt):
                    pidx = cnts['pi'] % 6; cnts['pi'] += 1
                    P.pe(lambda e: e.matmul(PS[pidx][:, :], lhsT=qi_[:, h, :], rhs=ki[:, kt * 512:(kt + 1) * 512], start=True, stop=False),
                         r=[QK, ("ki", kt // 4)], w=[f"ps{pidx}"])
                    P.pe(lambda e: e.matmul(PS[pidx][:, :], lhsT=qi_[0:64, h, :], rhs=kl[0:64, kt * 512:(kt + 1) * 512], start=False, stop=True),
                         r=[QK, ("kl", kt // 4)], w=[f"ps{pidx}"])
                    r_ = cnts['ri'] % 4; cnts['ri'] += 1
                    P.act(lambda e: e.activation(out=rl[r_][:], in_=PS[pidx][:, :], func=AF.Relu), r=[f"ps{pidx}"], w=[f"rl{r_}"])
                    if h == 0:
                        P.dve(lambda e: e.tensor_scalar(out=acc[:, kt * 512:(kt + 1) * 512], in0=rl[r_][:], scalar1=iw[:, j, 0:1], scalar2=None, op0=ALU.mult),
                              r=[f"rl{r_}", "iw"], w=[(AK, kt)])
                    else:
                        P.dve(lambda e: e.scalar_tensor_tensor(out=acc[:, kt * 512:(kt + 1) * 512], in0=rl[r_][:], scalar=iw[:, j, h:h + 1], in1=acc[:, kt * 512:(kt + 1) * 512], op0=ALU.mult, op1=ALU.add),
                              r=[f"rl{r_}", "iw", (AK, kt)], w=[(AK, kt)])
                steps.append(step)
        return steps

    def post_index(j):
        v = slot_vars(j); L, acc, AK, sm, SK = v["L"], v["acc"], v["AK"], v["sm"], v["SK"]
        P.dve(lambda e: e.tensor_reduce(out=sm[:, 0:1], in_=acc[:, 0:L], axis=AX.X, op=ALU.min), r=[AK], w=[(SK, 0)])
        P.dve(lambda e: e.tensor_reduce(out=sm[:, 5:6], in_=acc[:, 0:L], axis=AX.X, op=ALU.max), r=[AK], w=[(SK, 5)])
        P.dve(lambda e: e.tensor_tensor(out=sm[:, 1:2], in0=sm[:, 5:6], in1=sm[:, 0:1], op=ALU.subtract), r=[(SK, 5), (SK, 0)], w=[(SK, 1)])
        P.dve(lambda e: e.tensor_scalar(out=sm[:, 1:2], in0=sm[:, 1:2], scalar1=0.5, scalar2=1e-6, op0=ALU.mult, op1=ALU.add), r=[(SK, 1)], w=[(SK, 1)])
        P.pool(lambda e: e.iota(kpos[:], pattern=[[1, 1024]], base=1024 * j, channel_multiplier=0, allow_small_or_imprecise_dtypes=True), w=["kpos"])
        P.dve(lambda e: e.tensor_scalar(out=pen[:], in0=kpos[:], scalar1=qpos[:, j:j + 1], scalar2=NEG, op0=ALU.is_gt, op1=ALU.mult), r=["kpos", "qpos"], w=["pen"])
        P.dve(lambda e: e.tensor_tensor(out=acc[:, L - 1024:L], in0=acc[:, L - 1024:L], in1=pen[:], op=ALU.add), r=[AK, "pen"], w=[AK])

    def bisect_steps(j):
        v = slot_vars(j); L, acc, AK, sm, SK = v["L"], v["acc"], v["AK"], v["sm"], v["SK"]
        steps = []
        for it in range(nit):
            def step(it=it):
                use_act = (it % 4 != 0)
                P.dve(lambda e: e.tensor_tensor(out=sm[:, 2:3], in0=sm[:, 0:1], in1=sm[:, 1:2], op=ALU.add), r=[(SK, 0), (SK, 1)], w=[(SK, 2)])
                if use_act:
                    P.dve(lambda e: e.tensor_scalar(out=sm[:, 6:7], in0=sm[:, 2:3], scalar1=-1.0, scalar2=None, op0=ALU.mult), r=[(SK, 2)], w=[(SK, 6)])
                    P.act(lambda e: e.activation(out=junk[:, 0:L], in_=acc[:, 0:L], func=AF.Sign, bias=sm[:, 6:7], scale=1.0, accum_out=sm[:, 3:4]),
                          r=[AK, (SK, 6)], w=["junk", (SK, 3)])
                    P.dve(lambda e: e.tensor_scalar(out=sm[:, 4:5], in0=sm[:, 3:4], scalar1=float(2 * TOPK - 1 - L), scalar2=None, op0=ALU.is_ge), r=[(SK, 3)], w=[(SK, 4)])
                else:
                    P.dve(lambda e: e.tensor_scalar(out=junk[:, 0:L], in0=acc[:, 0:L], scalar1=sm[:, 2:3], scalar2=0.0, op0=ALU.is_ge, op1=ALU.add, accum_out=sm[:, 3:4]),
                          r=[AK, (SK, 2)], w=["junk", (SK, 3)])
                    P.dve(lambda e: e.tensor_scalar(out=sm[:, 4:5], in0=sm[:, 3:4], scalar1=TOPK - 0.5, scalar2=None, op0=ALU.is_ge), r=[(SK, 3)], w=[(SK, 4)])
                P.dve(lambda e: e.scalar_tensor_tensor(out=sm[:, 0:1], in0=sm[:, 4:5], scalar=sm[:, 1:2], in1=sm[:, 0:1], op0=ALU.mult, op1=ALU.add), r=[(SK, 4), (SK, 1), (SK, 0)], w=[(SK, 0)])
                P.dve(lambda e: e.tensor_scalar(out=sm[:, 1:2], in0=sm[:, 1:2], scalar1=0.5, scalar2=None, op0=ALU.mult), r=[(SK, 1)], w=[(SK, 1)])
            steps.append(step)
        return steps

    def finish(j):
        v = slot_vars(j); L, acc, AK, sm, SK = v["L"], v["acc"], v["AK"], v["sm"], v["SK"]
        P.dve(lambda e: e.tensor_scalar(out=msk[:, 0:L], in0=acc[:, 0:L], scalar1=sm[:, 0:1], scalar2=None, op0=ALU.is_ge), r=[AK, (SK, 0)], w=["msk"])
        for b4 in range(L // 512):
            t_ = cnts['ti'] % 2; cnts['ti'] += 1
            for q in range(4):
                kb = b4 * 4 + q
                P.pe(lambda e, kb=kb, q=q, t_=t_: e.transpose(out=PT[t_][:, q, :], in_=msk[:, kb * 128:(kb + 1) * 128], identity=ident[:]), r=["msk", "ident"], w=[(f"pt{t_}", q)])
            m_ = cnts['mi'] % 3; cnts['mi'] += 1
            P.act(lambda e, t_=t_, m_=m_: e.copy(out=mst[m_][:], in_=PT[t_][:]), r=[f"pt{t_}"], w=[f"mst{m_}"])
            P.dma(m_o[j][:, b4 * 4:(b4 + 1) * 4, :], mst[m_][:], r=[f"mst{m_}"])

    pre(0)
    for st_ in index_steps(0):
        st_()
    for j in range(nslot):
        post_index(j)
        bs = bisect_steps(j)
        if j + 1 < nslot:
            pre(j + 1)
            xs = index_steps(j + 1)
        else:
            xs = []
        per = -(-len(xs) // len(bs)) if xs else 0
        xi = 0
        for b in bs:
            b()
            for _ in range(per):
                if xi < len(xs):
                    xs[xi](); xi += 1
        while xi < len(xs):
            xs[xi](); xi += 1
        finish(j)
    P.emit()
    return nc

S = 8192; NQB = 64
SCALE = 128 ** -0.5
NBLK = NQB * (NQB + 1) // 2


def blk_off(qb):
    return qb * (qb + 1) // 2


def build_T(nqb=NQB):
    nc = bass.Bass("TRN2", target_bir_lowering=False)
    P = Prog(nc)
    def din(name, shape, dt=F32):
        return nc.dram_tensor(name, shape, dt, kind="ExternalInput").ap()
    def dout(name, shape, dt=F32):
        return nc.dram_tensor(name, shape, dt, kind="ExternalOutput").ap()
    def sb(name, shape, dt):
        return nc.alloc_sbuf_tensor("s_" + name, shape, dt)

    qT_d = din("qT", [128, S])
    kT_d = din("kT", [128, S])
    v_d = din("v", [128, NQB, 128])
    mask_d = din("mask", [128, NBLK, 128], BF16)
    o_o = dout("o", [128, NQB, 128], BF16)

    stg = sb("stg", [128, 2048], F32)
    qb_ = sb("qb", [128, S], BF16)
    kb_ = sb("kb", [128, S], BF16)
    va = sb("va", [128, NQB, 129], BF16)
    ones = sb("ones", [128, 128], F32)
    sq = [sb(f"sq{i}", [128, 512], F32) for i in range(2)]
    mx = sb("mx", [128, 8], F32)
    mk = [sb(f"mk{i}", [128, NQB, 128], BF16) for i in range(2)]
    E = [sb(f"E{i}", [128, 4, 128], BF16) for i in range(3)]
    PTt = [sb(f"PT{i}", [128, 4, 128], BF16) for i in range(3)]
    ost = [sb(f"ost{i}", [128, 8, 128], BF16) for i in range(2)]
    rc = sb("rc", [128, 4], F32)
    PS = [nc.alloc_psum_tensor(f"ps{i}", [128, 512], F32) for i in range(4)]
    PO = [nc.alloc_psum_tensor(f"po{i}", [128, 512], F32) for i in range(2)]
    PX = nc.alloc_psum_tensor("px", [128, 512], F32)

    P.pool(lambda e: e.memset(ones[:], 1.0), w=["ones"])
    P.pool(lambda e: e.memset(va[:, :, 128:129], 1.0), w=[("va", "ones")])
    P.pool(lambda e: e.memset(mx[:], 0.0), w=["mx"])
    for which, (src, dst, dname) in enumerate(((qT_d, qb_, "qb"), (kT_d, kb_, "kb"))):
        for c in range(4):
            P.dma(stg[:], src[:, c * 2048:(c + 1) * 2048], w=["stg"])
            P.act(lambda e, c=c, dst=dst: e.copy(out=dst[:, c * 2048:(c + 1) * 2048], in_=stg[:]), r=["stg"], w=[(dname, c)])
            for c2 in range(4):
                s_ = (c * 4 + c2) % 2
                P.dve(lambda e, c2=c2, s_=s_: e.tensor_tensor(out=sq[s_][:], in0=stg[:, c2 * 512:(c2 + 1) * 512], in1=stg[:, c2 * 512:(c2 + 1) * 512], op=ALU.mult), r=["stg"], w=[f"sq{s_}"])
                P.pe(lambda e, s_=s_: e.matmul(PX[:, :], lhsT=ones[:], rhs=sq[s_][:], start=True, stop=True), r=["ones", f"sq{s_}"], w=["px"])
                P.dve(lambda e, which=which: e.tensor_reduce(out=mx[:, 2 + which:3 + which], in_=PX[:, :], axis=AX.X, op=ALU.max), r=["px"], w=[("mx", 2 + which)])
                P.dve(lambda e, which=which: e.tensor_tensor(out=mx[:, which:which + 1], in0=mx[:, which:which + 1], in1=mx[:, 2 + which:3 + which], op=ALU.max), r=[("mx", which), ("mx", 2 + which)], w=[("mx", which)])
    P.dve(lambda e: e.tensor_tensor(out=mx[:, 4:5], in0=mx[:, 0:1], in1=mx[:, 1:2], op=ALU.mult), r=[("mx", 0), ("mx", 1)], w=[("mx", 4)])
    P.act(lambda e: e.activation(out=mx[:, 5:6], in_=mx[:, 4:5], func=AF.Sqrt), r=[("mx", 4)], w=[("mx", 5)])
    P.dve(lambda e: e.tensor_scalar(out=mx[:, 6:7], in0=mx[:, 5:6], scalar1=-SCALE, scalar2=None, op0=ALU.mult), r=[("mx", 5)], w=[("mx", 6)])
    for c in range(4):
        P.dma(stg[:].rearrange("p (b d) -> p b d", d=128), v_d[:, c * 16:(c + 1) * 16, :], w=["stg"])
        P.act(lambda e, c=c: e.copy(out=va[:, c * 16:(c + 1) * 16, 0:128], in_=stg[:].rearrange("p (b d) -> p b d", d=128)), r=["stg"], w=[("va", c)])

    E4 = E + [sb("E3", [128, 4, 128], BF16)]
    PT4 = PTt + [sb("PT3", [128, 4, 128], BF16)]
    groups = []
    for qb in range(nqb):
        nb = qb + 1
        for k0 in range(0, nb, 4):
            groups.append((qb, k0, min(4, nb - k0)))

    def stage1(gi):
        qb, k0, n = groups[gi]
        mb = qb % 2
        if k0 == 0:
            P.dma(mk[mb][:, 0:qb + 1, :], mask_d[:, blk_off(qb):blk_off(qb) + qb + 1, :], w=[f"mk{mb}"])
        pidx = gi % 4
        for q in range(n):
            kbi = k0 + q
            P.pe(lambda e, q=q, kbi=kbi, pidx=pidx, qb=qb: e.matmul(PS[pidx][:, q * 128:(q + 1) * 128], lhsT=kb_[:, kbi * 128:(kbi + 1) * 128], rhs=qb_[:, qb * 128:(qb + 1) * 128], start=True, stop=True),
                 r=[("kb", kbi // 16), ("qb", qb // 16)], w=[(f"ps{pidx}", q)])
        e_ = gi % 4
        P.act(lambda e, e_=e_, pidx=pidx, n=n: e.activation(out=E4[e_][:, 0:n, :], in_=PS[pidx][:, 0:n * 128].rearrange("p (b t) -> p b t", t=128), func=AF.Exp, bias=mx[:, 6:7], scale=SCALE),
              r=[f"ps{pidx}", ("mx", 6)], w=[f"E{e_}"])
        eng = P.dve
        eng(lambda e, e_=e_, n=n, k0=k0, mb=mb: e.tensor_tensor(out=PT4[e_][:, 0:n, :], in0=E4[e_][:, 0:n, :], in1=mk[mb][:, k0:k0 + n, :], op=ALU.mult),
            r=[f"E{e_}", f"mk{mb}"], w=[f"PT{e_}"])

    def stage2(gi):
        qb, k0, n = groups[gi]
        nb = qb + 1
        po = qb % 2
        e_ = gi % 4
        for q in range(n):
            kbi = k0 + q
            P.pe(lambda e, q=q, kbi=kbi, e_=e_, po=po, nb=nb: e.matmul(PO[po][:, 0:129], lhsT=PT4[e_][:, q, :], rhs=va[:, kbi, :], start=(kbi == 0), stop=(kbi == nb - 1)),
                 r=[f"PT{e_}", ("va", kbi // 16), ("va", "ones")], w=[f"po{po}"])
        if k0 + n == nb:
            ob = (qb // 8) % 2
            P.dve(lambda e, po=po, qb=qb: e.reciprocal(out=rc[:, qb % 4:qb % 4 + 1], in_=PO[po][:, 128:129]), r=[f"po{po}"], w=[("rc", qb % 4)])
            P.dve(lambda e, po=po, qb=qb, ob=ob: e.tensor_scalar(out=ost[ob][:, qb % 8, :], in0=PO[po][:, 0:128], scalar1=rc[:, qb % 4:qb % 4 + 1], scalar2=None, op0=ALU.mult),
                  r=[f"po{po}", ("rc", qb % 4)], w=[(f"ost{ob}", qb % 8)])
            if qb % 8 == 7 or qb == nqb - 1:
                q0 = (qb // 8) * 8
                cnt = qb - q0 + 1
                P.dma(o_o[:, q0:q0 + cnt, :], ost[ob][:, 0:cnt, :], r=[f"ost{ob}"])

    LOOK = 2
    for gi in range(len(groups) + LOOK):
        if gi < len(groups):
            stage1(gi)
        if gi - LOOK >= 0:
            stage2(gi - LOOK)
    P.emit()
    return nc

S = 8192; C = 128; GS = 4
RMS_EPS = 1e-6
QSCALE = 128 ** -0.5


def build_G(nch=64):
    T = nch * C
    NG = nch // GS
    nc = bass.Bass("TRN2", target_bir_lowering=False)
    P = Prog(nc)
    def din(name, shape, dt=F32):
        return nc.dram_tensor(name, shape, dt, kind="ExternalInput").ap()
    def dout(name, shape, dt=F32):
        return nc.dram_tensor(name, shape, dt, kind="ExternalOutput").ap()
    def sb(name, shape, dt):
        return nc.alloc_sbuf_tensor("s_" + name, shape, dt)

    qkv_d = din("qkvT", [3, 128, T])
    cw_d = din("cw", [128, 3, 4])
    z_d = din("z", [128, nch, 128])
    ab_d = din("ab", [2, nch, 128])
    hp_d = din("hp", [128, 2])
    nw_d = din("normw", [1, 128])
    o_o = dout("o", [128, nch, 128], BF16)
    gscr = nc.dram_tensor("gscr", [2, nch * 128], F32).ap()

    X = sb("X", [128, T + 3], F32)
    U = sb("U", [128, T], F32)
    kT = sb("kT", [128, T], BF16)
    qT = sb("qT", [128, T], BF16)
    qdT = sb("qdT", [128, T], BF16)
    vT = sb("vT", [128, T], BF16)
    cw = sb("cw", [128, 3, 4], F32)
    hp = sb("hp", [128, 2], F32)
    nwr = sb("nwr", [128, 128], F32)
    ones = sb("ones", [128, 128], F32)
    identf = sb("identf", [128, 128], F32)
    ident = sb("ident", [128, 128], BF16)
    utri = sb("utri", [128, 128], F32)
    dmask = sb("dmask", [128, 128], F32)
    nstrict = sb("nstrict", [128, 128], F32)
    epsb = sb("epsb", [128, 2], F32)
    tmp = [sb(f"tmp{i}", [128, 512], F32) for i in range(4)]
    a_sb = sb("a_sb", [64, 128], F32)
    b_sb = sb("b_sb", [64, 128], F32)
    gcc = sb("gcc", [64, 128], F32)
    cols = sb("cols", [128, 8, 64], F32)
    nea = sb("nea", [128, 2], F32)
    T1 = sb("T1", [128, 2048], F32)
    def gb(name, dt):
        return [sb(f"{name}{i}", [128, GS, 128], dt) for i in range(2)]
    bek_g = gb("bek", BF16); kdec_g = gb("kdec", BF16); bv_g = gb("bv", BF16)
    attn_g = gb("attn", BF16); u_g = gb("u", F32); wT_g = gb("wT", BF16)
    dc_g = gb("dc", F32); t_g = gb("tt", F32)
    Qb = [sb(f"Q{i}", [128, GS, 128], BF16) for i in range(2)]
    Rb = [sb(f"R{i}", [128, GS, 128], BF16) for i in range(2)]
    Yb = sb("Y", [128, GS, 128], BF16)
    zg = gb("zg", F32); gw = gb("gw", F32); og = gb("og", BF16)
    Sf = sb("Sf", [128, 128], F32)
    Sb = sb("Sb", [128, 128], BF16)
    vnew = sb("vnew", [128, 128], BF16)
    junk = sb("junk", [128, 128], F32)
    ssq = sb("ssq", [128, 2], F32)
    PS = [nc.alloc_psum_tensor(f"ps{i}", [128, 512], F32) for i in range(7)]
    PTb = nc.alloc_psum_tensor("ptb", [128, GS, 128], BF16)
    PQ, PR, PY, PA, PB, PSC_A, PSC_B = range(7)

    P.pool(lambda e: e.memset(ones[:], 1.0), w=["ones"])
    P.pool(lambda e: e.memset(epsb[:], RMS_EPS), w=["epsb"])
    P.pool(lambda e: e.memset(identf[:], 0.0), w=["identf"])
    P.pool(lambda e: e.affine_select(out=identf[:], in_=identf[:], pattern=[[-1, 128]], compare_op=ALU.not_equal, fill=1.0, base=0, channel_multiplier=1), r=["identf"], w=["identf"])
    P.pool(lambda e: e.tensor_copy(out=ident[:], in_=identf[:]), r=["identf"], w=["ident"])
    P.pool(lambda e: e.memset(utri[:], 1.0), w=["utri"])
    P.pool(lambda e: e.affine_select(out=utri[:], in_=utri[:], pattern=[[1, 128]], compare_op=ALU.is_ge, fill=0.0, base=0, channel_multiplier=-1), r=["utri"], w=["utri"])
    P.pool(lambda e: e.memset(dmask[:], 0.0), w=["dmask"])
    P.pool(lambda e: e.affine_select(out=dmask[:], in_=dmask[:], pattern=[[1, 128]], compare_op=ALU.is_ge, fill=-30000.0, base=0, channel_multiplier=-1), r=["dmask"], w=["dmask"])
    P.pool(lambda e: e.memset(nstrict[:], -1.0), w=["nstrict"])
    P.pool(lambda e: e.affine_select(out=nstrict[:], in_=nstrict[:], pattern=[[1, 128]], compare_op=ALU.is_gt, fill=0.0, base=0, channel_multiplier=-1), r=["nstrict"], w=["nstrict"])
    P.pool(lambda e: e.memset(X[:, 0:3], 0.0), w=[("X", "pad")])
    P.pool(lambda e: e.memset(Sf[:], 0.0), w=["Sf"])
    P.pool(lambda e: e.memset(Sb[:], 0.0), w=["Sb"])
    P.dma(cw[:], cw_d, w=["cw"])
    P.dma(hp[:], hp_d, w=["hp"])
    P.dma(nwr[:], nw_d.partition_broadcast(128), w=["nwr"])

    PW = min(2048, T)
    NP = T // PW
    ti_ = 0
    for ti, dst in ((0, qT), (1, kT), (2, vT)):
        dname = ("qT", "kT", "vT")[ti]
        for c in range(NP):
            P.dma(X[:, 3 + c * PW:3 + (c + 1) * PW], qkv_d[ti, :, c * PW:(c + 1) * PW], w=[("X", c)])
        for c in range(NP):
            lo = c * PW
            rk = [("X", c), ("X", "pad")] + ([("X", c - 1)] if c > 0 else [])
            P.dve(lambda e, lo=lo, ti=ti: e.tensor_scalar(out=U[:, lo:lo + PW], in0=X[:, lo + 3:lo + 3 + PW], scalar1=cw[:, ti, 3:4], scalar2=None, op0=ALU.mult), r=rk + ["cw"], w=[("U", c)])
            for jj in (2, 1, 0):
                P.dve(lambda e, lo=lo, ti=ti, jj=jj: e.scalar_tensor_tensor(out=U[:, lo:lo + PW], in0=X[:, lo + jj:lo + jj + PW], scalar=cw[:, ti, jj:jj + 1], in1=U[:, lo:lo + PW], op0=ALU.mult, op1=ALU.add),
                      r=rk + ["cw", ("U", c)], w=[("U", c)])
            P.act(lambda e, lo=lo: e.activation(out=U[:, lo:lo + PW], in_=U[:, lo:lo + PW], func=AF.Silu), r=[("U", c)], w=[("U", c)])
            if ti == 2:
                P.act(lambda e, lo=lo: e.copy(out=vT[:, lo:lo + PW], in_=U[:, lo:lo + PW]), r=[("U", c)], w=[("vT", c)])
                continue
            for c2 in range(PW // 512):
                l2 = lo + c2 * 512
                t_ = ti_ % 4; ti_ += 1
                P.dve(lambda e, l2=l2, t_=t_: e.tensor_tensor(out=tmp[t_][:], in0=U[:, l2:l2 + 512], in1=U[:, l2:l2 + 512], op=ALU.mult), r=[("U", c)], w=[f"tmp{t_}"])
                P.pe(lambda e, t_=t_: e.matmul(PS[PY][:, :], lhsT=ones[:], rhs=tmp[t_][:], start=True, stop=True), r=["ones", f"tmp{t_}"], w=[f"ps{PY}"])
                P.act(lambda e, t_=t_: e.activation(out=tmp[t_][:], in_=PS[PY][:, :], func=AF.Sqrt, bias=epsb[:, 0:1], scale=1.0), r=[f"ps{PY}", "epsb"], w=[f"tmp{t_}"])
                P.dve(lambda e, t_=t_: e.reciprocal(out=tmp[t_][:], in_=tmp[t_][:]), r=[f"tmp{t_}"], w=[f"tmp{t_}"])
                sc = QSCALE if ti == 0 else 1.0
                P.dve(lambda e, l2=l2, t_=t_, dst=dst, sc=sc: e.scalar_tensor_tensor(out=dst[:, l2:l2 + 512], in0=U[:, l2:l2 + 512], scalar=sc, in1=tmp[t_][:], op0=ALU.mult, op1=ALU.mult),
                      r=[("U", c), f"tmp{t_}"], w=[(dname, c)])

    P.dma(a_sb[0:nch, :], ab_d[0], w=["a_sb"])
    P.dma(b_sb[0:nch, :], ab_d[1], w=["b_sb"])
    P.act(lambda e: e.activation(out=a_sb[0:nch, :], in_=a_sb[0:nch, :], func=AF.Exp, bias=hp[0:nch, 1:2], scale=1.0), r=["a_sb", "hp"], w=["a_sb"])
    P.act(lambda e: e.activation(out=a_sb[0:nch, :], in_=a_sb[0:nch, :], func=AF.Ln, bias=ones[0:nch, 0:1], scale=1.0), r=["a_sb", "ones"], w=["a_sb"])
    P.act(lambda e: e.activation(out=nea[:, 0:1], in_=hp[:, 0:1], func=AF.Exp), r=["hp"], w=["nea"])
    P.dve(lambda e: e.tensor_scalar(out=a_sb[0:nch, :], in0=a_sb[0:nch, :], scalar1=nea[0:nch, 0:1], scalar2=-1.0, op0=ALU.mult, op1=ALU.mult), r=["a_sb", "nea"], w=["a_sb"])
    P.act(lambda e: e.activation(out=b_sb[0:nch, :], in_=b_sb[0:nch, :], func=AF.Sigmoid), r=["b_sb"], w=["b_sb"])
    P.dma(gscr[1].rearrange("(n i) -> n i", i=128), b_sb[0:nch, :], r=["b_sb"], w=[("gscr", 1)])
    P.pe(lambda e: e.transpose(out=PS[PQ][:, 0:nch], in_=a_sb[0:nch, :], identity=identf[0:nch, 0:nch]), r=["a_sb", "identf"], w=[f"ps{PQ}"])
    P.act(lambda e: e.copy(out=cols[:, 0, 0:nch], in_=PS[PQ][:, 0:nch]), r=[f"ps{PQ}"], w=[("cols", 0)])
    P.pe(lambda e: e.transpose(out=PS[PR][:, 0:nch], in_=b_sb[0:nch, :], identity=identf[0:nch, 0:nch]), r=["b_sb", "identf"], w=[f"ps{PR}"])
    P.act(lambda e: e.copy(out=cols[:, 1, 0:nch], in_=PS[PR][:, 0:nch]), r=[f"ps{PR}"], w=[("cols", 1)])
    P.pe(lambda e: e.matmul(PS[PA][:, 0:nch], lhsT=utri[:], rhs=cols[:, 0, 0:nch], start=True, stop=True), r=["utri", ("cols", 0)], w=[f"ps{PA}"])
    P.act(lambda e: e.copy(out=cols[:, 2, 0:nch], in_=PS[PA][:, 0:nch]), r=[f"ps{PA}"], w=[("cols", 2)])
    P.pe(lambda e: e.matmul(PS[PB][:, 0:nch], lhsT=ones[:], rhs=cols[:, 0, 0:nch], start=True, stop=True), r=["ones", ("cols", 0)], w=[f"ps{PB}"])
    P.act(lambda e: e.copy(out=cols[:, 3, 0:nch], in_=PS[PB][:, 0:nch]), r=[f"ps{PB}"], w=[("cols", 3)])
    P.act(lambda e: e.activation(out=cols[:, 4, 0:nch], in_=cols[:, 3, 0:nch], func=AF.Exp), r=[("cols", 3)], w=[("cols", 4)])
    P.act(lambda e: e.activation(out=cols[:, 5, 0:nch], in_=cols[:, 2, 0:nch], func=AF.Exp), r=[("cols", 2)], w=[("cols", 5)])
    P.dve(lambda e: e.tensor_tensor(out=cols[:, 5, 0:nch], in0=cols[:, 5, 0:nch], in1=cols[:, 1, 0:nch], op=ALU.mult), r=[("cols", 5), ("cols", 1)], w=[("cols", 5)])
    P.dve(lambda e: e.tensor_tensor(out=cols[:, 6, 0:nch], in0=cols[:, 3, 0:nch], in1=cols[:, 2, 0:nch], op=ALU.subtract), r=[("cols", 3), ("cols", 2)], w=[("cols", 6)])
    P.act(lambda e: e.activation(out=cols[:, 6, 0:nch], in_=cols[:, 6, 0:nch], func=AF.Exp), r=[("cols", 6)], w=[("cols", 6)])
    P.dve(lambda e: e.tensor_scalar(out=cols[:, 7, 0:nch], in0=cols[:, 2, 0:nch], scalar1=-1.0, scalar2=None, op0=ALU.mult), r=[("cols", 2)], w=[("cols", 7)])
    P.pe(lambda e: e.transpose(out=PS[PY][0:nch, 0:128], in_=cols[:, 2, 0:nch], identity=identf[:]), r=[("cols", 2), "identf"], w=[f"ps{PY}"])
    P.act(lambda e: e.copy(out=gcc[0:nch, :], in_=PS[PY][0:nch, 0:128]), r=[f"ps{PY}"], w=["gcc"])
    P.dma(gscr[0].rearrange("(n i) -> n i", i=128), gcc[0:nch, :], r=["gcc"], w=[("gscr", 0)])
    GR = X[:, 3:3 + T]
    BR = U[:, 0:T]
    for c in range(NP):
        P.dma(X[:, 3 + c * PW:3 + (c + 1) * PW], gscr[0:1, c * PW:(c + 1) * PW].partition_broadcast(128), r=[("gscr", 0)], w=[("X", c)])
        P.dma(U[:, c * PW:(c + 1) * PW], gscr[1:2, c * PW:(c + 1) * PW].partition_broadcast(128), r=[("gscr", 1)], w=[("U", c)])
    npc = PW // 128
    for c in range(NP):
        lo = c * PW
        P.act(lambda e, lo=lo: e.activation(out=T1[:, 0:PW], in_=X[:, 3 + lo:3 + lo + PW], func=AF.Exp), r=[("X", c)], w=["T1"])
        P.dve(lambda e, lo=lo: e.tensor_tensor(out=qdT[:, lo:lo + PW], in0=qT[:, lo:lo + PW], in1=T1[:, 0:PW], op=ALU.mult), r=[("qT", c), "T1"], w=[("qdT", c)])
        P.dve(lambda e, lo=lo: e.tensor_tensor(out=X[:, 3 + lo:3 + lo + PW].rearrange("p (n i) -> p n i", i=128), in0=X[:, 3 + lo:3 + lo + PW].rearrange("p (n i) -> p n i", i=128),
                                                in1=dmask[:].unsqueeze(1).to_broadcast([128, npc, 128]), op=ALU.add), r=[("X", c), "dmask"], w=[("X", c)])
        P.dve(lambda e, lo=lo: e.tensor_tensor(out=U[:, lo:lo + PW].rearrange("p (n i) -> p n i", i=128), in0=U[:, lo:lo + PW].rearrange("p (n i) -> p n i", i=128),
                                                in1=nstrict[:].unsqueeze(1).to_broadcast([128, npc, 128]), op=ALU.mult), r=[("U", c), "nstrict"], w=[("U", c)])

    def colb(ci, n0):
        return cols[:, ci, n0:n0 + GS].unsqueeze(2).to_broadcast([128, GS, 128])

    def precompute(g):
        parts = []
        n0 = g * GS
        pb = g % 2
        pc = (n0 * 128) // PW
        flat = lambda ap: ap.rearrange("p g i -> p (g i)")
        def partA():
            for q in range(GS):
                n = n0 + q
                P.pe(lambda e, q=q, n=n: e.transpose(out=PTb[:, q, :], in_=kT[:, n * 128:(n + 1) * 128], identity=ident[:]), r=[("kT", pc), "ident"], w=[("ptb", q)])
            P.dve(lambda e: e.tensor_tensor(out=bek_g[pb][:], in0=PTb[:], in1=colb(5, n0), op=ALU.mult), r=["ptb", ("cols", 5)], w=[f"bek{pb}"])
            P.dve(lambda e: e.tensor_tensor(out=kdec_g[pb][:], in0=PTb[:], in1=colb(6, n0), op=ALU.mult), r=["ptb", ("cols", 6)], w=[f"kdec{pb}"])
            for q in range(GS):
                n = n0 + q
                P.pe(lambda e, q=q, n=n: e.transpose(out=PTb[:, q, :], in_=vT[:, n * 128:(n + 1) * 128], identity=ident[:]), r=[("vT", pc), "ident"], w=[("ptb", q)])
            P.dve(lambda e: e.tensor_tensor(out=bv_g[pb][:], in0=PTb[:], in1=colb(1, n0), op=ALU.mult), r=["ptb", ("cols", 1)], w=[f"bv{pb}"])

        def partB():
            for q in range(GS):
                n = n0 + q
                P.pe(lambda e, q=q, n=n: e.matmul(PS[PA][:, q * 128:(q + 1) * 128], lhsT=kT[:, n * 128:(n + 1) * 128], rhs=kT[:, n * 128:(n + 1) * 128], start=True, stop=True), r=[("kT", pc)], w=[(f"ps{PA}", q)])
                P.pe(lambda e, q=q, n=n: e.matmul(PS[PB][:, q * 128:(q + 1) * 128], lhsT=kT[:, n * 128:(n + 1) * 128], rhs=qT[:, n * 128:(n + 1) * 128], start=True, stop=True), r=[("kT", pc), ("qT", pc)], w=[(f"ps{PB}", q)])
                P.act(lambda e, q=q, n=n: e.activation(out=dc_g[pb][:, q, :], in_=X[:, 3 + n * 128:3 + (n + 1) * 128], func=AF.Exp, bias=cols[:, 7, n:n + 1], scale=1.0), r=[("X", pc), ("cols", 7)], w=[(f"dc{pb}", q)])
            P.dve(lambda e: e.tensor_tensor(out=flat(attn_g[pb][:]), in0=PS[PB][:, :], in1=flat(dc_g[pb][:]), op=ALU.mult), r=[f"ps{PB}", f"dc{pb}"], w=[f"attn{pb}"])
            P.dve(lambda e: e.tensor_tensor(out=flat(t_g[pb][:]), in0=PS[PA][:, :], in1=flat(dc_g[pb][:]), op=ALU.mult), r=[f"ps{PA}", f"dc{pb}"], w=[f"tt{pb}"])
            P.dve(lambda e: e.tensor_tensor(out=flat(Qb[0][:]), in0=flat(t_g[pb][:]), in1=U[:, n0 * 128:(n0 + GS) * 128], op=ALU.mult), r=[f"tt{pb}", ("U", pc)], w=["Q0"])
            for q in range(GS):
                P.pe(lambda e, q=q: e.transpose(out=PTb[:, q, :], in_=Qb[0][:, q, :], identity=ident[:]), r=["Q0", "ident"], w=[("ptb", q)])
            P.act(lambda e: e.copy(out=Rb[0][:], in_=PTb[:]), r=["ptb"], w=["R0"])
            P.dve(lambda e: e.tensor_tensor(out=Yb[:], in0=Qb[0][:], in1=ident[:].unsqueeze(1).to_broadcast([128, GS, 128]), op=ALU.add), r=["Q0", "ident"], w=["Y"])

        cur_box = [0]
        def level(lvl):
            cur = cur_box[0]
            nx = 1 - cur
            if lvl <= 5:
                for q in range(GS):
                    P.pe(lambda e, q=q, cur=cur: e.matmul(PS[PQ][:, q * 128:(q + 1) * 128], lhsT=Rb[cur][:, q, :], rhs=Qb[cur][:, q, :], start=True, stop=True), r=[f"R{cur}", f"Q{cur}"], w=[(f"ps{PQ}", q)])
            for q in range(GS):
                P.pe(lambda e, q=q, cur=cur: e.matmul(PS[PR][:, q * 128:(q + 1) * 128], lhsT=Qb[cur][:, q, :], rhs=Rb[cur][:, q, :], start=True, stop=True), r=[f"R{cur}", f"Q{cur}"], w=[(f"ps{PR}", q)])
            if lvl <= 5:
                P.act(lambda e, nx=nx: e.copy(out=flat(Qb[nx][:]), in_=PS[PQ][:, :]), r=[f"ps{PQ}"], w=[f"Q{nx}"])
            P.act(lambda e, nx=nx: e.copy(out=flat(Rb[nx][:]), in_=PS[PR][:, :]), r=[f"ps{PR}"], w=[f"R{nx}"])
            for q in range(GS):
                P.pe(lambda e, q=q, nx=nx: e.matmul(PS[PY][:, q * 128:(q + 1) * 128], lhsT=Rb[nx][:, q, :], rhs=Yb[:, q, :], start=True, stop=True), r=[f"R{nx}", "Y"], w=[(f"ps{PY}", q)])
            P.dve(lambda e: e.tensor_tensor(out=flat(Yb[:]), in0=PS[PY][:, :], in1=flat(Yb[:]), op=ALU.add), r=[f"ps{PY}", "Y"], w=["Y"])
            cur_box[0] = nx
        def partE():
            for q in range(GS):
                P.pe(lambda e, q=q: e.matmul(PS[PQ][:, q * 128:(q + 1) * 128], lhsT=Yb[:, q, :], rhs=bv_g[pb][:, q, :], start=True, stop=True), r=["Y", f"bv{pb}"], w=[(f"ps{PQ}", q)])
                P.pe(lambda e, q=q: e.matmul(PS[PR][:, q * 128:(q + 1) * 128], lhsT=bek_g[pb][:, q, :], rhs=Yb[:, q, :], start=True, stop=True), r=["Y", f"bek{pb}"], w=[(f"ps{PR}", q)])
            P.act(lambda e: e.copy(out=flat(u_g[pb][:]), in_=PS[PQ][:, :]), r=[f"ps{PQ}"], w=[f"u{pb}"])
            P.act(lambda e: e.copy(out=flat(wT_g[pb][:]), in_=PS[PR][:, :]), r=[f"ps{PR}"], w=[f"wT{pb}"])
            P.dma(zg[pb][:], z_d[:, n0:n0 + GS, :], w=[f"zg{pb}"])
            P.act(lambda e: e.activation(out=zg[pb][:], in_=zg[pb][:], func=AF.Silu), r=[f"zg{pb}"], w=[f"zg{pb}"])
            P.dve(lambda e: e.tensor_tensor(out=gw[pb][:], in0=zg[pb][:], in1=nwr[:].unsqueeze(1).to_broadcast([128, GS, 128]), op=ALU.mult), r=[f"zg{pb}", "nwr"], w=[f"gw{pb}"])


        return [partA, partB, lambda: (level(1), level(2)), lambda: (level(3), level(4)), lambda: (level(5), level(6), partE())]

    def scan_step(g, q):
        n0 = g * GS
        pb = g % 2
        pc = (n0 * 128) // PW
        if True:
            n = n0 + q
            P.pe(lambda e, q=q: e.matmul(PS[PSC_A][:, 0:128], lhsT=wT_g[pb][:, q, :], rhs=Sb[:], start=True, stop=True), r=[f"wT{pb}", "Sb"], w=[(f"ps{PSC_A}", 0)])
            P.pe(lambda e, n=n: e.matmul(PS[PSC_B][:, 0:128], lhsT=qdT[:, n * 128:(n + 1) * 128], rhs=Sb[:], start=True, stop=False), r=[("qdT", pc), "Sb"], w=[f"ps{PSC_B}"])
            P.dve(lambda e, q=q: e.tensor_tensor(out=vnew[:], in0=u_g[pb][:, q, :], in1=PS[PSC_A][:, 0:128], op=ALU.subtract), r=[f"u{pb}", (f"ps{PSC_A}", 0)], w=["vnew"])
            P.pe(lambda e, q=q: e.matmul(PS[PSC_B][:, 0:128], lhsT=attn_g[pb][:, q, :], rhs=vnew[:], start=False, stop=True), r=[f"attn{pb}", "vnew"], w=[f"ps{PSC_B}"])
            P.pe(lambda e, q=q: e.matmul(PS[PSC_A][:, 128:256], lhsT=kdec_g[pb][:, q, :], rhs=vnew[:], start=True, stop=True), r=[f"kdec{pb}", "vnew"], w=[(f"ps{PSC_A}", 1)])
            P.dve(lambda e, n=n: e.scalar_tensor_tensor(out=Sf[:], in0=Sf[:], scalar=cols[:, 4, n:n + 1], in1=PS[PSC_A][:, 128:256], op0=ALU.mult, op1=ALU.add), r=["Sf", ("cols", 4), (f"ps{PSC_A}", 1)], w=["Sf"])
            P.act(lambda e: e.copy(out=Sb[:], in_=Sf[:]), r=["Sf"], w=["Sb"])
            P.act(lambda e: e.activation(out=junk[:], in_=PS[PSC_B][:, 0:128], func=AF.Square, accum_out=ssq[:, 0:1]), r=[f"ps{PSC_B}"], w=["junk", ("ssq", 0)])
            P.act(lambda e: e.activation(out=ssq[:, 1:2], in_=ssq[:, 0:1], func=AF.Sqrt, bias=epsb[:, 0:1], scale=1.0 / 128), r=[("ssq", 0), "epsb"], w=[("ssq", 1)])
            P.dve(lambda e: e.reciprocal(out=ssq[:, 1:2], in_=ssq[:, 1:2]), r=[("ssq", 1)], w=[("ssq", 1)])
            P.dve(lambda e, q=q: e.scalar_tensor_tensor(out=og[pb][:, q, :], in0=PS[PSC_B][:, 0:128], scalar=ssq[:, 1:2], in1=gw[pb][:, q, :], op0=ALU.mult, op1=ALU.mult),
                  r=[f"ps{PSC_B}", ("ssq", 1), f"gw{pb}"], w=[(f"og{pb}", q)])
        if q == GS - 1:
            P.dma(o_o[:, n0:n0 + GS, :], og[pb][:], r=[f"og{pb}"])

    for p_ in precompute(0):
        p_()
    for g in range(NG):
        parts = precompute(g + 1) if g + 1 < NG else []
        for q in range(GS):
            if q < len(parts):
                parts[q]()
            scan_step(g, q)
        for p_ in parts[GS:]:
            p_()
    P.emit()
    return nc
import ml_dtypes

W_NAMES = ("w_in", "ffn_up", "ffn_down", "w_out", "w_ukv")


def build_W(cols):
    nc = bass.Bass("TRN2", target_bir_lowering=False)
    P = Prog(nc)
    CH = 4096
    stg = [nc.alloc_sbuf_tensor(f"s_stg{i}", [128, CH], F32) for i in range(3)]
    ob = [nc.alloc_sbuf_tensor(f"s_ob{i}", [128, CH], BF16) for i in range(3)]
    k = 0
    for i, n in enumerate(cols):
        src = nc.dram_tensor(f"w{i}", [128, n], F32, kind="ExternalInput").ap()
        dst = nc.dram_tensor(f"o{i}", [128, n], BF16, kind="ExternalOutput").ap()
        for c0 in range(0, n, CH):
            w = min(CH, n - c0)
            b = k % 3
            P.dma(stg[b][:, 0:w], src[:, c0:c0 + w], w=[f"stg{b}"])
            if k % 2 == 0:
                P.act(lambda e, b=b, w=w: e.copy(out=ob[b][:, 0:w], in_=stg[b][:, 0:w]), r=[f"stg{b}"], w=[f"ob{b}"])
            else:
                P.dve(lambda e, b=b, w=w: e.tensor_copy(out=ob[b][:, 0:w], in_=stg[b][:, 0:w]), r=[f"stg{b}"], w=[f"ob{b}"])
            P.dma(dst[:, c0:c0 + w], ob[b][:, 0:w], r=[f"ob{b}"])
            k += 1
    P.emit()
    return nc


def _rope_tab(pos, dim):
    inv = (10000.0 ** (-np.arange(0, dim, 2, dtype=np.float32) / dim)).astype(np.float32)
    ang = pos[:, None].astype(np.float32) * inv[None, :]
    ang = np.concatenate([ang, ang], -1)
    return np.cos(ang).astype(np.float32), np.sin(ang).astype(np.float32)


def _rmat(dim, reps):
    R = np.zeros((128, 128), np.float32)
    h = dim // 2
    for b in range(reps):
        for m in range(dim):
            if m < h:
                R[b * dim + m + h, b * dim + m] = -1.0
            else:
                R[b * dim + m - h, b * dim + m] = 1.0
    return R


def _fm(v):
    return np.ascontiguousarray(np.asarray(v, np.float32).reshape(-1, 128).T)


_PROGS = {}


def _prog(key, fn):
    if key not in _PROGS:
        _PROGS[key] = fn()
    return _PROGS[key]


def _run(nc, in_maps):
    res = run_bass_kernel_spmd(nc, in_maps, core_ids=list(range(NCORE)))
    return res.results


def kernel(**inp):
    bf = ml_dtypes.bfloat16
    inp = {k: np.asarray(v) for k, v in inp.items()}
    x = inp["x"][0]
    L = DEPTH
    slices = []
    for c in range(NCORE):
        d = {}
        for i, nm in enumerate(W_NAMES):
            W = inp[nm]
            Rc = W.shape[1] // NCORE
            d[f"w{i}"] = np.ascontiguousarray(W[:, c * Rc:(c + 1) * Rc, :]).reshape(128, -1)
        slices.append(d)
    cols = [slices[0][f"w{i}"].shape[1] for i in range(len(W_NAMES))]
    ncw = _prog("W", lambda: build_W(cols))
    resw = _run(ncw, slices)
    wbf = {}
    for i, nm in enumerate(W_NAMES):
        W = inp[nm]
        Rc = W.shape[1] // NCORE
        parts = [np.asarray(resw[c][f"o{i}"]).reshape(L, Rc, W.shape[2]) for c in range(NCORE)]
        wbf[nm] = np.concatenate(parts, axis=1)
    del slices, resw

    rmat = np.stack([_rmat(128, 1), _rmat(64, 2)])
    ropes = {}
    def rope_for(t0):
        if t0 not in ropes:
            pos = t0 + np.arange(TS)
            c128, s128 = _rope_tab(pos, 128)
            c64, s64 = _rope_tab(pos, 64)
            ropes[t0] = np.stack([c128.T, s128.T, np.concatenate([c64.T, c64.T], 0), np.concatenate([s64.T, s64.T], 0)]).astype(np.float32)
        return ropes[t0]

    def run_A(layer_post, layer_proj, xT_full, mixT_full):
        do_post = layer_post is not None
        do_proj = layer_proj is not None
        nc = _prog(("A", do_post, do_proj), lambda: build_A(do_post, do_proj))
        maps = []
        for c in range(NCORE):
            d = {}
            t0s = [c * NSH * TS + sh * TS for sh in range(NSH)]
            if do_post:
                xs = []; ms = []
                for t0 in t0s:
                    if t0 == 0:
                        xs.append(np.concatenate([np.zeros((D, HALO), np.float32), xT_full[:, 0:TS]], 1))
                        ms.append(np.concatenate([np.zeros((D, HALO), bf), mixT_full[:, 0:TS]], 1))
                    else:
                        xs.append(xT_full[:, t0 - HALO:t0 + TS])
                        ms.append(mixT_full[:, t0 - HALO:t0 + TS])
                d["xT"] = np.ascontiguousarray(np.stack(xs))
                d["mixT"] = np.ascontiguousarray(np.stack(ms))
                i = layer_post
                d["w_out"] = wbf["w_out"][i]; d["ffn_up"] = wbf["ffn_up"][i]; d["ffn_down"] = wbf["ffn_down"][i]
                d["lnp"] = np.ascontiguousarray(np.stack([_fm(inp["ln1_g"][i]), _fm(inp["ln1_b"][i]), _fm(inp["ln2_g"][i]), _fm(inp["ln2_b"][i])], 1))
                cwa = np.concatenate([inp["ffn_conv_w"][i], inp["ffn_conv_b"][i][None]], 0).astype(np.float32)
                d["convw"] = np.ascontiguousarray(cwa.T.reshape(88, 128, 4).transpose(1, 0, 2))
                hf = np.ones((128, NSH), np.float32)
                if c == 0:
                    hf[:, 0] = 0.0
                d["haloflag"] = hf
            else:
                d["xT"] = np.ascontiguousarray(np.stack([xT_full[:, t0:t0 + TS] for t0 in t0s]))
            if do_proj:
                i = layer_proj
                d["w_in"] = wbf["w_in"][i]; d["w_ukv"] = wbf["w_ukv"][i]
                d["kvnw"] = _fm(inp["kv_norm_w"][i])
                d["idxgb"] = np.ascontiguousarray(np.stack([inp["idx_k_norm_g"][i], inp["idx_k_norm_b"][i]], 1).astype(np.float32))
                d["rope"] = np.ascontiguousarray(np.stack([rope_for(t0) for t0 in t0s]))
                d["rmat"] = rmat
            maps.append(d)
        res = _run(nc, maps)
        x2T = None; pT = None
        if do_post:
            x2T = np.concatenate([np.asarray(res[c]["x2T"][sh]) for c in range(NCORE) for sh in range(NSH)], axis=1)
        if do_proj:
            pT = np.concatenate([np.asarray(res[c]["pT"][sh]) for c in range(NCORE) for sh in range(NSH)], axis=1)
        return x2T, pT

    def run_I(pT):
        nc = _prog("I", lambda: build_I())
        kiT2 = np.ascontiguousarray(np.concatenate([pT[R_IK:R_IK + 64], pT[R_IK:R_IK + 64]], 0))
        maps = []
        for c in range(NCORE):
            qbs = slot_qbs(c)
            d = {"kiT2": kiT2}
            d["iqT"] = np.ascontiguousarray(np.stack([pT[R_IQ:R_IQ + 1024, q * 128:(q + 1) * 128] for q in qbs]))
            d["iw"] = np.ascontiguousarray(np.stack([pT[R_IW:R_IW + 16, q * 128:(q + 1) * 128].T for q in qbs], 1))
            d["qpos"] = np.ascontiguousarray(np.stack([np.arange(q * 128, (q + 1) * 128) for q in qbs], 1).astype(np.float32))
            maps.append(d)
        res = _run(nc, maps)
        mask = np.zeros((128, NBLK, 128), bf)
        for c in range(NCORE):
            for j, q in enumerate(slot_qbs(c)):
                mask[:, blk_off(q):blk_off(q) + q + 1, :] = np.asarray(res[c][f"m{j}"])[:, 0:q + 1, :]
        return mask

    def tokmajor(a):
        return np.ascontiguousarray(a.T.reshape(64, 128, 128).transpose(1, 0, 2))

    def run_T(pT, mask):
        nc = _prog("T", lambda: build_T())
        maps = []
        for c in range(NCORE):
            maps.append({"qT": np.ascontiguousarray(pT[R_AQ + c * 128:R_AQ + (c + 1) * 128]),
                         "kT": np.ascontiguousarray(pT[R_K + c * 128:R_K + (c + 1) * 128]),
                         "v": tokmajor(pT[R_V + c * 128:R_V + (c + 1) * 128]),
                         "mask": mask})
        res = _run(nc, maps)
        return np.concatenate([np.asarray(res[c]["o"]).transpose(1, 0, 2).reshape(S, 128) for c in range(NCORE)], axis=1)

    def run_G(pT, i):
        nc = _prog("G", lambda: build_G())
        maps = []
        gcw = inp["gdn_conv_w"][i].astype(np.float32)
        for c in range(NCORE):
            d = {}
            d["qkvT"] = np.ascontiguousarray(np.stack([pT[ti * 1024 + c * 128:ti * 1024 + (c + 1) * 128] for ti in range(3)]))
            d["cw"] = np.ascontiguousarray(np.stack([gcw[:, ti * 1024 + c * 128:ti * 1024 + (c + 1) * 128].T for ti in range(3)], 1))
            d["z"] = tokmajor(pT[3072 + c * 128:3072 + (c + 1) * 128])
            d["ab"] = np.ascontiguousarray(np.stack([pT[R_GAB + c].reshape(64, 128), pT[R_GAB + 8 + c].reshape(64, 128)]))
            d["hp"] = np.ascontiguousarray(np.tile(np.array([[inp["gdn_a_log"][i][c], inp["gdn_dt_bias"][i][c]]], np.float32), (128, 1)))
            d["normw"] = np.ascontiguousarray(inp["gdn_norm_w"][i][None, :].astype(np.float32))
            maps.append(d)
        res = _run(nc, maps)
        return np.concatenate([np.asarray(res[c]["o"]).transpose(1, 0, 2).reshape(S, 128) for c in range(NCORE)], axis=1)

    xT_full = np.ascontiguousarray(x.T.astype(np.float32))
    _, pT = run_A(None, 0, xT_full, None)
    for i in range(L):
        mask = run_I(pT)
        o_att = run_T(pT, mask)
        o_gdn = run_G(pT, i)
        mixT = np.ascontiguousarray(np.concatenate([o_gdn, o_att], axis=1).T)
        xT_full, pT = run_A(i, i + 1 if i + 1 < L else None, xT_full, mixT)
    return np.ascontiguousarray(xT_full.T)[None].astype(np.float32)
```
